# Optimizing a Trainium2 kernel written in Bass

```python
import jax, jax.numpy as jnp
from jax import lax
import numpy as np

D_MODEL = 2048
BATCH = 4
SEQ = 2048
DEPTH = 2
DEC_BATCH = 128
DEC_SEQ = 4
PAST_LEN = 16384
PAGE_SIZE = 128

GROUP = 128
POOL_WINDOWS = (2, 4, 8, 16)
N_POOL_GROUPS = len(POOL_WINDOWS)
D_POOL = GROUP * N_POOL_GROUPS
POOL_STATE = max(POOL_WINDOWS) - 1
D_SC = 512
SC_WIDTH = 3
D_CM = 512
CM_WIDTH = 31
D_SG = 512
SG_GROUPS = D_SG // GROUP
CHUNK = 128
N_BRANCH = 4
N_IN = D_POOL + 3 * D_SC + 2 * D_CM + 2 * D_SG + N_BRANCH * D_MODEL
MEM_LEN = 256
X_HEADS = 4
X_HEAD_DIM = 128
D_X = X_HEADS * X_HEAD_DIM
PEER_HEADS = 8
PEER_DKEY = 256
PEER_NKEYS = 128
PEER_TOPK = 16
N_EXPERTS = PEER_NKEYS * PEER_NKEYS
PEER_BLOCK = 128
EPS = 1e-6

kernel_name = 'hybrid_pool_conv_sgmlp_peer_decoder_step'


def rmsnorm(x, g):
    xf = x.astype(jnp.float32)
    y = xf * lax.rsqrt(jnp.mean(xf * xf, axis=-1, keepdims=True) + EPS)
    return (y * g.astype(jnp.float32)).astype(x.dtype)


def layernorm(x, g, b):
    xf = x.astype(jnp.float32)
    mu = jnp.mean(xf, axis=-1, keepdims=True)
    var = jnp.mean(jnp.square(xf - mu), axis=-1, keepdims=True)
    y = (xf - mu) * lax.rsqrt(var + EPS) * g.astype(jnp.float32) + b.astype(jnp.float32)
    return y.astype(x.dtype)


def causal_dwconv(ext, w):
    return lax.conv_general_dilated(ext, w[:, None, :].astype(ext.dtype), (1,), 'VALID',
                                    dimension_numbers=('NWC', 'WIO', 'NWC'),
                                    feature_group_count=ext.shape[-1])


def pool_mixer(a, prev, start_pos, pool_w, pool_scale):
    n, t, _ = a.shape
    ext = jnp.concatenate([prev, a], axis=1)
    cs = jnp.cumsum(ext.astype(jnp.float32), axis=1)
    cs = jnp.concatenate([jnp.zeros_like(cs[:, :1]), cs], axis=1)
    pos = start_pos + jnp.arange(t)
    end = POOL_STATE + 1
    outs = []
    for gi, w in enumerate(POOL_WINDOWS):
        sl = slice(gi * GROUP, (gi + 1) * GROUP)
        s = cs[:, end:end + t, sl] - cs[:, end - w:end - w + t, sl]
        cnt = jnp.minimum(w, pos + 1).astype(jnp.float32)[None, :, None]
        outs.append(s / cnt - a[:, :, sl].astype(jnp.float32))
    p = jnp.stack(outs, axis=2).astype(a.dtype)
    p = jnp.einsum('ntgc,gcd->ntgd', p, pool_w).reshape(n, t, D_POOL) * pool_scale
    return p, ext[:, -POOL_STATE:]


def chunk_mixer(u, v, ln_g, ln_b, sg_w, sg_b):
    n, t, _ = u.shape
    vn = layernorm(v, ln_g, ln_b)
    tp = -(-t // CHUNK) * CHUNK
    vp = jnp.pad(vn, ((0, 0), (0, tp - t), (0, 0))).reshape(n, tp // CHUNK, CHUNK, SG_GROUPS, GROUP)
    mask = jnp.tril(jnp.ones((CHUNK, CHUNK), dtype=bool))
    w = jnp.where(mask[None], sg_w, jnp.zeros((), sg_w.dtype))
    mixed = jnp.einsum('gts,ncsgd->nctgd', w, vp) + sg_b.T[None, None, :, :, None]
    mixed = mixed.reshape(n, tp, D_SG)[:, :t]
    return u * mixed, vn


def mixer_block(h, pool_prev, sc_prev, cm_prev, start_pos, lp):
    n, t, _ = h.shape
    z = h @ lp['w_in']
    cuts = np.cumsum([D_POOL, D_SC, D_SC, D_SC, 2 * D_CM, D_SG, D_SG]).tolist()
    a, bg, cg, hb, glu, u, v, gates = jnp.split(z, cuts, axis=-1)
    pa, pool_new = pool_mixer(a, pool_prev, start_pos, lp['pool_w'], lp['pool_scale'])
    ya = pa @ lp['w_pool_out']
    ext_b = jnp.concatenate([sc_prev, cg * hb], axis=1)
    yb = (bg * causal_dwconv(ext_b, lp['sc_w'])) @ lp['w_sc_out']
    sc_new = ext_b[:, -(SC_WIDTH - 1):]
    g1, g2 = jnp.split(glu, 2, axis=-1)
    ext_c = jnp.concatenate([cm_prev, g1 * jax.nn.sigmoid(g2)], axis=1)
    c = causal_dwconv(ext_c, lp['cm_w']) + lp['cm_b']
    yc = jax.nn.silu(layernorm(c, lp['cm_ln_g'], lp['cm_ln_b'])) @ lp['w_cm_out']
    cm_new = ext_c[:, -(CM_WIDTH - 1):]
    pd, vn = chunk_mixer(u, v, lp['sg_ln_g'], lp['sg_ln_b'], lp['sg_w'], lp['sg_b'])
    yd = pd @ lp['w_sg_out']
    sig = jax.nn.sigmoid(gates.reshape(n, t, N_BRANCH, D_MODEL) + lp['b_gate'])
    merged = sig[:, :, 0] * ya + sig[:, :, 1] * yb + sig[:, :, 2] * yc + sig[:, :, 3] * yd
    return merged @ lp['w_o'], pool_new, sc_new, cm_new, vn


def memory_kv(mem, g_mem, w_xk, w_xv):
    n = mem.shape[0]
    m = rmsnorm(mem, g_mem)
    k = (m @ w_xk).reshape(n, MEM_LEN, X_HEADS, X_HEAD_DIM)
    v = (m @ w_xv).reshape(n, MEM_LEN, X_HEADS, X_HEAD_DIM)
    return k, v


def memory_attend(h, k, v, w_xq, w_xo):
    n, t, _ = h.shape
    q = (h @ w_xq).reshape(n, t, X_HEADS, X_HEAD_DIM)
    s = jnp.einsum('nthd,nmhd->nhtm', q, k).astype(jnp.float32) * (X_HEAD_DIM ** -0.5)
    p = jax.nn.softmax(s, axis=-1).astype(h.dtype)
    o = jnp.einsum('nhtm,nmhd->nthd', p, v).reshape(n, t, D_X)
    return o @ w_xo


def peer(h, w_pq, keys, tab_u, tab_v):
    n, t, d = h.shape
    blk = PEER_BLOCK if t % PEER_BLOCK == 0 else t
    hb = h.reshape(-1, blk, d)

    def one_block(xb):
        q = (xb @ w_pq).reshape(blk, PEER_HEADS, 2, PEER_DKEY // 2).astype(jnp.float32)
        s = jnp.einsum('thpc,hpnc->thpn', q, keys.astype(jnp.float32))
        sv, si = lax.top_k(s, PEER_TOPK)
        cand = sv[:, :, 0, :, None] + sv[:, :, 1, None, :]
        cv, ci = lax.top_k(cand.reshape(blk, PEER_HEADS, PEER_TOPK * PEER_TOPK), PEER_TOPK)
        i1 = jnp.take_along_axis(si[:, :, 0], ci // PEER_TOPK, axis=-1)
        i2 = jnp.take_along_axis(si[:, :, 1], ci % PEER_TOPK, axis=-1)
        eidx = i1 * PEER_NKEYS + i2
        gw = jax.nn.softmax(cv, axis=-1).astype(xb.dtype)
        ue = jnp.take(tab_u, eidx, axis=0)
        act = jax.nn.gelu(jnp.einsum('td,thkd->thk', xb, ue), approximate=False)
        ve = jnp.take(tab_v, eidx, axis=0)
        return jnp.einsum('thk,thkd->td', gw * act, ve)

    return lax.map(one_block, hb).reshape(n, t, d)


def trunk_layer(x, k_mem, v_mem, pool_prev, sc_prev, cm_prev, start_pos, lp):
    y, pool_new, sc_new, cm_new, vn = mixer_block(rmsnorm(x, lp['g_mix']), pool_prev, sc_prev,
                                                  cm_prev, start_pos, lp)
    x = x + y
    x = x + memory_attend(rmsnorm(x, lp['g_x']), k_mem, v_mem, lp['w_xq'], lp['w_xo'])
    x = x + peer(rmsnorm(x, lp['g_peer']), lp['w_pq'], lp['peer_keys'], lp['peer_u'], lp['peer_v'])
    return x, pool_new, sc_new, cm_new, vn


def setup_inputs(seed: int = 0) -> dict:
    key = jax.random.key(seed)
    ks = iter(jax.random.split(key, 48))
    D = D_MODEL
    L = DEPTH

    def nrm(shape, scale=1.0):
        return jax.random.normal(next(ks), shape, jnp.float32) * scale

    def gain(shape):
        return 1.0 + nrm(shape, 0.02)

    return {
        'x_prompt': nrm((BATCH, SEQ, D)),
        'x_sample': nrm((DEC_BATCH, DEC_SEQ, D)),
        'mem_prompt': nrm((BATCH, MEM_LEN, D)),
        'cache_mem_k': nrm((L, DEC_BATCH, MEM_LEN, X_HEADS, X_HEAD_DIM)),
        'cache_mem_v': nrm((L, DEC_BATCH, MEM_LEN, X_HEADS, X_HEAD_DIM)),
        'state_pool': nrm((L, DEC_BATCH, POOL_STATE, D_POOL)),
        'state_sconv': nrm((L, DEC_BATCH, SC_WIDTH - 1, D_SC)),
        'state_cconv': nrm((L, DEC_BATCH, CM_WIDTH - 1, D_CM)),
        'g_mix': gain((L, D)),
        'w_in': nrm((L, D, N_IN), D ** -0.5),
        'b_gate': nrm((L, N_BRANCH, D), 0.02),
        'pool_w': nrm((L, N_POOL_GROUPS, GROUP, GROUP), GROUP ** -0.5),
        'pool_scale': gain((L, D_POOL)),
        'w_pool_out': nrm((L, D_POOL, D), D_POOL ** -0.5),
        'sc_w': nrm((L, SC_WIDTH, D_SC), SC_WIDTH ** -0.5),
        'w_sc_out': nrm((L, D_SC, D), D_SC ** -0.5),
        'cm_w': nrm((L, CM_WIDTH, D_CM), CM_WIDTH ** -0.5),
        'cm_b': nrm((L, D_CM), 0.02),
        'cm_ln_g': gain((L, D_CM)),
        'cm_ln_b': nrm((L, D_CM), 0.02),
        'w_cm_out': nrm((L, D_CM, D), D_CM ** -0.5),
        'sg_ln_g': gain((L, D_SG)),
        'sg_ln_b': nrm((L, D_SG), 0.02),
        'sg_w': nrm((L, SG_GROUPS, CHUNK, CHUNK), CHUNK ** -0.5),
        'sg_b': gain((L, SG_GROUPS, CHUNK)),
        'w_sg_out': nrm((L, D_SG, D), D_SG ** -0.5),
        'w_o': nrm((L, D, D), D ** -0.5),
        'g_x': gain((L, D)),
        'g_mem': gain((L, D)),
        'w_xq': nrm((L, D, D_X), D ** -0.5),
        'w_xk': nrm((L, D, D_X), D ** -0.5),
        'w_xv': nrm((L, D, D_X), D ** -0.5),
        'w_xo': nrm((L, D_X, D), D_X ** -0.5),
        'g_peer': gain((L, D)),
        'w_pq': nrm((L, D, PEER_HEADS * PEER_DKEY), D ** -0.5),
        'peer_keys': nrm((L, PEER_HEADS, 2, PEER_NKEYS, PEER_DKEY // 2), (PEER_DKEY // 2) ** -0.5),
        'peer_u': nrm((L, N_EXPERTS, D), D ** -0.5),
        'peer_v': nrm((L, N_EXPERTS, D), (PEER_HEADS * PEER_TOPK) ** -0.5),
        'g_final': gain((D,)),
    }


def reference(x_prompt, x_sample, mem_prompt, cache_mem_k, cache_mem_v, state_pool, state_sconv,
              state_cconv, g_mix, w_in, b_gate, pool_w, pool_scale, w_pool_out, sc_w, w_sc_out,
              cm_w, cm_b, cm_ln_g, cm_ln_b, w_cm_out, sg_ln_g, sg_ln_b, sg_w, sg_b, w_sg_out, w_o,
              g_x, g_mem, w_xq, w_xk, w_xv, w_xo, g_peer, w_pq, peer_keys, peer_u, peer_v, g_final):
    xp, xs = x_prompt, x_sample
    nb = x_prompt.shape[0]
    mk_p, mv_p, pool_p, sc_p, cm_p = [], [], [], [], []
    pool_s, sc_s, cm_s, cv_s = [], [], [], []
    for l in range(DEPTH):
        lp = dict(g_mix=g_mix[l], w_in=w_in[l], b_gate=b_gate[l], pool_w=pool_w[l],
                  pool_scale=pool_scale[l], w_pool_out=w_pool_out[l], sc_w=sc_w[l],
                  w_sc_out=w_sc_out[l], cm_w=cm_w[l], cm_b=cm_b[l], cm_ln_g=cm_ln_g[l],
                  cm_ln_b=cm_ln_b[l], w_cm_out=w_cm_out[l], sg_ln_g=sg_ln_g[l], sg_ln_b=sg_ln_b[l],
                  sg_w=sg_w[l], sg_b=sg_b[l], w_sg_out=w_sg_out[l], w_o=w_o[l], g_x=g_x[l],
                  w_xq=w_xq[l], w_xo=w_xo[l], g_peer=g_peer[l], w_pq=w_pq[l],
                  peer_keys=peer_keys[l], peer_u=peer_u[l], peer_v=peer_v[l])
        k_p, v_p = memory_kv(mem_prompt, g_mem[l], w_xk[l], w_xv[l])
        xp, st_pool, st_sc, st_cm, _ = trunk_layer(
            xp, k_p, v_p,
            jnp.zeros((nb, POOL_STATE, D_POOL), xp.dtype),
            jnp.zeros((nb, SC_WIDTH - 1, D_SC), xp.dtype),
            jnp.zeros((nb, CM_WIDTH - 1, D_CM), xp.dtype),
            0, lp)
        mk_p.append(k_p)
        mv_p.append(v_p)
        pool_p.append(st_pool)
        sc_p.append(st_sc)
        cm_p.append(st_cm)
        xs, st_pool, st_sc, st_cm, vn = trunk_layer(
            xs, cache_mem_k[l], cache_mem_v[l], state_pool[l], state_sconv[l], state_cconv[l],
            PAST_LEN, lp)
        pool_s.append(st_pool)
        sc_s.append(st_sc)
        cm_s.append(st_cm)
        cv_s.append(vn)
    y_prompt = rmsnorm(xp, g_final)
    y_sample = rmsnorm(xs, g_final)
    return (y_prompt, y_sample, jnp.stack(mk_p), jnp.stack(mv_p), jnp.stack(pool_p), jnp.stack(sc_p),
            jnp.stack(cm_p), jnp.stack(pool_s), jnp.stack(sc_s), jnp.stack(cm_s), jnp.stack(cv_s))
```

```python
import os
import numpy as np
import concourse.bass as bass
import concourse.mybir as mybir
from concourse.bass_utils import run_bass_kernel_spmd
from contextlib import ExitStack

F32 = mybir.dt.float32
BF16 = mybir.dt.bfloat16
I32 = mybir.dt.int32
U32 = mybir.dt.uint32
ALU = mybir.AluOpType
AF = mybir.ActivationFunctionType
AX = mybir.AxisListType

ENG = ['pe', 'act', 'dve', 'pool', 'sp']
BNAME = {'pe': 'tensor', 'act': 'scalar', 'dve': 'vector', 'pool': 'gpsimd', 'sp': 'sync'}

D = 2048
NIN = 12288
EPS = 1e-6
NT = 10
TOK = 1216
PASSES = [(0, 3, False), (3, 3, False), (6, 3, True)]
POOL_W = (2, 4, 8, 16)
SCL = 128.0 ** -0.5


class Prog:
    def __init__(self):
        self.q = {e: [] for e in ENG}
        self.nev = {e: 0 for e in ENG}
        self.dmacnt = {}
        self.buf = {}
        self.seen = {e: {} for e in ENG}

    def op(self, eng, fn, reads=(), writes=(), dma_sem=None):
        writes = list(writes) + [b for b in reads if b.startswith('ps') and b not in writes]
        need = {}
        for b in reads:
            st = self.buf.setdefault(b, [[], []])
            for (s, v) in st[0]:
                need[s] = max(need.get(s, 0), v)
        for b in writes:
            st = self.buf.setdefault(b, [[], []])
            for (s, v) in st[0] + st[1]:
                need[s] = max(need.get(s, 0), v)
        seen = self.seen[eng]
        waits = []
        for s, v in need.items():
            if seen.get(s, 0) < v:
                seen[s] = v
                waits.append((s, v))
        if dma_sem is None:
            self.nev[eng] += 1
            ev = ('e_' + eng, self.nev[eng])
            inc = ('e_' + eng, 1)
        else:
            self.dmacnt[dma_sem] = self.dmacnt.get(dma_sem, 0) + 16
            ev = ('d_' + dma_sem, self.dmacnt[dma_sem])
            inc = ('d_' + dma_sem, 16)
        for b in reads:
            self.buf[b][1].append(ev)
        for b in writes:
            self.buf[b][0] = [ev]
            self.buf[b][1] = []
        self.q[eng].append((waits, fn, inc))
        return ev

    def emit(self, nc):
        names = ['e_' + e for e in ENG] + ['d_' + k for k in self.dmacnt]
        with ExitStack() as st:
            sems = {n: st.enter_context(nc.semaphore(n)) for n in names}
            with nc.Block() as block:
                for e in ENG:
                    def body(engine, e=e):
                        for (waits, fn, inc) in self.q[e]:
                            for (s, v) in waits:
                                engine.wait_ge(sems[s], v)
                            ins = fn(engine)
                            ins.then_inc(sems[inc[0]], inc[1])
                        if e == 'sp':
                            for k, v in self.dmacnt.items():
                                engine.wait_ge(sems['d_' + k], v)
                            for e2 in ENG:
                                if e2 != 'sp' and self.nev[e2] > 0:
                                    engine.wait_ge(sems['e_' + e2], self.nev[e2])
                    getattr(block, BNAME[e])(body)


class _Stop(Exception):
    pass


def build_program(stop_after=None, dbg=False, peer_rows=16384):
    nc = bass.Bass("TRN2", target_bir_lowering=False)
    P = Prog()
    st = ExitStack()

    def din(name, shape, dt=F32):
        return nc.dram_tensor(name, list(shape), dt, kind="ExternalInput").ap()

    def dout(name, shape, dt=F32):
        return nc.dram_tensor(name, list(shape), dt, kind="ExternalOutput").ap()

    xin = din("xin", [TOK, D])
    flag = din("flag", [128, 1])
    cnt = din("cnt", [128, 4, 15])
    mem = din("mem", [256, D])
    ck = din("ck", [2, 16, 256, 512])
    cvv = din("cvv", [2, 16, 256, 512])
    spool = din("spool", [2, 16, 15, 512])
    ssc = din("ssc", [2, 16, 2, 512])
    scc = din("scc", [2, 16, 30, 512])
    g_mix = din("g_mix", [2, D]); w_in = din("w_in", [2, D, NIN]); b_gate = din("b_gate", [2, 4, D])
    pool_w = din("pool_w", [2, 4, 128, 128]); pool_scale = din("pool_scale", [2, 512])
    w_pool_out = din("w_pool_out", [2, 512, D]); sc_w = din("sc_w", [2, 3, 512]); w_sc_out = din("w_sc_out", [2, 512, D])
    cm_w = din("cm_w", [2, 31, 512]); cm_b = din("cm_b", [2, 512]); cm_ln_g = din("cm_ln_g", [2, 512]); cm_ln_b = din("cm_ln_b", [2, 512])
    w_cm_out = din("w_cm_out", [2, 512, D]); sg_ln_g = din("sg_ln_g", [2, 512]); sg_ln_b = din("sg_ln_b", [2, 512])
    sg_wT = din("sg_wT", [2, 128, 4, 128]); sgws = din("sgws", [2, 64, 4, 64]); sg_b = din("sg_b", [2, 4, 128])
    w_sg_out = din("w_sg_out", [2, 512, D]); w_o = din("w_o", [2, D, D]); g_x = din("g_x", [2, D]); g_mem = din("g_mem", [2, D])
    w_xq = din("w_xq", [2, D, 512]); w_xk = din("w_xk", [2, D, 512]); w_xv = din("w_xv", [2, D, 512]); w_xo = din("w_xo", [2, 512, D])
    g_peer = din("g_peer", [2, D]); w_pq = din("w_pq", [2, D, D]); keysT = din("keysT", [2, 128, 2048])
    peer_u = din("peer_u", [2, peer_rows, D]); peer_v = din("peer_v", [2, peer_rows, D]); g_final = din("g_final", [1, D])
    c_identf = din("c_identf", [128, 128]); c_tri = din("c_tri", [128, 128]); c_blk = din("c_blk", [64, 64])
    c_wwin = din("c_wwin", [128, 255]); c_iota = din("c_iota", [128, 16]); c_amask = din("c_amask", [128, 16, 64])

    y_o = dout("y", [TOK, D])
    mk_o = dout("mk", [2, 256, 512]); mv_o = dout("mv", [2, 256, 512])
    pool_p_o = dout("pool_p", [2, 15, 512]); sc_p_o = dout("sc_p", [2, 2, 512]); cm_p_o = dout("cm_p", [2, 30, 512])
    pool_s_o = dout("pool_s", [2, 16, 15, 512]); sc_s_o = dout("sc_s", [2, 16, 2, 512]); cm_s_o = dout("cm_s", [2, 16, 30, 512])
    cv_s_o = dout("cv_s", [2, 16, 4, 512])
    dbg_o = dout("dbg", [TOK, D]) if dbg else None

    def sb(name, shape, dt):
        return st.enter_context(nc.sbuf_tensor(name, list(shape), dt))

    X = sb("X", [128, NT, D], F32)
    WS = sb("WS", [128, 2, 16, 256], BF16)
    GB = sb("GB", [128, D], F32)
    IDF = sb("IDF", [128, 128], F32); IDB = sb("IDB", [128, 128], BF16)
    ONESF = sb("ONESF", [128, 128], F32); ONESB = sb("ONESB", [128, 128], BF16)
    WWIN = sb("WWIN", [128, 255], F32); IOTA = sb("IOTA", [128, 16], F32)
    TRI = sb("TRI", [128, 128], F32); BLK = sb("BLK", [64, 64], F32)
    AMASK = sb("AMASK", [128, 16, 64], BF16)
    FLAG = sb("FLAG", [128, 1], F32); CNT = sb("CNT", [128, 4, 15], F32)
    SS = sb("SS", [128, 8], F32)
    SCRN = 24576
    SCR = sb("SCR", [128, SCRN], F32)
    PSB = [st.enter_context(nc.psum_tensor(f"PSB{i}", [128, 2048], F32)) for i in range(2)]

    def bank(i):
        return PSB[i // 4][:, (i % 4) * 512:(i % 4 + 1) * 512]

    def _pat(shape):
        names = "abcdef"[:len(shape) - 1]
        return "p (" + " ".join(names) + ") -> p " + " ".join(names)

    def _kw(shape):
        names = "abcdef"[:len(shape) - 1]
        return {n: s_ for n, s_ in zip(names[1:], shape[2:])}

    class Carver:
        def __init__(self):
            self.off = 0

        def _shape(self, ap, shape):
            ap = ap[:shape[0]] if shape[0] < 128 else ap
            return ap.rearrange(_pat(shape), **_kw(shape)) if len(shape) > 2 else ap

        def f32(self, shape):
            n = int(np.prod(shape[1:]))
            ap = SCR[:, self.off:self.off + n]
            self.off += n
            assert self.off <= SCRN, self.off
            return self._shape(ap, shape)

        def bf16(self, shape):
            n = int(np.prod(shape[1:]))
            nf = (n + 1) // 2
            ap = SCR[:, self.off:self.off + nf].bitcast(BF16)[:, 0:n]
            self.off += nf
            assert self.off <= SCRN, self.off
            return self._shape(ap, shape)

        def i32(self, shape):
            n = int(np.prod(shape[1:]))
            ap = SCR[:, self.off:self.off + n].bitcast(I32)
            self.off += n
            assert self.off <= SCRN, self.off
            return self._shape(ap, shape)

    cur_pass = [0]
    banks_avail = [list(range(8))]
    _bk = [0]

    def nb():
        lst = banks_avail[0]
        _bk[0] = (_bk[0] + 1) % len(lst)
        return lst[_bk[0]]

    def bk(i):
        return f'ps{i}'

    slow_mode = [False]

    def dma(eng, out, in_, reads, writes, sem):
        if slow_mode[0]:
            P.op(eng, lambda e: e.dma_start(out=out, in_=in_, allow_slow_non_contiguous=True), reads=reads, writes=writes, dma_sem=sem)
        else:
            P.op(eng, lambda e: e.dma_start(out=out, in_=in_), reads=reads, writes=writes, dma_sem=sem)

    class slow_dma:
        def __enter__(self):
            slow_mode[0] = True
        def __exit__(self, *a):
            slow_mode[0] = False

    def wq(s):
        return [f'w{s}q{j}' for j in range(4)]

    wn = [0]

    def wload(parts):
        s = wn[0] % 2
        wn[0] += 1
        for (k0, nk, src) in parts:
            keys = [f'w{s}q{j}' for j in range(k0 // 4, (k0 + nk + 3) // 4)]
            sem = f'w{s}' if nk == 16 else f'w{s}q{k0 // 4}'
            dma('pool', WS[:, s, k0:k0 + nk, :], src.rearrange("(k p) c -> p k c", p=128), [], keys, sem)
        return s

    def run_steps(steps, hook=None):
        if not steps:
            return
        slots = {0: wload(steps[0][0])}
        for i, (parts, fn) in enumerate(steps):
            if i + 1 < len(steps):
                slots[i + 1] = wload(steps[i + 1][0])
            if hook is not None:
                hook(i)
            if stop_after is not None and stop_after[0] == 'step' and stop_after[1] == i and cur_pass[0] == int(os.environ.get('DBG_PASS', '0')):
                raise _Stop()
            fn(slots[i])

    def full(src):
        return [(0, 16, src)]

    def halves(src512, fn):
        return [(full(src512[:, h * 256:(h + 1) * 256]), (lambda s, h=h: fn(s, h))) for h in range(2)]

    def mm_fm(b, s, cg, rhs_ap, C, rkeys, nk=16, k0=0):
        def f(e):
            ins = None
            for k in range(nk):
                ins = e.matmul(bank(b)[:, :C], lhsT=WS[:, s, k0 + k, cg * 128:(cg + 1) * 128], rhs=rhs_ap[:, k, :C],
                               start=(k == 0), stop=(k == nk - 1))
            return ins
        keys = [f'w{s}q{j}' for j in range(k0 // 4, (k0 + nk + 3) // 4)]
        P.op('pe', f, reads=keys + rkeys, writes=[bk(b)])

    _alt = [0]

    def evac_eng():
        _alt[0] ^= 1
        return 'act' if _alt[0] else 'dve'

    def copy(eng, out, in_, reads, writes):
        if eng == 'act':
            P.op('act', lambda e: e.copy(out=out, in_=in_), reads=reads, writes=writes)
        else:
            P.op('dve', lambda e: e.tensor_copy(out=out, in_=in_), reads=reads, writes=writes)

    def tt(out, in0, in1, op, reads, writes, eng='dve'):
        P.op(eng, lambda e: e.tensor_tensor(out=out, in0=in0, in1=in1, op=op), reads=reads, writes=writes)

    def tsc(out, in0, s1, s2, op0, op1, reads, writes, eng='dve'):
        if s2 is None:
            P.op(eng, lambda e: e.tensor_scalar(out=out, in0=in0, scalar1=s1, scalar2=None, op0=op0), reads=reads, writes=writes)
        else:
            P.op(eng, lambda e: e.tensor_scalar(out=out, in0=in0, scalar1=s1, scalar2=s2, op0=op0, op1=op1), reads=reads, writes=writes)

    def stt(out, in0, scalar, in1, op0, op1, reads, writes, eng='dve', accum=None):
        if accum is None:
            P.op(eng, lambda e: e.scalar_tensor_tensor(out=out, in0=in0, scalar=scalar, in1=in1, op0=op0, op1=op1), reads=reads, writes=writes)
        else:
            P.op(eng, lambda e: e.scalar_tensor_tensor(out=out, in0=in0, scalar=scalar, in1=in1, op0=op0, op1=op1, accum_out=accum), reads=reads, writes=writes)

    def rsqrt(out, in_, scale, bias, reads, writes):
        P.op('act', lambda e: e.activation(out=out, in_=in_, func=AF.Sqrt, bias=bias, scale=scale), reads=reads, writes=writes)
        P.op('dve', lambda e: e.reciprocal(out=out, in_=out), reads=writes, writes=writes)

    def guard(rkeys, wkeys):
        mx = {}
        for k in list(rkeys) + list(wkeys):
            st_ = P.buf.setdefault(k, [[], []])
            for (sname, v) in st_[0] + st_[1]:
                mx[sname] = max(mx.get(sname, 0), v)
        evs = list(mx.items())
        for k in wkeys:
            st_ = P.buf.setdefault(k, [[], []])
            st_[0] = list(evs)
            st_[1] = []

    def barrier():
        allk = list(P.buf.keys())
        guard(allk, allk)

    dma('sp', IDF[:], c_identf, [], ['IDF'], 'c0')
    dma('sp', WWIN[:], c_wwin, [], ['WWIN'], 'c1')
    dma('sp', IOTA[:], c_iota, [], ['IOTA'], 'c2')
    dma('sp', TRI[:], c_tri, [], ['TRI'], 'c3')
    dma('sp', BLK[:], c_blk, [], ['BLK'], 'c4')
    dma('pool', AMASK[:], c_amask, [], ['AMASK'], 'c5')
    dma('sp', FLAG[:], flag, [], ['FLAG'], 'c6')
    dma('sp', CNT[:], cnt, [], ['CNT'], 'c7')
    copy('dve', IDB[:], IDF[:], ['IDF'], ['IDB'])
    P.op('dve', lambda e: e.memset(ONESF[:], 1.0 / 512.0), writes=['ONESF'])
    P.op('dve', lambda e: e.memset(ONESB[:], 1.0), writes=['ONESB'])
    for t in range(NT):
        np_ = 128 if t < 9 else 64
        dma('sp', X[:np_, t, :], xin[t * 128:t * 128 + np_, :], [], [f'X{t}'], f'X{t}')

    def load_gb(src_row):
        dma('sp', GB[:], src_row.to_broadcast([128, D]), [], ['GB'], 'GB')

    def norm_tile(xt_ap, xkeys, np_, hn_ap, hnkeys, hnT, hnTkey, col0):
        P.op('act', lambda e: e.activation(out=hn_ap[:np_, :], in_=xt_ap, func=AF.Square, accum_out=SS[:np_, 0:1]),
             reads=xkeys, writes=hnkeys + ['SS0'])
        rsqrt(SS[:np_, 2:3], SS[:np_, 0:1], 1.0 / D, EPS, ['SS0'], ['SS2'])
        stt(hn_ap[:np_, :], xt_ap, SS[:np_, 2:3], GB[:np_, :], ALU.mult, ALU.mult, xkeys + ['SS2', 'GB'], hnkeys)
        if hnT is None:
            return
        for half in range(2):
            b = nb()
            pb = bank(b).bitcast(BF16)

            def f(e, half=half, pb=pb):
                ins = None
                for kk in range(8):
                    k = half * 8 + kk
                    ins = e.transpose(out=pb[:, kk * 128:kk * 128 + np_], in_=hn_ap[:np_, k * 128:(k + 1) * 128], identity=IDB[:np_, :np_])
                return ins
            P.op('pe', f, reads=hnkeys + ['IDB'], writes=[bk(b)])
            src = pb.rearrange("p (k c) -> p k c", c=128)[:, :, :np_]
            copy(evac_eng(), hnT[:, half * 8:(half + 1) * 8, col0:col0 + np_], src, [bk(b)], [hnTkey])

    def pass_cols(p):
        t0, npt, has_s = PASSES[p]
        tiles = [(t0 + i, 128, i * 128) for i in range(npt)]
        if has_s:
            tiles.append((9, 64, npt * 128))
        C = npt * 128 + (64 if has_s else 0)
        return tiles, C, npt * 128

    def sview(ap2d):
        return ap2d.rearrange("p (n t) -> p n t", t=4)

    def mixer_phase(l):
        cv_ = Carver()
        hnT = cv_.bf16([128, 16, 448])
        PBR = [cv_.bf16([128, 4, 448]) for _ in range(4)]
        m0 = cv_.off
        MERGED = cv_.bf16([128, 16, 448])
        T1 = cv_.f32([128, 4, 480])
        T2o = cv_.off
        T2 = cv_.f32([128, 4, 480])
        T3o = cv_.off
        T3 = cv_.f32([128, 4, 512])
        EXS_E = cv_.f32([128, 4, 16, 6])
        HALO = cv_.f32([128, 4, 48])
        SNEW = cv_.f32([128, 4, 4, 16])
        STG = cv_.f32([128, 512])
        MEAN = cv_.f32([128, 448]); RSTD = cv_.f32([128, 448])
        BST = cv_.f32([128, 8])
        POOLW = cv_.bf16([128, 4, 128]); PSCALE = cv_.f32([128, 4])
        SCW = cv_.f32([128, 4, 3]); CMW = cv_.f32([128, 4, 31]); CMB = cv_.f32([128, 4])
        CMG = cv_.f32([128, 4]); CMBB = cv_.f32([128, 4])
        SGG = cv_.f32([128, 512]); SGB = cv_.f32([128, 512])
        SGWT = cv_.bf16([128, 4, 128]); SGWS = cv_.bf16([64, 4, 64])
        SGTMP = cv_.f32([128, 4, 128])
        SGBR = cv_.bf16([128, 4, 128]); SGBRS = cv_.bf16([128, 4, 64])
        BGATE = cv_.f32([128, 4, 16])
        AEXS = SCR[:, m0:m0 + 1216].rearrange("p (g n r) -> p g n r", g=4, n=16)
        CEXS = SCR[:, m0 + 1216:m0 + 1216 + 2176].rearrange("p (g n r) -> p g n r", g=4, n=16)
        HN1 = SCR[:, T2o:T2o + 1024].bitcast(BF16)
        PP = SCR[:, T3o:T3o + 896].bitcast(BF16).rearrange("p (g c) -> p g c", g=4)
        WO = [SCR[:, T3o + 1024 + i * 512:T3o + 1024 + (i + 1) * 512].bitcast(BF16).rearrange("p (k c) -> p k c", k=4) for i in range(2)]
        T1o = T2o - 1920
        VN = SCR[:, T1o:T1o + 512]
        VNB = SCR[:, T1o + 512:T1o + 512 + 1024].bitcast(BF16).rearrange("p (t c) -> p t c", t=4)
        T1K = ['T1h'] + [f'T1m{g}' for g in range(4)]
        T2K = [f'T2{g}' for g in range(4)] + [f'T2s{g}' for g in range(4)]
        T3K = [f'T3{g}' for g in range(4)]

        dma('pool', POOLW[:], pool_w[l].rearrange("g c d -> c g d"), [], ['POOLW'], 'lp0')
        with slow_dma():
            dma('sp', PSCALE[:], pool_scale[l].rearrange("(g c) -> c g", c=128), [], ['PSCALE'], 'lp1')
            for g in range(4):
                dma('sp', SCW[:, g, :], sc_w[l][:, g * 128:(g + 1) * 128].rearrange("k c -> c k"), [], ['SCW'], 'lp2')
                dma('sp', CMW[:, g, :], cm_w[l][:, g * 128:(g + 1) * 128].rearrange("k c -> c k"), [], ['CMW'], 'lp3')
            dma('sp', CMB[:], cm_b[l].rearrange("(g c) -> c g", c=128), [], ['CMB'], 'lp4')
            dma('sp', CMG[:], cm_ln_g[l].rearrange("(g c) -> c g", c=128), [], ['CMG'], 'lp5')
            dma('sp', CMBB[:], cm_ln_b[l].rearrange("(g c) -> c g", c=128), [], ['CMBB'], 'lp6')
            for i in range(4):
                dma('sp', BGATE[:, i, :], b_gate[l][i].rearrange("(d c) -> c d", c=128), [], ['BGATE'], 'lp7')
        dma('sp', SGG[:], sg_ln_g[l:l + 1, :].to_broadcast([128, 512]), [], ['SGG'], 'lp8')
        dma('sp', SGB[:], sg_ln_b[l:l + 1, :].to_broadcast([128, 512]), [], ['SGB'], 'lp9')
        dma('sp', SGTMP[:], sg_wT[l], [], ['SGTMP'], 'lp10')
        tt(SGWT[:], SGTMP[:], TRI[:].unsqueeze(1).to_broadcast([128, 4, 128]), ALU.mult, ['SGTMP', 'TRI'], ['SGWT'])
        dma('sp', SGTMP[:64, :, 0:64], sgws[l], ['SGWT'], ['SGTMP'], 'lp10')
        tt(SGWS[:], SGTMP[:64, :, 0:64], BLK[:].unsqueeze(1).to_broadcast([64, 4, 64]), ALU.mult, ['SGTMP', 'BLK'], ['SGWS'])
        dma('sp', SGTMP[:], sg_b[l:l + 1, :, :].to_broadcast([128, 4, 128]), ['SGWS'], ['SGTMP'], 'lp10')
        tsc(SGBR[:], SGTMP[:], 1.0 / 128.0, None, ALU.mult, None, ['SGTMP'], ['SGBR'])
        copy('dve', SGBRS[:].rearrange("p g (n t) -> p g n t", t=4), SGBR[:, :, 0:4].unsqueeze(2).to_broadcast([128, 4, 16, 4]), ['SGBR'], ['SGBRS'])
        load_gb(g_mix[l:l + 1, :])
        tsc(X[:, 0, :], X[:, 0, :], FLAG[:, 0:1], None, ALU.mult, None, ['X0', 'FLAG'], ['X0'])
        P.op('dve', lambda e: e.memset(HALO[:], 0.0), writes=['HALO'])
        AH = HALO[:, :, 0:15]; EH = HALO[:, :, 15:17]; CH = HALO[:, :, 17:47]

        def state_out(src_fm, skeys, ncols, dsts):
            b = nb()

            def f(e):
                ins = None
                for g in range(4):
                    ins = e.transpose(out=bank(b)[:ncols, g * 128:(g + 1) * 128], in_=src_fm[:, g, :], identity=IDF[:, :])
                return ins
            P.op('pe', f, reads=skeys + ['IDF'], writes=[bk(b)])
            copy(evac_eng(), STG[:ncols, :], bank(b)[:ncols, :], [bk(b)], ['STG'])
            for (r0, nr, dst) in dsts:
                dma('sp', dst, STG[r0:r0 + nr, :], ['STG'], [], 'STGo')

        for p in range(3):
            cur_pass[0] = p
            tiles, C, Cp = pass_cols(p)
            has_s = PASSES[p][2]
            for (t, np_, c0) in tiles:
                norm_tile(X[:np_, t, :], [f'X{t}'], np_, HN1, T2K, hnT, 'hnT', c0)
            if has_s:
                def load_state(src, R, EXS, exkey):
                    rows = 16 * R
                    flat = src.rearrange("n r c -> (n r) c")
                    r0 = 0
                    while r0 < rows:
                        nseq = min(128, rows - r0) // R
                        nr = nseq * R
                        dma('sp', STG[:nr, :], flat[r0:r0 + nr, :], [], ['STG'], 'STG')
                        b = nb()

                        def f(e, nr=nr, b=b):
                            ins = None
                            for g in range(4):
                                ins = e.transpose(out=bank(b)[:, g * 128:g * 128 + nr], in_=STG[:nr, g * 128:(g + 1) * 128], identity=IDF[:nr, :nr])
                            return ins
                        P.op('pe', f, reads=['STG', 'IDF'], writes=[bk(b)])
                        n0 = r0 // R
                        src_v = bank(b).rearrange("p (g c) -> p g c", g=4)[:, :, :nr].rearrange("p g (n r) -> p g n r", r=R)
                        copy(evac_eng(), EXS[:, :, n0:n0 + nseq, 0:R], src_v, [bk(b)], [exkey])
                        r0 += nr
                load_state(spool[l], 15, AEXS, 'MERGED')
                load_state(ssc[l], 2, EXS_E, 'EXS_E')
                load_state(scc[l], 30, CEXS, 'MERGED')
                dma('sp', pool_s_o[l][:, 0:11, :], spool[l][:, 4:15, :], [], [], 'dd0')
                dma('sp', cm_s_o[l][:, 0:26, :], scc[l][:, 4:30, :], [], [], 'dd1')

            def snew_view(g):
                return SNEW[:, g, :, :].rearrange("p t n -> p n t")

            SNEWf = SNEW.rearrange("p g t n -> p g (t n)")
            steps = []
            base = w_in[l]

            def stepA(s, half):
                AEX = T1
                if half == 0:
                    copy('dve', AEX[:, :, 0:15], AH, ['HALO'], ['T1h'])
                for cg in range(2):
                    g = half * 2 + cg
                    b = nb()
                    mm_fm(b, s, cg, hnT, C, ['hnT'])
                    copy('act', AEX[:, g, 15:15 + Cp], bank(b)[:, :Cp], [bk(b)], [f'T1m{g}'])
                    if has_s:
                        copy('act', AEXS[:, g, :, 15:19], sview(bank(b)[:, Cp:Cp + 64]), [bk(b)], ['MERGED'])
                        copy('dve', snew_view(g), sview(bank(b)[:, Cp:Cp + 64]), [bk(b)], ['SNEW'])
                if half == 0:
                    return
                if has_s:
                    state_out(SNEWf, ['SNEW'], 64, [(tq * 16, 16, pool_s_o[l][:, 11 + tq, :]) for tq in range(4)])
                Sb = T2
                for g, w in enumerate(POOL_W):
                    rk = ['T1h', f'T1m{g}']
                    tt(Sb[:, g, :Cp], AEX[:, g, 15:15 + Cp], AEX[:, g, 14:14 + Cp], ALU.add, rk, [f'T2{g}'])
                    for j in range(2, w):
                        tt(Sb[:, g, :Cp], Sb[:, g, :Cp], AEX[:, g, 15 - j:15 - j + Cp], ALU.add, rk + [f'T2{g}'], [f'T2{g}'])
                    stt(PP[:, g, :Cp], Sb[:, g, :Cp], 1.0 / w, AEX[:, g, 15:15 + Cp], ALU.mult, ALU.subtract, rk + [f'T2{g}'], T3K)
                    if p == 0:
                        tt(Sb[:, g, 128:143], Sb[:, g, 128:143], CNT[:, g, :], ALU.mult, [f'T2{g}', 'CNT'] + T3K, [f'T2{g}'])
                        tt(PP[:, g, 128:143], Sb[:, g, 128:143], AEX[:, g, 143:158], ALU.subtract, rk + [f'T2{g}'], T3K)
                    if has_s:
                        Ss = sview(Sb[:, g, Cp:Cp + 64])
                        tt(Ss, AEXS[:, g, :, 15:19], AEXS[:, g, :, 14:18], ALU.add, ['MERGED'], [f'T2s{g}'])
                        for j in range(2, w):
                            tt(Ss, Ss, AEXS[:, g, :, 15 - j:19 - j], ALU.add, ['MERGED', f'T2s{g}'], [f'T2s{g}'])
                        stt(sview(PP[:, g, Cp:Cp + 64]), Ss, 1.0 / w, AEXS[:, g, :, 15:19], ALU.mult, ALU.subtract, ['MERGED', f'T2s{g}'], T3K)
                for g in range(4):
                    b = nb()
                    P.op('pe', lambda e, g=g, b=b: e.matmul(bank(b)[:, :C], lhsT=POOLW[:, g, :], rhs=PP[:, g, :C], start=True, stop=True),
                         reads=['POOLW'] + T3K, writes=[bk(b)])
                    P.op('act', lambda e, g=g, b=b: e.mul(out=PBR[0][:, g, :C], in_=bank(b)[:, :C], mul=PSCALE[:, g:g + 1]),
                         reads=[bk(b), 'PSCALE'], writes=[f'PA{g}'])
                copy('dve', AH, AEX[:, :, Cp:Cp + 15], T1K, ['HALO'])
                if p == 2:
                    state_out(AEX[:, :, Cp:Cp + 15], T1K, 15, [(0, 15, pool_p_o[l])])
            steps += halves(base[:, 0:512], stepA)

            def stepB_cg(s, half):
                for cg in range(2):
                    g = half * 2 + cg
                    b = nb()
                    mm_fm(b, s, cg, hnT, C, ['hnT'])
                    copy('act', T3[:, g, :C], bank(b)[:, :C], [bk(b)], [f'T3{g}'])
            steps += halves(base[:, 1024:1536], stepB_cg)

            def stepB_hb(s, half):
                EEX = T1
                if half == 0:
                    copy('dve', EEX[:, :, 0:2], EH, ['HALO'], ['T1h'])
                for cg in range(2):
                    g = half * 2 + cg
                    b = nb()
                    mm_fm(b, s, cg, hnT, C, ['hnT'])
                    tt(EEX[:, g, 2:2 + Cp], bank(b)[:, :Cp], T3[:, g, :Cp], ALU.mult, [bk(b), f'T3{g}'], [f'T1m{g}'])
                    if has_s:
                        tt(EXS_E[:, g, :, 2:6], sview(bank(b)[:, Cp:Cp + 64]), sview(T3[:, g, Cp:Cp + 64]), ALU.mult, [bk(b), f'T3{g}'], ['EXS_E'])
                        copy('dve', snew_view(g), EXS_E[:, g, :, 2:6], ['EXS_E'], ['SNEW'])
                if half == 0:
                    return
                if has_s:
                    state_out(SNEWf, ['SNEW'], 64, [(tq * 16, 16, sc_s_o[l][:, tq - 2, :]) for tq in (2, 3)])
                ACC = T2
                for g in range(4):
                    rk = ['T1h', f'T1m{g}', 'SCW']
                    tsc(ACC[:, g, :Cp], EEX[:, g, 0:Cp], SCW[:, g, 0:1], None, ALU.mult, None, rk, [f'T2{g}'])
                    for k in (1, 2):
                        stt(ACC[:, g, :Cp], EEX[:, g, k:k + Cp], SCW[:, g, k:k + 1], ACC[:, g, :Cp], ALU.mult, ALU.add, rk + [f'T2{g}'], [f'T2{g}'])
                    if has_s:
                        As = sview(ACC[:, g, Cp:Cp + 64])
                        tsc(As, EXS_E[:, g, :, 0:4], SCW[:, g, 0:1], None, ALU.mult, None, ['EXS_E', 'SCW'], [f'T2s{g}'])
                        for k in (1, 2):
                            stt(As, EXS_E[:, g, :, k:k + 4], SCW[:, g, k:k + 1], As, ALU.mult, ALU.add, ['EXS_E', 'SCW', f'T2s{g}'], [f'T2s{g}'])
                copy('dve', EH, EEX[:, :, Cp:Cp + 2], T1K, ['HALO'])
                if p == 2:
                    state_out(EEX[:, :, Cp:Cp + 2], T1K, 2, [(0, 2, sc_p_o[l])])
            steps += halves(base[:, 1536:2048], stepB_hb)

            def stepB_bg(s, half):
                for cg in range(2):
                    g = half * 2 + cg
                    b = nb()
                    mm_fm(b, s, cg, hnT, C, ['hnT'])
                    tt(PBR[1][:, g, :C], bank(b)[:, :C], T2[:, g, :C], ALU.mult, [bk(b), f'T2{g}', f'T2s{g}'], [f'PB{g}'])
            steps += halves(base[:, 512:1024], stepB_bg)

            def stepC_g2(s, half):
                for cg in range(2):
                    g = half * 2 + cg
                    b = nb()
                    mm_fm(b, s, cg, hnT, C, ['hnT'])
                    P.op('act', lambda e, g=g, b=b: e.activation(out=T3[:, g, :C], in_=bank(b)[:, :C], func=AF.Sigmoid),
                         reads=[bk(b)], writes=[f'T3{g}'])
            steps += halves(base[:, 2560:3072], stepC_g2)

            def stepC_g1(s, half):
                CEX = T1
                if half == 0:
                    copy('dve', CEX[:, :, 0:30], CH, ['HALO'], ['T1h'])
                for cg in range(2):
                    g = half * 2 + cg
                    b = nb()
                    mm_fm(b, s, cg, hnT, C, ['hnT'])
                    tt(CEX[:, g, 30:30 + Cp], bank(b)[:, :Cp], T3[:, g, :Cp], ALU.mult, [bk(b), f'T3{g}'], [f'T1m{g}'])
                    if has_s:
                        tt(CEXS[:, g, :, 30:34], sview(bank(b)[:, Cp:Cp + 64]), sview(T3[:, g, Cp:Cp + 64]), ALU.mult, [bk(b), f'T3{g}'], ['MERGED'])
                        copy('dve', snew_view(g), CEXS[:, g, :, 30:34], ['MERGED'], ['SNEW'])
                if half == 0:
                    return
                if has_s:
                    state_out(SNEWf, ['SNEW'], 64, [(tq * 16, 16, cm_s_o[l][:, 26 + tq, :]) for tq in range(4)])
                ACC = T2
                for g in range(4):
                    rk = ['T1h', f'T1m{g}', 'CMW', 'CMB']
                    tsc(ACC[:, g, :Cp], CEX[:, g, 0:Cp], CMW[:, g, 0:1], CMB[:, g:g + 1], ALU.mult, ALU.add, rk, [f'T2{g}'])
                    for k in range(1, 31):
                        stt(ACC[:, g, :Cp], CEX[:, g, k:k + Cp], CMW[:, g, k:k + 1], ACC[:, g, :Cp], ALU.mult, ALU.add, rk + [f'T2{g}'], [f'T2{g}'])
                    if has_s:
                        As = sview(ACC[:, g, Cp:Cp + 64])
                        tsc(As, CEXS[:, g, :, 0:4], CMW[:, g, 0:1], CMB[:, g:g + 1], ALU.mult, ALU.add, ['MERGED', 'CMW', 'CMB'], [f'T2s{g}'])
                        for k in range(1, 31):
                            stt(As, CEXS[:, g, :, k:k + 4], CMW[:, g, k:k + 1], As, ALU.mult, ALU.add, ['MERGED', 'CMW', f'T2s{g}'], [f'T2s{g}'])
                copy('dve', CH, CEX[:, :, Cp:Cp + 30], T1K, ['HALO'])
                if p == 2:
                    state_out(CEX[:, :, Cp:Cp + 30], T1K, 30, [(0, 30, cm_p_o[l])])
                SQ = T3
                for g in range(4):
                    P.op('act', lambda e, g=g: e.activation(out=SQ[:, g, :C], in_=ACC[:, g, :C], func=AF.Square),
                         reads=[f'T2{g}', f'T2s{g}'], writes=[f'T3{g}'])
                bm = nb(); bq = nb()

                def fm(e):
                    ins = None
                    for g in range(4):
                        ins = e.matmul(bank(bm)[:, :C], lhsT=ONESF[:], rhs=ACC[:, g, :C], start=(g == 0), stop=(g == 3))
                    return ins
                P.op('pe', fm, reads=['ONESF'] + T2K, writes=[bk(bm)])

                def fq(e):
                    ins = None
                    for g in range(4):
                        ins = e.matmul(bank(bq)[:, :C], lhsT=ONESF[:], rhs=SQ[:, g, :C], start=(g == 0), stop=(g == 3))
                    return ins
                P.op('pe', fq, reads=['ONESF'] + T3K, writes=[bk(bq)])
                copy('act', MEAN[:, :C], bank(bm)[:, :C], [bk(bm)], ['MEAN'])
                tt(RSTD[:, :C], MEAN[:, :C], MEAN[:, :C], ALU.mult, ['MEAN'], ['RSTD'])
                tt(RSTD[:, :C], bank(bq)[:, :C], RSTD[:, :C], ALU.subtract, [bk(bq), 'RSTD'], ['RSTD'])
                rsqrt(RSTD[:, :C], RSTD[:, :C], 1.0, EPS, ['RSTD'], ['RSTD'])
                for g in range(4):
                    tt(ACC[:, g, :C], ACC[:, g, :C], MEAN[:, :C], ALU.subtract, [f'T2{g}', f'T2s{g}', 'MEAN'], [f'T2{g}', f'T2s{g}'])
                    tt(ACC[:, g, :C], ACC[:, g, :C], RSTD[:, :C], ALU.mult, [f'T2{g}', f'T2s{g}', 'RSTD'], [f'T2{g}', f'T2s{g}'])
                    P.op('act', lambda e, g=g: e.activation(out=PBR[2][:, g, :C], in_=ACC[:, g, :C], func=AF.Silu,
                                                            bias=CMBB[:, g:g + 1], scale=CMG[:, g:g + 1]),
                         reads=[f'T2{g}', f'T2s{g}', 'CMG', 'CMBB'], writes=[f'PC{g}'])
            steps += halves(base[:, 2048:2560], stepC_g1)

            def stepD_u(s, half):
                for cg in range(2):
                    g = half * 2 + cg
                    b = nb()
                    mm_fm(b, s, cg, hnT, C, ['hnT'])
                    copy('act', T3[:, g, :C], bank(b)[:, :C], [bk(b)], [f'T3{g}'])
            steps += halves(base[:, 3072:3584], stepD_u)

            def stepD_v(s, half):
                if half == 0:
                    banks_avail[0] = [0, 1, 2, 3]
                for ti, (t, np_, c0) in enumerate(tiles):
                    b = 4 + ti

                    def f(e, b=b, np_=np_, c0=c0):
                        ins = None
                        for k in range(16):
                            ins = e.matmul(bank(b)[:np_, half * 256:(half + 1) * 256], lhsT=hnT[:, k, c0:c0 + np_], rhs=WS[:, s, k, :], start=(k == 0), stop=(k == 15))
                        return ins
                    P.op('pe', f, reads=wq(s) + ['hnT'], writes=[bk(b)])
                if half == 0:
                    return
                if os.environ.get('DBG_CUT') == '1':
                    raise _Stop()
                for ti, (t, np_, c0) in enumerate(tiles):
                    b = 4 + ti
                    P.op('dve', lambda e, b=b, np_=np_: e.tensor_reduce(out=SS[:np_, 4:5], in_=bank(b)[:np_, :], axis=AX.X, op=ALU.add), reads=[bk(b)], writes=['SS4'])
                    P.op('act', lambda e, b=b, np_=np_: e.activation(out=VN[:np_, :], in_=bank(b)[:np_, :], func=AF.Square, accum_out=SS[:np_, 5:6]),
                         reads=[bk(b)], writes=T1K + ['SS5'])
                    if os.environ.get('DBG_CUT') == '2':
                        raise _Stop()
                    tsc(SS[:np_, 4:5], SS[:np_, 4:5], 1.0 / 512.0, None, ALU.mult, None, ['SS4'], ['SS4'])
                    tt(SS[:np_, 7:8], SS[:np_, 4:5], SS[:np_, 4:5], ALU.mult, ['SS4'], ['SS7'])
                    stt(SS[:np_, 5:6], SS[:np_, 5:6], 1.0 / 512.0, SS[:np_, 7:8], ALU.mult, ALU.subtract, ['SS5', 'SS7'], ['SS5'])
                    rsqrt(SS[:np_, 6:7], SS[:np_, 5:6], 1.0, EPS, ['SS5'], ['SS6'])
                    if os.environ.get('DBG_CUT') == '3':
                        raise _Stop()
                    tsc(VN[:np_, :], bank(b)[:np_, :], SS[:np_, 4:5], SS[:np_, 6:7], ALU.subtract, ALU.mult, [bk(b), 'SS4', 'SS6'], T1K)
                    if os.environ.get('DBG_CUT') == '4':
                        raise _Stop()
                    tt(VN[:np_, :], VN[:np_, :], SGG[:np_, :], ALU.mult, T1K + ['SGG'], T1K)
                    tt(VN[:np_, :], VN[:np_, :], SGB[:np_, :], ALU.add, T1K + ['SGB'], T1K)
                    copy('act', VNB[:np_, ti, :], VN[:np_, :], T1K, T1K)
                    if np_ == 64:
                        dma('sp', cv_s_o[l].rearrange("n t c -> (n t) c"), VN[:64, :], T1K, [], 'VNo')
                    if os.environ.get('DBG_CUT') == '5':
                        raise _Stop()
                    for g in range(4):
                        b2 = nb()
                        if np_ == 128:
                            def f2(e, b2=b2, g=g, ti=ti):
                                e.matmul(bank(b2)[:, :128], lhsT=VNB[:, ti, g * 128:(g + 1) * 128], rhs=SGWT[:, g, :], start=True, stop=False)
                                return e.matmul(bank(b2)[:, :128], lhsT=ONESB[:, :], rhs=SGBR[:, g, :], start=False, stop=True)
                            P.op('pe', f2, reads=T1K + ['SGWT', 'SGBR', 'ONESB'], writes=[bk(b2)])
                        else:
                            def f2(e, b2=b2, g=g, ti=ti):
                                e.matmul(bank(b2)[:, :64], lhsT=VNB[:64, ti, g * 128:(g + 1) * 128], rhs=SGWS[:, g, :], start=True, stop=False)
                                return e.matmul(bank(b2)[:, :64], lhsT=ONESB[:, :], rhs=SGBRS[:, g, :], start=False, stop=True)
                            P.op('pe', f2, reads=T1K + ['SGWS', 'SGBRS', 'ONESB'], writes=[bk(b2)])
                        if os.environ.get('DBG_CUT') == '6':
                            raise _Stop()
                        tt(PBR[3][:, g, c0:c0 + np_], bank(b2)[:, :np_], T3[:, g, c0:c0 + np_], ALU.mult, [bk(b2), f'T3{g}'], [f'PD{g}'])
                banks_avail[0] = list(range(8))
            steps += halves(base[:, 3584:4096], stepD_v)
            n_branch_steps = len(steps)

            wouts = [w_pool_out, w_sc_out, w_cm_out, w_sg_out]
            pkeys = [[f'PA{g}' for g in range(4)], [f'PB{g}' for g in range(4)], [f'PC{g}' for g in range(4)], [f'PD{g}' for g in range(4)]]
            MACC = [T1[:, dc, :448] for dc in range(2)]
            SIG = [T2[:, j, :448] for j in range(2)]
            TMP = [T2[:, 2 + j, :448] for j in range(2)]
            MK = ['MACC0', 'MACC1', 'SIG0', 'SIG1', 'TMP0', 'TMP1', 'WO0', 'WO1']
            won = [0]
            for dg in range(8):
                for i in range(4):
                    def stepM(s, dg=dg, i=i):
                        wo = won[0] % 2
                        won[0] += 1
                        dma('pool', WO[wo][:], wouts[i][l][:, dg * 256:(dg + 1) * 256].rearrange("(k p) c -> p k c", p=128), [], [f'WO{wo}'], f'WO{wo}')
                        for dc in range(2):
                            d = dg * 2 + dc
                            bg_ = nb()
                            mm_fm(bg_, s, dc, hnT, C, ['hnT'])
                            by = nb()

                            def fy(e, by=by, wo=wo, dc=dc, i=i):
                                ins = None
                                for kc in range(4):
                                    ins = e.matmul(bank(by)[:, :C], lhsT=WO[wo][:, kc, dc * 128:(dc + 1) * 128], rhs=PBR[i][:, kc, :C],
                                                   start=(kc == 0), stop=(kc == 3))
                                return ins
                            P.op('pe', fy, reads=[f'WO{wo}'] + pkeys[i], writes=[bk(by)])
                            sg = SIG[dc]
                            sk = f'SIG{dc}'
                            P.op('act', lambda e, bg_=bg_, sg=sg, i=i, d=d: e.activation(out=sg[:, :C], in_=bank(bg_)[:, :C], func=AF.Sigmoid,
                                                                                      bias=BGATE[:, i, d:d + 1]),
                                 reads=[bk(bg_), 'BGATE'], writes=[sk])
                            if i == 0:
                                tt(MACC[dc][:, :C], sg[:, :C], bank(by)[:, :C], ALU.mult, [sk, bk(by)], [f'MACC{dc}'])
                            else:
                                tm = TMP[dc]
                                tk = f'TMP{dc}'
                                tt(tm[:, :C], sg[:, :C], bank(by)[:, :C], ALU.mult, [sk, bk(by)], [tk])
                                if i < 3:
                                    tt(MACC[dc][:, :C], MACC[dc][:, :C], tm[:, :C], ALU.add, [f'MACC{dc}', tk], [f'MACC{dc}'])
                                else:
                                    tt(MERGED[:, d, :C], MACC[dc][:, :C], tm[:, :C], ALU.add, [f'MACC{dc}', tk], ['MERGED'])
                    c0_ = 4096 + i * 2048 + dg * 256
                    steps.append((full(base[:, c0_:c0_ + 256]), stepM))

            for dg in range(8):
                def stepO(s, dg=dg):
                    for (t, np_, c0) in tiles:
                        b = nb()

                        def f(e, b=b, np_=np_, c0=c0):
                            ins = None
                            for k in range(16):
                                ins = e.matmul(bank(b)[:np_, 0:256], lhsT=MERGED[:, k, c0:c0 + np_], rhs=WS[:, s, k, :], start=(k == 0), stop=(k == 15))
                            return ins
                        P.op('pe', f, reads=wq(s) + ['MERGED'], writes=[bk(b)])
                        tt(X[:np_, t, dg * 256:(dg + 1) * 256], X[:np_, t, dg * 256:(dg + 1) * 256], bank(b)[:np_, 0:256], ALU.add,
                           [f'X{t}', bk(b)], [f'X{t}'])
                steps.append((full(w_o[l][:, dg * 256:(dg + 1) * 256]), stepO))

            def hook(i):
                if i == n_branch_steps:
                    guard(T1K + T2K + T3K + ['MERGED', 'EXS_E'], MK + ['MERGED'])
            run_steps(steps, hook)
            if os.environ.get('DBG_END') == str(p):
                raise _Stop()
            guard(MK, T1K + T2K + T3K)
            if os.environ.get('DBG_END2') == str(p):
                raise _Stop()

    def attn_phase(l):
        cv_ = Carver()
        hnT = cv_.bf16([128, 16, 448])
        HN1 = cv_.bf16([128, D])
        QTa = cv_.bf16([128, 4, 448])
        OT = cv_.bf16([128, 4, 448])
        PR = cv_.f32([128, 4, 256])
        PN = cv_.bf16([128, 4, 256])
        PT = cv_.bf16([128, 8, 128])
        PTn = [cv_.bf16([128, 8, 64]) for _ in range(2)]
        QBLK = cv_.bf16([128, 4, 16, 64])
        KS = [cv_.f32([128, 2, 512]) for _ in range(2)]
        KTS = [cv_.bf16([128, 4, 256]) for _ in range(2)]
        VS = [cv_.bf16([128, 2, 512]) for _ in range(2)]
        STG = cv_.f32([128, 256])
        KT = cv_.bf16([128, 4, 256]); VB = cv_.bf16([128, 2, 512])
        MEMX = cv_.f32([128, 2, D])
        load_gb(g_mem[l:l + 1, :])
        for mt in range(2):
            dma('sp', MEMX[:, mt, :], mem[mt * 128:(mt + 1) * 128, :], [], [f'MEMX{mt}'], f'MEMX{mt}')
            norm_tile(MEMX[:, mt, :], [f'MEMX{mt}'], 128, HN1, ['HN1'], hnT, 'hnT', mt * 128)

        def kv_k(s, half):
            for mt in range(2):
                b = nb()

                def f(e, b=b, mt=mt):
                    ins = None
                    for k in range(16):
                        ins = e.matmul(bank(b)[:, 0:256], lhsT=hnT[:, k, mt * 128:(mt + 1) * 128], rhs=WS[:, s, k, :], start=(k == 0), stop=(k == 15))
                    return ins
                P.op('pe', f, reads=wq(s) + ['hnT'], writes=[bk(b)])
                copy('act', STG[:, :], bank(b)[:, 0:256], [bk(b)], ['STG'])
                dma('sp', mk_o[l][mt * 128:(mt + 1) * 128, half * 256:(half + 1) * 256], STG[:, :], ['STG'], [], 'STGo')
            for cg in range(2):
                h = half * 2 + cg
                b = nb()
                mm_fm(b, s, cg, hnT, 256, ['hnT'])
                copy(evac_eng(), KT[:, h, :], bank(b)[:, :256], [bk(b)], ['KT'])

        def kv_v(s, half):
            for mt in range(2):
                b = nb()

                def f(e, b=b, mt=mt):
                    ins = None
                    for k in range(16):
                        ins = e.matmul(bank(b)[:, 0:256], lhsT=hnT[:, k, mt * 128:(mt + 1) * 128], rhs=WS[:, s, k, :], start=(k == 0), stop=(k == 15))
                    return ins
                P.op('pe', f, reads=wq(s) + ['hnT'], writes=[bk(b)])
                copy('act', STG[:, :], bank(b)[:, 0:256], [bk(b)], ['STG'])
                copy('dve', VB[:, mt, half * 256:(half + 1) * 256], bank(b)[:, 0:256], [bk(b)], ['VB'])
                dma('sp', mv_o[l][mt * 128:(mt + 1) * 128, half * 256:(half + 1) * 256], STG[:, :], ['STG'], [], 'STGo')
        run_steps(halves(w_xk[l], kv_k) + halves(w_xv[l], kv_v))

        load_gb(g_x[l:l + 1, :])
        tcount = [0]

        def softmax_and_pt(np_, bs, S4=None, skeys=None, tbank=None):
            if S4 is None:
                S4 = PSB[bs][:np_, 0:1024].rearrange("p (h m) -> p h m", h=4)
                skeys = [bk(bs * 4), bk(bs * 4 + 1)]
                tbank = bs * 4 + 2
            P.op('dve', lambda e: e.tensor_reduce(out=SS[:np_, 0:4], in_=S4, axis=AX.X, op=ALU.max), reads=skeys, writes=['SS0'])
            tsc(SS[:np_, 0:4], SS[:np_, 0:4], -SCL, None, ALU.mult, None, ['SS0'], ['SS0'])
            for h in range(4):
                P.op('act', lambda e, h=h: e.activation(out=PR[:np_, h, :], in_=S4[:, h, :], func=AF.Exp, bias=SS[:np_, h:h + 1], scale=SCL,
                                                        accum_out=SS[:np_, 4 + h:5 + h]),
                     reads=skeys + ['SS0'], writes=[f'PR{h}', f'SSa{h}'])
            sak = [f'SSa{h}' for h in range(4)]
            P.op('dve', lambda e: e.reciprocal(out=SS[:np_, 4:8], in_=SS[:np_, 4:8]), reads=sak, writes=sak)
            tt(PN[:np_, :, :], PR[:np_, :, :], SS[:np_, 4:8].unsqueeze(2).to_broadcast([np_, 4, 256]), ALU.mult,
               [f'PR{h}' for h in range(4)] + sak, ['PN'])
            b = tbank
            pb = bank(b).bitcast(BF16)

            def f(e):
                ins = None
                for h in range(4):
                    for mc in range(2):
                        j = h * 2 + mc
                        ins = e.transpose(out=pb[:, j * 128:j * 128 + np_], in_=PN[:np_, h, mc * 128:(mc + 1) * 128], identity=IDB[:np_, :np_])
                return ins
            P.op('pe', f, reads=['PN', 'IDB'], writes=[bk(b)])
            copy(evac_eng(), PT[:, :, :np_], pb.rearrange("p (j c) -> p j c", c=128)[:, :, :np_], [bk(b)], ['PT'])

        for p in range(3):
            tiles, C, Cp = pass_cols(p)
            for (t, np_, c0) in tiles:
                norm_tile(X[:np_, t, :], [f'X{t}'], np_, HN1, ['HN1'], hnT, 'hnT', c0)

            def step_q(s, half):
                for cg in range(2):
                    h = half * 2 + cg
                    b = nb()
                    mm_fm(b, s, cg, hnT, C, ['hnT'])
                    copy(evac_eng(), QTa[:, h, :C], bank(b)[:, :C], [bk(b)], ['QTa'])
                if half == 0:
                    return
                for (t, np_, c0) in tiles:
                    bs = tcount[0] % 2
                    tcount[0] += 1
                    if np_ == 128:
                        def f(e, bs=bs, c0=c0):
                            ins = None
                            for h in range(4):
                                ins = e.matmul(PSB[bs][:, h * 256:(h + 1) * 256], lhsT=QTa[:, h, c0:c0 + 128], rhs=KT[:, h, :], start=True, stop=True)
                            return ins
                        P.op('pe', f, reads=['QTa', 'KT'], writes=[bk(bs * 4), bk(bs * 4 + 1)])
                        softmax_and_pt(128, bs)
                        b = bs * 4 + 3

                        def f2(e, b=b):
                            ins = None
                            for h in range(4):
                                for mc in range(2):
                                    ins = e.matmul(bank(b)[:, h * 128:(h + 1) * 128], lhsT=VB[:, mc, h * 128:(h + 1) * 128], rhs=PT[:, h * 2 + mc, :],
                                                   start=(mc == 0), stop=(mc == 1))
                            return ins
                        P.op('pe', f2, reads=['VB', 'PT'], writes=[bk(b)])
                        copy(evac_eng(), OT[:, :, c0:c0 + 128], bank(b).rearrange("p (h c) -> p h c", h=4), [bk(b)], ['OT'])
                    else:
                        for h in range(4):
                            tt(QBLK[:, h, :, :], QTa[:, h, c0:c0 + 64].unsqueeze(1).to_broadcast([128, 16, 64]), AMASK[:, :, :], ALU.mult,
                               ['QTa', 'AMASK'], [f'QBLK{h}'])
                        obs = 1 - bs
                        for n in range(16):
                            kb = n % 2
                            dma('sp', KS[kb][:], ck[l, n].rearrange("(mc p) c -> p mc c", p=128), [], [f'KS{kb}'], f'KS{kb}')
                            for hf_ in range(2):
                                tb = obs * 4 + hf_

                                def ft(e, kb=kb, hf_=hf_, tb=tb):
                                    ins = None
                                    for hh in range(2):
                                        h = hf_ * 2 + hh
                                        for mc in range(2):
                                            ins = e.transpose(out=bank(tb)[:, (hh * 2 + mc) * 128:(hh * 2 + mc + 1) * 128],
                                                              in_=KS[kb][:, mc, h * 128:(h + 1) * 128], identity=IDF[:, :])
                                    return ins
                                P.op('pe', ft, reads=[f'KS{kb}', 'IDF'], writes=[bk(tb)])
                                copy(evac_eng(), KTS[kb][:, hf_ * 2:hf_ * 2 + 2, :], bank(tb).rearrange("p (h m) -> p h m", h=2), [bk(tb)], [f'KTS{kb}_{hf_}'])

                            def fs(e, n=n, kb=kb, bs=bs):
                                ins = None
                                for h in range(4):
                                    ins = e.matmul(PSB[bs][:64, h * 512:h * 512 + 256], lhsT=QBLK[:, h, n, :], rhs=KTS[kb][:, h, :],
                                                   start=(n == 0), stop=(n == 15))
                                return ins
                            P.op('pe', fs, reads=[f'QBLK{h}' for h in range(4)] + [f'KTS{kb}_0', f'KTS{kb}_1'], writes=[bk(bs * 4 + h) for h in range(4)])
                        softmax_and_pt(64, bs, S4=PSB[bs][:64, :].rearrange("p (h x) -> p h x", h=4)[:, :, 0:256],
                                       skeys=[bk(bs * 4 + h) for h in range(4)], tbank=obs * 4 + 3)
                        ob = obs * 4 + 2
                        for n in range(16):
                            vb_ = n % 2
                            dma('pool', VS[vb_][:], cvv[l, n].rearrange("(mc p) c -> p mc c", p=128), [], [f'VS{vb_}'], f'VS{vb_}')
                            tt(PTn[vb_][:, :, :], PT[:, :, :64], AMASK[:, n:n + 1, :].to_broadcast([128, 8, 64]), ALU.mult, ['PT', 'AMASK'], [f'PTn{vb_}'])

                            def fo(e, n=n, vb_=vb_, ob=ob):
                                ins = None
                                for h in range(4):
                                    for mc in range(2):
                                        ins = e.matmul(bank(obs * 4 + h)[:, 0:64], lhsT=VS[vb_][:, mc, h * 128:(h + 1) * 128], rhs=PTn[vb_][:, h * 2 + mc, :],
                                                       start=(n == 0 and mc == 0), stop=(n == 15 and mc == 1))
                                return ins
                            P.op('pe', fo, reads=[f'VS{vb_}', f'PTn{vb_}'], writes=[bk(obs * 4 + h) for h in range(4)])
                        copy(evac_eng(), OT[:, :, c0:c0 + 64], PSB[obs][:, :].rearrange("p (h x) -> p h x", h=4)[:, :, 0:64], [bk(obs * 4 + h) for h in range(4)], ['OT'])

            def step_o(s, half):
                for (t, np_, c0) in tiles:
                    for dq in range(4):
                        dg = half * 4 + dq
                        b = nb()

                        def f(e, b=b, np_=np_, c0=c0, dq=dq):
                            ins = None
                            for h in range(4):
                                ins = e.matmul(bank(b)[:np_, 0:256], lhsT=OT[:, h, c0:c0 + np_], rhs=WS[:, s, dq * 4 + h, :], start=(h == 0), stop=(h == 3))
                            return ins
                        P.op('pe', f, reads=wq(s) + ['OT'], writes=[bk(b)])
                        tt(X[:np_, t, dg * 256:(dg + 1) * 256], X[:np_, t, dg * 256:(dg + 1) * 256], bank(b)[:np_, 0:256], ALU.add,
                           [f'X{t}', bk(b)], [f'X{t}'])
            xo_steps = []
            for half in range(2):
                parts = [(dq * 4, 4, w_xo[l][:, (half * 4 + dq) * 256:(half * 4 + dq + 1) * 256]) for dq in range(4)]
                xo_steps.append((parts, (lambda s, half=half: step_o(s, half))))
            run_steps(halves(w_xq[l], step_q) + xo_steps)

    def peer_phase(l):
        cv_ = Carver()
        hnT = cv_.bf16([128, 16, 448])
        HN1 = cv_.bf16([128, D])
        QT = cv_.bf16([128, 16, 448])
        CAND = cv_.f32([128, 8, 256])
        S2 = cv_.f32([128, 256])
        SV = cv_.f32([128, 16, 16]); SIi = cv_.i32([128, 16, 16]); SIF = cv_.f32([128, 16, 16])
        CVv = cv_.f32([128, 8, 16]); CIi = cv_.i32([128, 8, 16])
        AI = cv_.i32([128, 8, 16]); BI = cv_.i32([128, 8, 16])
        AFl = cv_.f32([128, 8, 16]); BFl = cv_.f32([128, 8, 16])
        I1 = cv_.f32([128, 8, 16]); I2 = cv_.f32([128, 8, 16]); EIDF = cv_.f32([128, 128])
        GW = cv_.f32([128, 8, 16])
        EIDT = cv_.i32([128, 128]); GWT = cv_.f32([128, 128])
        ACTV = cv_.f32([128, 4]); CMAT = cv_.f32([128, 128])
        KEYST = cv_.bf16([128, 2048])
        UG = [cv_.f32([128, D]) for _ in range(2)]
        VG = [cv_.f32([128, D]) for _ in range(2)]
        S = SCR[:, 0:2048]
        SI = SIi.bitcast(U32); CI = CIi.bitcast(U32)
        MASK = CAND.rearrange("p h (a b) -> p h a b", a=16)

        dma('pool', KEYST[:], keysT[l], [], ['KEYST'], 'KEYST')
        load_gb(g_peer[l:l + 1, :])
        for p in range(3):
            tiles, C, Cp = pass_cols(p)
            for ti, (t, np_, c0) in enumerate(tiles):
                norm_tile(X[:np_, t, :], [f'X{t}'], np_, HN1, ['HN1'], hnT, 'hnT', c0)
            steps = []
            for sg_ in range(4):
                def step_pq(s, half, sg_=sg_):
                    for cg in range(2):
                        j = sg_ * 4 + half * 2 + cg
                        b = nb()
                        mm_fm(b, s, cg, hnT, C, ['hnT'])
                        copy(evac_eng(), QT[:, j, :C], bank(b)[:, :C], [bk(b)], ['QT'])
                steps += halves(w_pq[l][:, sg_ * 512:(sg_ + 1) * 512], step_pq)
            run_steps(steps)
            def peer_tile(ti, t, np_, c0):
                norm_tile(X[:np_, t, :], [f'X{t}'], np_, HN1, ['HN1'], None, None, 0)

                def fsc(e, np_=np_, c0=c0):
                    ins = None
                    for j in range(16):
                        ins = e.matmul(PSB[0][:np_, j * 128:(j + 1) * 128], lhsT=QT[:, j, c0:c0 + np_], rhs=KEYST[:, j * 128:(j + 1) * 128], start=True, stop=True)
                    return ins
                P.op('pe', fsc, reads=['QT', 'KEYST'], writes=[bk(0), bk(1), bk(2), bk(3)])
                for q_ in range(4):
                    copy(evac_eng(), S[:np_, q_ * 512:(q_ + 1) * 512], bank(q_)[:np_, :], [bk(q_)], ['hnT'])
                for j in range(16):
                    Sj = S[:np_, j * 128:(j + 1) * 128]
                    P.op('dve', lambda e, j=j, Sj=Sj: e.max(out=SV[:np_, j, 0:8], in_=Sj), reads=['hnT'], writes=['SV'])
                    P.op('dve', lambda e, j=j, Sj=Sj: e.max_index(out=SI[:np_, j, 0:8], in_max=SV[:np_, j, 0:8], in_values=Sj), reads=['hnT', 'SV'], writes=['SI'])
                    P.op('dve', lambda e, j=j, Sj=Sj: e.match_replace(out=S2[:np_, 0:128], in_to_replace=SV[:np_, j, 0:8], in_values=Sj, imm_value=-1e30),
                         reads=['hnT', 'SV'], writes=['S2'])
                    P.op('dve', lambda e, j=j: e.max(out=SV[:np_, j, 8:16], in_=S2[:np_, 0:128]), reads=['S2'], writes=['SV'])
                    P.op('dve', lambda e, j=j: e.max_index(out=SI[:np_, j, 8:16], in_max=SV[:np_, j, 8:16], in_values=S2[:np_, 0:128]), reads=['S2', 'SV'], writes=['SI'])
                SV4 = SV.rearrange("p (h q) k -> p h q k", q=2)
                SIF4 = SIF.rearrange("p (h q) k -> p h q k", q=2)
                copy('dve', SIF[:np_], SIi[:np_], ['SI'], ['SIF'])
                C4 = CAND.rearrange("p h (a b) -> p h a b", a=16)
                tt(C4[:np_], SV4[:np_, :, 0, :].unsqueeze(3).to_broadcast([np_, 8, 16, 16]), SV4[:np_, :, 1, :].unsqueeze(2).to_broadcast([np_, 8, 16, 16]),
                   ALU.add, ['SV'], ['CAND'])
                for h in range(8):
                    Ch = CAND[:np_, h, :]
                    P.op('dve', lambda e, h=h, Ch=Ch: e.max(out=CVv[:np_, h, 0:8], in_=Ch), reads=['CAND'], writes=['CV'])
                    P.op('dve', lambda e, h=h, Ch=Ch: e.max_index(out=CI[:np_, h, 0:8], in_max=CVv[:np_, h, 0:8], in_values=Ch), reads=['CAND', 'CV'], writes=['CI'])
                    P.op('dve', lambda e, h=h, Ch=Ch: e.match_replace(out=S2[:np_, :], in_to_replace=CVv[:np_, h, 0:8], in_values=Ch, imm_value=-1e30),
                         reads=['CAND', 'CV'], writes=['S2'])
                    P.op('dve', lambda e, h=h: e.max(out=CVv[:np_, h, 8:16], in_=S2[:np_, :]), reads=['S2'], writes=['CV'])
                    P.op('dve', lambda e, h=h: e.max_index(out=CI[:np_, h, 8:16], in_max=CVv[:np_, h, 8:16], in_values=S2[:np_, :]), reads=['S2', 'CV'], writes=['CI'])
                P.op('dve', lambda e: e.tensor_single_scalar(out=AI[:np_], in_=CIi[:np_], scalar=4, op=ALU.arith_shift_right), reads=['CI'], writes=['AI'])
                P.op('dve', lambda e: e.tensor_single_scalar(out=BI[:np_], in_=CIi[:np_], scalar=15, op=ALU.bitwise_and), reads=['CI'], writes=['BI'])
                copy('dve', AFl[:np_], AI[:np_], ['AI'], ['AFl'])
                copy('dve', BFl[:np_], BI[:np_], ['BI'], ['BFl'])
                iota4 = IOTA[:np_, :].unsqueeze(1).unsqueeze(1).to_broadcast([np_, 8, 16, 16])
                for (Fl, fk, q, Io, ik) in ((AFl, 'AFl', 0, I1, 'I1'), (BFl, 'BFl', 1, I2, 'I2')):
                    tt(MASK[:np_], Fl[:np_].unsqueeze(3).to_broadcast([np_, 8, 16, 16]), iota4, ALU.is_equal, [fk, 'IOTA', 'CV', 'CI'], ['CAND'])
                    tt(MASK[:np_], MASK[:np_], SIF4[:np_, :, q, :].unsqueeze(2).to_broadcast([np_, 8, 16, 16]), ALU.mult, ['CAND', 'SIF'], ['CAND'])
                    P.op('dve', lambda e, Io=Io: e.tensor_reduce(out=Io[:np_], in_=MASK[:np_], axis=AX.X, op=ALU.add), reads=['CAND'], writes=[ik])
                stt(EIDF[:np_, :], I1[:np_].rearrange("p h k -> p (h k)"), 128.0, I2[:np_].rearrange("p h k -> p (h k)"), ALU.mult, ALU.add,
                    ['I1', 'I2'], ['EIDF'])
                tt(GW[:np_], CVv[:np_], CVv[:np_, :, 0:1].to_broadcast([np_, 8, 16]), ALU.subtract, ['CV'], ['GW'])
                P.op('act', lambda e: e.activation(out=GW[:np_], in_=GW[:np_], func=AF.Exp), reads=['GW'], writes=['GW'])
                P.op('dve', lambda e: e.tensor_reduce(out=SS[:np_, 0:8], in_=GW[:np_], axis=AX.X, op=ALU.add), reads=['GW'], writes=['SS0'])
                P.op('dve', lambda e: e.reciprocal(out=SS[:np_, 0:8], in_=SS[:np_, 0:8]), reads=['SS0'], writes=['SS0'])
                tt(GW[:np_], GW[:np_], SS[:np_, 0:8].unsqueeze(2).to_broadcast([np_, 8, 16]), ALU.mult, ['GW', 'SS0'], ['GW'])
                P.op('pe', lambda e: e.transpose(out=bank(0)[:, :np_], in_=EIDF[:np_, :], identity=IDF[:np_, :np_]), reads=['EIDF', 'IDF'], writes=[bk(0)])
                copy('dve', EIDT[:, :np_], bank(0)[:, :np_], [bk(0)], ['EIDT'])
                P.op('pe', lambda e: e.transpose(out=bank(1)[:, :np_], in_=GW[:np_].rearrange("p h k -> p (h k)"), identity=IDF[:np_, :np_]), reads=['GW', 'IDF'], writes=[bk(1)])
                copy('act', GWT[:, :np_], bank(1)[:, :np_], [bk(1)], ['GWT'])
                okeys = [bk(4), bk(5), bk(6), bk(7)]
                akeys = [bk(0), bk(1), bk(2), bk(3)]
                def fb_op(tk):
                    def fb(e, tk=tk):
                        ins = None
                        for c in range(4):
                            ins = e.matmul(bank(c)[:, :], lhsT=IDB[:np_, tk:tk + 1].to_broadcast([np_, 128]), rhs=HN1[:np_, c * 512:(c + 1) * 512], start=True, stop=True)
                        return ins
                    P.op('pe', fb, reads=['IDB', 'HN1'], writes=akeys)

                def gather(tk, gb_):
                    P.op('pool', lambda e, tk=tk, gb_=gb_: e.indirect_dma_start(out=UG[gb_], out_offset=None, in_=peer_u.rearrange("l r d -> (l r) d"), element_offset=l * peer_rows * D,
                                                                               in_offset=bass.IndirectOffsetOnAxis(ap=EIDT[:, tk:tk + 1], axis=0)),
                         reads=['EIDT'], writes=[f'UG{gb_}'], dma_sem=f'UG{gb_}')
                    P.op('pool', lambda e, tk=tk, gb_=gb_: e.indirect_dma_start(out=VG[gb_], out_offset=None, in_=peer_v.rearrange("l r d -> (l r) d"), element_offset=l * peer_rows * D,
                                                                               in_offset=bass.IndirectOffsetOnAxis(ap=EIDT[:, tk:tk + 1], axis=0)),
                         reads=['EIDT'], writes=[f'VG{gb_}'], dma_sem=f'VG{gb_}')
                gather(0, 0)
                fb_op(0)
                for tk in range(np_):
                    gb_ = tk % 2
                    if tk + 1 < np_:
                        gather(tk + 1, 1 - gb_)
                    P.op('dve', lambda e: e.memset(ACTV[:, 0:1], 0.0), writes=['ACTV'])
                    stt(UG[gb_], UG[gb_], 1.0, PSB[0][:, :], ALU.mult, ALU.mult, [f'UG{gb_}', 'ACTV'] + akeys, [f'UG{gb_}', 'ACTV'], accum=ACTV[:, 0:1])
                    if tk + 1 < np_:
                        fb_op(tk + 1)
                    P.op('act', lambda e: e.activation(out=ACTV[:, 1:2], in_=ACTV[:, 0:1], func=AF.Gelu), reads=['ACTV'], writes=['GEL'])
                    tsc(CMAT[:, :], WWIN[:, 127 - tk:255 - tk], ACTV[:, 1:2], GWT[:, tk:tk + 1], ALU.mult, ALU.mult, ['WWIN', 'GEL', 'GWT'], ['CMAT'])

                    def fv(e, tk=tk, gb_=gb_):
                        ins = None
                        for c in range(4):
                            ins = e.matmul(bank(4 + c)[:np_, :], lhsT=CMAT[:, :np_], rhs=VG[gb_][:, c * 512:(c + 1) * 512], start=(tk == 0), stop=(tk == np_ - 1))
                        return ins
                    P.op('pe', fv, reads=['CMAT', f'VG{gb_}'], writes=okeys)
                tt(X[:np_, t, :], X[:np_, t, :], PSB[1][:np_, :], ALU.add, [f'X{t}'] + okeys, [f'X{t}'])
            for ti, (t, np_, c0) in enumerate(tiles):
                peer_tile(ti, t, np_, c0)

    def final_phase():
        load_gb(g_final[0:1, :])
        YS = [SCR[:, i * D:(i + 1) * D] for i in range(2)]
        for t in range(NT):
            np_ = 128 if t < 9 else 64
            yb = t % 2
            P.op('act', lambda e, t=t, np_=np_, yb=yb: e.activation(out=YS[yb][:np_, :], in_=X[:np_, t, :], func=AF.Square, accum_out=SS[:np_, 0:1]),
                 reads=[f'X{t}'], writes=[f'YS{yb}', 'SS0'])
            rsqrt(SS[:np_, 2:3], SS[:np_, 0:1], 1.0 / D, EPS, ['SS0'], ['SS2'])
            stt(YS[yb][:np_, :], X[:np_, t, :], SS[:np_, 2:3], GB[:np_, :], ALU.mult, ALU.mult, [f'X{t}', 'SS2', 'GB'], [f'YS{yb}'])
            dma('sp', y_o[t * 128:t * 128 + np_, :], YS[yb][:np_, :], [f'YS{yb}'], [], f'YSo{yb}')

    def dump_x():
        for t in range(NT):
            np_ = 128 if t < 9 else 64
            dma('sp', dbg_o[t * 128:t * 128 + np_, :], X[:np_, t, :], [f'X{t}'], [], 'dbg')

    done = False
    for l in range(2):
        if stop_after == ('load', 0):
            done = True
            break
        try:
            mixer_phase(l)
        except _Stop:
            done = True
            break
        barrier()
        if stop_after == ('mixer', l):
            done = True
            break
        attn_phase(l)
        barrier()
        if stop_after == ('attn', l):
            done = True
            break
        peer_phase(l)
        barrier()
        if stop_after == ('peer', l):
            done = True
            break
    if dbg:
        dump_x()
    if not done:
        final_phase()
    P.emit(nc)
    return nc, st


_CACHE = {}


def _consts():
    identf = np.eye(128, dtype=np.float32)
    s = np.arange(128)
    tri = (s[:, None] <= s[None, :]).astype(np.float32)
    q = np.arange(64)
    blk = ((q[:, None] // 4 == q[None, :] // 4) & (q[:, None] % 4 <= q[None, :] % 4)).astype(np.float32)
    wwin = np.zeros((128, 255), np.float32)
    wwin[:, 127] = 1.0
    iota = np.tile(np.arange(16, dtype=np.float32)[None, :], (128, 1))
    am = (np.arange(64)[None, :] // 4 == np.arange(16)[:, None]).astype(np.float32)
    amask = np.tile(am[None], (128, 1, 1)).astype(np.float32)
    return dict(c_identf=identf, c_tri=tri, c_blk=blk, c_wwin=wwin, c_iota=iota, c_amask=amask)


def make_in_maps(inp, peer_rows=16384):
    f = lambda a: np.ascontiguousarray(a, dtype=np.float32)
    shared = {}
    for k in ['g_mix', 'w_in', 'b_gate', 'pool_w', 'pool_scale', 'w_pool_out', 'sc_w', 'w_sc_out', 'cm_w', 'cm_b', 'cm_ln_g', 'cm_ln_b',
              'w_cm_out', 'sg_ln_g', 'sg_ln_b', 'sg_b', 'w_sg_out', 'w_o', 'g_x', 'g_mem', 'w_xq', 'w_xk', 'w_xv', 'w_xo', 'g_peer', 'w_pq',
              'peer_u', 'peer_v']:
        shared[k] = f(inp[k]) if not k.startswith('peer_') else f(inp[k][:, :peer_rows])
    shared['g_final'] = f(inp['g_final']).reshape(1, D)
    sgw = f(inp['sg_w'])
    shared['sg_wT'] = np.ascontiguousarray(sgw.transpose(0, 3, 1, 2))
    small = sgw[:, :, 0:4, 0:4].transpose(0, 3, 1, 2)
    shared['sgws'] = np.ascontiguousarray(np.tile(small, (1, 16, 1, 16)))
    pk = f(inp['peer_keys'])
    shared['keysT'] = np.ascontiguousarray(pk.transpose(0, 4, 1, 2, 3).reshape(2, 128, 2048))
    shared.update(_consts())
    xp = f(inp['x_prompt']); xs = f(inp['x_sample'])
    maps = []
    for c in range(8):
        b, hf = c // 2, c % 2
        m = dict(shared)
        xin = np.zeros((TOK, D), np.float32)
        if hf == 1:
            xin[0:1152] = xp[b, 896:2048]
        else:
            xin[128:1152] = xp[b, 0:1024]
        xin[1152:1216] = xs[16 * c:16 * c + 16].reshape(64, D)
        m['xin'] = xin
        m['flag'] = np.full((128, 1), float(hf), np.float32)
        cnt = np.zeros((128, 4, 15), np.float32)
        for g, w in enumerate(POOL_W):
            for pos in range(15):
                cnt[:, g, pos] = (1.0 / min(w, pos + 1)) if hf == 0 else (1.0 / w)
        m['cnt'] = cnt
        m['mem'] = f(inp['mem_prompt'][b])
        m['ck'] = f(inp['cache_mem_k'][:, 16 * c:16 * c + 16]).reshape(2, 16, 256, 512)
        m['cvv'] = f(inp['cache_mem_v'][:, 16 * c:16 * c + 16]).reshape(2, 16, 256, 512)
        m['spool'] = f(inp['state_pool'][:, 16 * c:16 * c + 16])
        m['ssc'] = f(inp['state_sconv'][:, 16 * c:16 * c + 16])
        m['scc'] = f(inp['state_cconv'][:, 16 * c:16 * c + 16])
        maps.append(m)
    return maps


def assemble(res):
    y_p = np.zeros((4, 2048, D), np.float32); y_s = np.zeros((128, 4, D), np.float32)
    mk = np.zeros((2, 4, 256, 4, 128), np.float32); mv = np.zeros_like(mk)
    pool_p = np.zeros((2, 4, 15, 512), np.float32); sc_p = np.zeros((2, 4, 2, 512), np.float32); cm_p = np.zeros((2, 4, 30, 512), np.float32)
    pool_s = np.zeros((2, 128, 15, 512), np.float32); sc_s = np.zeros((2, 128, 2, 512), np.float32); cm_s = np.zeros((2, 128, 30, 512), np.float32)
    cv_s = np.zeros((2, 128, 4, 512), np.float32)
    for c in range(8):
        r = res[c]
        b, hf = c // 2, c % 2
        y = np.asarray(r['y'])
        y_p[b, hf * 1024:(hf + 1) * 1024] = y[128:1152]
        y_s[16 * c:16 * c + 16] = y[1152:1216].reshape(16, 4, D)
        if hf == 1:
            mk[:, b] = np.asarray(r['mk']).reshape(2, 256, 4, 128)
            mv[:, b] = np.asarray(r['mv']).reshape(2, 256, 4, 128)
            pool_p[:, b] = np.asarray(r['pool_p']); sc_p[:, b] = np.asarray(r['sc_p']); cm_p[:, b] = np.asarray(r['cm_p'])
        pool_s[:, 16 * c:16 * c + 16] = np.asarray(r['pool_s']); sc_s[:, 16 * c:16 * c + 16] = np.asarray(r['sc_s'])
        cm_s[:, 16 * c:16 * c + 16] = np.asarray(r['cm_s']); cv_s[:, 16 * c:16 * c + 16] = np.asarray(r['cv_s'])
    return (y_p, y_s, mk, mv, pool_p, sc_p, cm_p, pool_s, sc_s, cm_s, cv_s)


def kernel(**inputs):
    if 'nc' not in _CACHE:
        _CACHE['nc'] = build_program()
    nc, _st = _CACHE['nc']
    maps = make_in_maps(inputs)
    res = run_bass_kernel_spmd(nc, maps, core_ids=list(range(8)))
    return assemble(res.results)
```

```python
import os
import numpy as np
import concourse.bass as bass
import concourse.mybir as mybir
from concourse.bass_utils import run_bass_kernel_spmd
from contextlib import ExitStack

F32 = mybir.dt.float32
BF16 = mybir.dt.bfloat16
I32 = mybir.dt.int32
U32 = mybir.dt.uint32
ALU = mybir.AluOpType
AF = mybir.ActivationFunctionType
AX = mybir.AxisListType

ENG = ['pe', 'act', 'dve', 'pool', 'sp']
BNAME = {'pe': 'tensor', 'act': 'scalar', 'dve': 'vector', 'pool': 'gpsimd', 'sp': 'sync'}

D = 2048
NIN = 12288
EPS = 1e-6
NT = 10
TOK = 1216
PASSES = [(0, 3, False), (3, 3, False), (6, 3, True)]
POOL_W = (2, 4, 8, 16)
SCL = 128.0 ** -0.5


class Prog:
    def __init__(self):
        self.q = {e: [] for e in ENG}
        self.nev = {e: 0 for e in ENG}
        self.dmacnt = {}
        self.buf = {}
        self.seen = {e: {} for e in ENG}

    def op(self, eng, fn, reads=(), writes=(), dma_sem=None):
        writes = list(writes) + [b for b in reads if b.startswith('ps') and b not in writes]
        need = {}
        for b in reads:
            st = self.buf.setdefault(b, [[], []])
            for (s, v) in st[0]:
                need[s] = max(need.get(s, 0), v)
        for b in writes:
            st = self.buf.setdefault(b, [[], []])
            for (s, v) in st[0] + st[1]:
                need[s] = max(need.get(s, 0), v)
        seen = self.seen[eng]
        waits = []
        for s, v in need.items():
            if seen.get(s, 0) < v:
                seen[s] = v
                waits.append((s, v))
        if dma_sem is None:
            self.nev[eng] += 1
            ev = ('e_' + eng, self.nev[eng])
            inc = ('e_' + eng, 1)
        else:
            self.dmacnt[dma_sem] = self.dmacnt.get(dma_sem, 0) + 16
            ev = ('d_' + dma_sem, self.dmacnt[dma_sem])
            inc = ('d_' + dma_sem, 16)
        for b in reads:
            self.buf[b][1].append(ev)
        for b in writes:
            self.buf[b][0] = [ev]
            self.buf[b][1] = []
        self.q[eng].append((waits, fn, inc))
        return ev

    def emit(self, nc):
        names = ['e_' + e for e in ENG] + ['d_' + k for k in self.dmacnt]
        with ExitStack() as st:
            sems = {n: st.enter_context(nc.semaphore(n)) for n in names}
            with nc.Block() as block:
                for e in ENG:
                    def body(engine, e=e):
                        for (waits, fn, inc) in self.q[e]:
                            for (s, v) in waits:
                                engine.wait_ge(sems[s], v)
                            ins = fn(engine)
                            ins.then_inc(sems[inc[0]], inc[1])
                        if e == 'sp':
                            for k, v in self.dmacnt.items():
                                engine.wait_ge(sems['d_' + k], v)
                            for e2 in ENG:
                                if e2 != 'sp' and self.nev[e2] > 0:
                                    engine.wait_ge(sems['e_' + e2], self.nev[e2])
                    getattr(block, BNAME[e])(body)


class _Stop(Exception):
    pass


def build_program(stop_after=None, dbg=False, peer_rows=16384):
    nc = bass.Bass("TRN2", target_bir_lowering=False)
    P = Prog()
    st = ExitStack()

    def din(name, shape, dt=F32):
        return nc.dram_tensor(name, list(shape), dt, kind="ExternalInput").ap()

    def dout(name, shape, dt=F32):
        return nc.dram_tensor(name, list(shape), dt, kind="ExternalOutput").ap()

    xin = din("xin", [TOK, D])
    flag = din("flag", [128, 1])
    cnt = din("cnt", [128, 4, 15])
    mem = din("mem", [256, D])
    ck = din("ck", [2, 16, 256, 512])
    cvv = din("cvv", [2, 16, 256, 512])
    spool = din("spool", [2, 16, 15, 512])
    ssc = din("ssc", [2, 16, 2, 512])
    scc = din("scc", [2, 16, 30, 512])
    g_mix = din("g_mix", [2, D]); w_in = din("w_in", [2, D, NIN]); b_gate = din("b_gate", [2, 4, D])
    pool_w = din("pool_w", [2, 4, 128, 128]); pool_scale = din("pool_scale", [2, 512])
    w_pool_out = din("w_pool_out", [2, 512, D]); sc_w = din("sc_w", [2, 3, 512]); w_sc_out = din("w_sc_out", [2, 512, D])
    cm_w = din("cm_w", [2, 31, 512]); cm_b = din("cm_b", [2, 512]); cm_ln_g = din("cm_ln_g", [2, 512]); cm_ln_b = din("cm_ln_b", [2, 512])
    w_cm_out = din("w_cm_out", [2, 512, D]); sg_ln_g = din("sg_ln_g", [2, 512]); sg_ln_b = din("sg_ln_b", [2, 512])
    sg_wT = din("sg_wT", [2, 128, 4, 128]); sgws = din("sgws", [2, 64, 4, 64]); sg_b = din("sg_b", [2, 4, 128])
    w_sg_out = din("w_sg_out", [2, 512, D]); w_o = din("w_o", [2, D, D]); g_x = din("g_x", [2, D]); g_mem = din("g_mem", [2, D])
    w_xq = din("w_xq", [2, D, 512]); w_xk = din("w_xk", [2, D, 512]); w_xv = din("w_xv", [2, D, 512]); w_xo = din("w_xo", [2, 512, D])
    g_peer = din("g_peer", [2, D]); w_pq = din("w_pq", [2, D, D]); keysT = din("keysT", [2, 128, 2048])
    peer_u = din("peer_u", [2, peer_rows, D]); peer_v = din("peer_v", [2, peer_rows, D]); g_final = din("g_final", [1, D])
    c_identf = din("c_identf", [128, 128]); c_tri = din("c_tri", [128, 128]); c_blk = din("c_blk", [64, 64])
    c_wwin = din("c_wwin", [128, 255]); c_iota = din("c_iota", [128, 16]); c_amask = din("c_amask", [128, 16, 64])

    y_o = dout("y", [TOK, D])
    mk_o = dout("mk", [2, 256, 512]); mv_o = dout("mv", [2, 256, 512])
    pool_p_o = dout("pool_p", [2, 15, 512]); sc_p_o = dout("sc_p", [2, 2, 512]); cm_p_o = dout("cm_p", [2, 30, 512])
    pool_s_o = dout("pool_s", [2, 16, 15, 512]); sc_s_o = dout("sc_s", [2, 16, 2, 512]); cm_s_o = dout("cm_s", [2, 16, 30, 512])
    cv_s_o = dout("cv_s", [2, 16, 4, 512])
    dbg_o = dout("dbg", [TOK, D]) if dbg else None

    def sb(name, shape, dt):
        return st.enter_context(nc.sbuf_tensor(name, list(shape), dt))

    X = sb("X", [128, NT, D], F32)
    WS = sb("WS", [128, 2, 16, 256], BF16)
    GB = sb("GB", [128, D], F32)
    IDF = sb("IDF", [128, 128], F32); IDB = sb("IDB", [128, 128], BF16)
    ONESF = sb("ONESF", [128, 128], F32); ONESB = sb("ONESB", [128, 128], BF16)
    WWIN = sb("WWIN", [128, 255], F32); IOTA = sb("IOTA", [128, 16], F32)
    TRI = sb("TRI", [128, 128], F32); BLK = sb("BLK", [64, 64], F32)
    AMASK = sb("AMASK", [128, 16, 64], BF16)
    FLAG = sb("FLAG", [128, 1], F32); CNT = sb("CNT", [128, 4, 15], F32)
    SS = sb("SS", [128, 8], F32)
    SCRN = 24576
    SCR = sb("SCR", [128, SCRN], F32)
    PSB = [st.enter_context(nc.psum_tensor(f"PSB{i}", [128, 2048], F32)) for i in range(2)]

    def bank(i):
        return PSB[i // 4][:, (i % 4) * 512:(i % 4 + 1) * 512]

    def _pat(shape):
        names = "abcdef"[:len(shape) - 1]
        return "p (" + " ".join(names) + ") -> p " + " ".join(names)

    def _kw(shape):
        names = "abcdef"[:len(shape) - 1]
        return {n: s_ for n, s_ in zip(names[1:], shape[2:])}

    class Carver:
        def __init__(self):
            self.off = 0

        def _shape(self, ap, shape):
            ap = ap[:shape[0]] if shape[0] < 128 else ap
            return ap.rearrange(_pat(shape), **_kw(shape)) if len(shape) > 2 else ap

        def f32(self, shape):
            n = int(np.prod(shape[1:]))
            ap = SCR[:, self.off:self.off + n]
            self.off += n
            assert self.off <= SCRN, self.off
            return self._shape(ap, shape)

        def bf16(self, shape):
            n = int(np.prod(shape[1:]))
            nf = (n + 1) // 2
            ap = SCR[:, self.off:self.off + nf].bitcast(BF16)[:, 0:n]
            self.off += nf
            assert self.off <= SCRN, self.off
            return self._shape(ap, shape)

        def i32(self, shape):
            n = int(np.prod(shape[1:]))
            ap = SCR[:, self.off:self.off + n].bitcast(I32)
            self.off += n
            assert self.off <= SCRN, self.off
            return self._shape(ap, shape)

    cur_pass = [0]
    banks_avail = [list(range(8))]
    _bk = [0]

    def nb():
        lst = banks_avail[0]
        _bk[0] = (_bk[0] + 1) % len(lst)
        return lst[_bk[0]]

    def bk(i):
        return f'ps{i}'

    slow_mode = [False]

    def dma(eng, out, in_, reads, writes, sem):
        if slow_mode[0]:
            P.op(eng, lambda e: e.dma_start(out=out, in_=in_, allow_slow_non_contiguous=True), reads=reads, writes=writes, dma_sem=sem)
        else:
            P.op(eng, lambda e: e.dma_start(out=out, in_=in_), reads=reads, writes=writes, dma_sem=sem)

    class slow_dma:
        def __enter__(self):
            slow_mode[0] = True
        def __exit__(self, *a):
            slow_mode[0] = False

    def wq(s):
        return [f'w{s}q{j}' for j in range(4)]

    wn = [0]

    def wload(parts):
        s = wn[0] % 2
        wn[0] += 1
        for (k0, nk, src) in parts:
            keys = [f'w{s}q{j}' for j in range(k0 // 4, (k0 + nk + 3) // 4)]
            sem = f'w{s}' if nk == 16 else f'w{s}q{k0 // 4}'
            dma('pool', WS[:, s, k0:k0 + nk, :], src.rearrange("(k p) c -> p k c", p=128), [], keys, sem)
        return s

    def run_steps(steps, hook=None):
        if not steps:
            return
        slots = {0: wload(steps[0][0])}
        for i, (parts, fn) in enumerate(steps):
            if i + 1 < len(steps):
                slots[i + 1] = wload(steps[i + 1][0])
            if hook is not None:
                hook(i)
            if stop_after is not None and stop_after[0] == 'step' and stop_after[1] == i and cur_pass[0] == int(os.environ.get('DBG_PASS', '0')):
                raise _Stop()
            fn(slots[i])

    def full(src):
        return [(0, 16, src)]

    def halves(src512, fn):
        return [(full(src512[:, h * 256:(h + 1) * 256]), (lambda s, h=h: fn(s, h))) for h in range(2)]

    def mm_fm(b, s, cg, rhs_ap, C, rkeys, nk=16, k0=0):
        def f(e):
            ins = None
            for k in range(nk):
                ins = e.matmul(bank(b)[:, :C], lhsT=WS[:, s, k0 + k, cg * 128:(cg + 1) * 128], rhs=rhs_ap[:, k, :C],
                               start=(k == 0), stop=(k == nk - 1))
            return ins
        keys = [f'w{s}q{j}' for j in range(k0 // 4, (k0 + nk + 3) // 4)]
        P.op('pe', f, reads=keys + rkeys, writes=[bk(b)])

    _alt = [0]

    def evac_eng():
        _alt[0] ^= 1
        return 'act' if _alt[0] else 'dve'

    def copy(eng, out, in_, reads, writes):
        if eng == 'act':
            P.op('act', lambda e: e.copy(out=out, in_=in_), reads=reads, writes=writes)
        else:
            P.op('dve', lambda e: e.tensor_copy(out=out, in_=in_), reads=reads, writes=writes)

    def tt(out, in0, in1, op, reads, writes, eng='dve'):
        P.op(eng, lambda e: e.tensor_tensor(out=out, in0=in0, in1=in1, op=op), reads=reads, writes=writes)

    def tsc(out, in0, s1, s2, op0, op1, reads, writes, eng='dve'):
        if s2 is None:
            P.op(eng, lambda e: e.tensor_scalar(out=out, in0=in0, scalar1=s1, scalar2=None, op0=op0), reads=reads, writes=writes)
        else:
            P.op(eng, lambda e: e.tensor_scalar(out=out, in0=in0, scalar1=s1, scalar2=s2, op0=op0, op1=op1), reads=reads, writes=writes)

    def stt(out, in0, scalar, in1, op0, op1, reads, writes, eng='dve', accum=None):
        if accum is None:
            P.op(eng, lambda e: e.scalar_tensor_tensor(out=out, in0=in0, scalar=scalar, in1=in1, op0=op0, op1=op1), reads=reads, writes=writes)
        else:
            P.op(eng, lambda e: e.scalar_tensor_tensor(out=out, in0=in0, scalar=scalar, in1=in1, op0=op0, op1=op1, accum_out=accum), reads=reads, writes=writes)

    def rsqrt(out, in_, scale, bias, reads, writes):
        P.op('act', lambda e: e.activation(out=out, in_=in_, func=AF.Sqrt, bias=bias, scale=scale), reads=reads, writes=writes)
        P.op('dve', lambda e: e.reciprocal(out=out, in_=out), reads=writes, writes=writes)

    def guard(rkeys, wkeys):
        mx = {}
        for k in list(rkeys) + list(wkeys):
            st_ = P.buf.setdefault(k, [[], []])
            for (sname, v) in st_[0] + st_[1]:
                mx[sname] = max(mx.get(sname, 0), v)
        evs = list(mx.items())
        for k in wkeys:
            st_ = P.buf.setdefault(k, [[], []])
            st_[0] = list(evs)
            st_[1] = []

    def barrier():
        allk = list(P.buf.keys())
        guard(allk, allk)

    dma('sp', IDF[:], c_identf, [], ['IDF'], 'c0')
    dma('sp', WWIN[:], c_wwin, [], ['WWIN'], 'c1')
    dma('sp', IOTA[:], c_iota, [], ['IOTA'], 'c2')
    dma('sp', TRI[:], c_tri, [], ['TRI'], 'c3')
    dma('sp', BLK[:], c_blk, [], ['BLK'], 'c4')
    dma('pool', AMASK[:], c_amask, [], ['AMASK'], 'c5')
    dma('sp', FLAG[:], flag, [], ['FLAG'], 'c6')
    dma('sp', CNT[:], cnt, [], ['CNT'], 'c7')
    copy('dve', IDB[:], IDF[:], ['IDF'], ['IDB'])
    P.op('dve', lambda e: e.memset(ONESF[:], 1.0 / 512.0), writes=['ONESF'])
    P.op('dve', lambda e: e.memset(ONESB[:], 1.0), writes=['ONESB'])
    for t in range(NT):
        np_ = 128 if t < 9 else 64
        dma('sp', X[:np_, t, :], xin[t * 128:t * 128 + np_, :], [], [f'X{t}'], f'X{t}')

    def load_gb(src_row):
        dma('sp', GB[:], src_row.to_broadcast([128, D]), [], ['GB'], 'GB')

    def norm_tile(xt_ap, xkeys, np_, hn_ap, hnkeys, hnT, hnTkey, col0):
        P.op('act', lambda e: e.activation(out=hn_ap[:np_, :], in_=xt_ap, func=AF.Square, accum_out=SS[:np_, 0:1]),
             reads=xkeys, writes=hnkeys + ['SS0'])
        rsqrt(SS[:np_, 2:3], SS[:np_, 0:1], 1.0 / D, EPS, ['SS0'], ['SS2'])
        stt(hn_ap[:np_, :], xt_ap, SS[:np_, 2:3], GB[:np_, :], ALU.mult, ALU.mult, xkeys + ['SS2', 'GB'], hnkeys)
        if hnT is None:
            return
        for half in range(2):
            b = nb()
            pb = bank(b).bitcast(BF16)

            def f(e, half=half, pb=pb):
                ins = None
                for kk in range(8):
                    k = half * 8 + kk
                    ins = e.transpose(out=pb[:, kk * 128:kk * 128 + np_], in_=hn_ap[:np_, k * 128:(k + 1) * 128], identity=IDB[:np_, :np_])
                return ins
            P.op('pe', f, reads=hnkeys + ['IDB'], writes=[bk(b)])
            src = pb.rearrange("p (k c) -> p k c", c=128)[:, :, :np_]
            copy(evac_eng(), hnT[:, half * 8:(half + 1) * 8, col0:col0 + np_], src, [bk(b)], [hnTkey])

    def pass_cols(p):
        t0, npt, has_s = PASSES[p]
        tiles = [(t0 + i, 128, i * 128) for i in range(npt)]
        if has_s:
            tiles.append((9, 64, npt * 128))
        C = npt * 128 + (64 if has_s else 0)
        return tiles, C, npt * 128

    def sview(ap2d):
        return ap2d.rearrange("p (n t) -> p n t", t=4)

    def mixer_phase(l):
        cv_ = Carver()
        hnT = cv_.bf16([128, 16, 448])
        PBR = [cv_.bf16([128, 4, 448]) for _ in range(4)]
        m0 = cv_.off
        MERGED = cv_.bf16([128, 16, 448])
        T1 = cv_.f32([128, 4, 480])
        T2o = cv_.off
        T2 = cv_.f32([128, 4, 480])
        T3o = cv_.off
        T3 = cv_.f32([128, 4, 512])
        EXS_E = cv_.f32([128, 4, 16, 6])
        HALO = cv_.f32([128, 4, 48])
        SNEW = cv_.f32([128, 4, 4, 16])
        STG = cv_.f32([128, 512])
        MEAN = cv_.f32([128, 448]); RSTD = cv_.f32([128, 448])
        BST = cv_.f32([128, 8])
        POOLW = cv_.bf16([128, 4, 128]); PSCALE = cv_.f32([128, 4])
        SCW = cv_.f32([128, 4, 3]); CMW = cv_.f32([128, 4, 31]); CMB = cv_.f32([128, 4])
        CMG = cv_.f32([128, 4]); CMBB = cv_.f32([128, 4])
        SGG = cv_.f32([128, 512]); SGB = cv_.f32([128, 512])
        SGWT = cv_.bf16([128, 4, 128]); SGWS = cv_.bf16([64, 4, 64])
        SGTMP = cv_.f32([128, 4, 128])
        SGBR = cv_.bf16([128, 4, 128]); SGBRS = cv_.bf16([128, 4, 64])
        BGATE = cv_.f32([128, 4, 16])
        AEXS = SCR[:, m0:m0 + 1216].rearrange("p (g n r) -> p g n r", g=4, n=16)
        CEXS = SCR[:, m0 + 1216:m0 + 1216 + 2176].rearrange("p (g n r) -> p g n r", g=4, n=16)
        HN1 = SCR[:, T2o:T2o + 1024].bitcast(BF16)
        PP = SCR[:, T3o:T3o + 896].bitcast(BF16).rearrange("p (g c) -> p g c", g=4)
        WO = [SCR[:, T3o + 1024 + i * 512:T3o + 1024 + (i + 1) * 512].bitcast(BF16).rearrange("p (k c) -> p k c", k=4) for i in range(2)]
        T1o = T2o - 1920
        VN = SCR[:, T1o:T1o + 512]
        VNB = SCR[:, T1o + 512:T1o + 512 + 1024].bitcast(BF16).rearrange("p (t c) -> p t c", t=4)
        T1K = ['T1h'] + [f'T1m{g}' for g in range(4)]
        T2K = [f'T2{g}' for g in range(4)] + [f'T2s{g}' for g in range(4)]
        T3K = [f'T3{g}' for g in range(4)]

        dma('pool', POOLW[:], pool_w[l].rearrange("g c d -> c g d"), [], ['POOLW'], 'lp0')
        with slow_dma():
            dma('sp', PSCALE[:], pool_scale[l].rearrange("(g c) -> c g", c=128), [], ['PSCALE'], 'lp1')
            for g in range(4):
                dma('sp', SCW[:, g, :], sc_w[l][:, g * 128:(g + 1) * 128].rearrange("k c -> c k"), [], ['SCW'], 'lp2')
                dma('sp', CMW[:, g, :], cm_w[l][:, g * 128:(g + 1) * 128].rearrange("k c -> c k"), [], ['CMW'], 'lp3')
            dma('sp', CMB[:], cm_b[l].rearrange("(g c) -> c g", c=128), [], ['CMB'], 'lp4')
            dma('sp', CMG[:], cm_ln_g[l].rearrange("(g c) -> c g", c=128), [], ['CMG'], 'lp5')
            dma('sp', CMBB[:], cm_ln_b[l].rearrange("(g c) -> c g", c=128), [], ['CMBB'], 'lp6')
            for i in range(4):
                dma('sp', BGATE[:, i, :], b_gate[l][i].rearrange("(d c) -> c d", c=128), [], ['BGATE'], 'lp7')
        dma('sp', SGG[:], sg_ln_g[l:l + 1, :].to_broadcast([128, 512]), [], ['SGG'], 'lp8')
        dma('sp', SGB[:], sg_ln_b[l:l + 1, :].to_broadcast([128, 512]), [], ['SGB'], 'lp9')
        dma('sp', SGTMP[:], sg_wT[l], [], ['SGTMP'], 'lp10')
        tt(SGWT[:], SGTMP[:], TRI[:].unsqueeze(1).to_broadcast([128, 4, 128]), ALU.mult, ['SGTMP', 'TRI'], ['SGWT'])
        dma('sp', SGTMP[:64, :, 0:64], sgws[l], ['SGWT'], ['SGTMP'], 'lp10')
        tt(SGWS[:], SGTMP[:64, :, 0:64], BLK[:].unsqueeze(1).to_broadcast([64, 4, 64]), ALU.mult, ['SGTMP', 'BLK'], ['SGWS'])
        dma('sp', SGTMP[:], sg_b[l:l + 1, :, :].to_broadcast([128, 4, 128]), ['SGWS'], ['SGTMP'], 'lp10')
        tsc(SGBR[:], SGTMP[:], 1.0 / 128.0, None, ALU.mult, None, ['SGTMP'], ['SGBR'])
        copy('dve', SGBRS[:].rearrange("p g (n t) -> p g n t", t=4), SGBR[:, :, 0:4].unsqueeze(2).to_broadcast([128, 4, 16, 4]), ['SGBR'], ['SGBRS'])
        load_gb(g_mix[l:l + 1, :])
        tsc(X[:, 0, :], X[:, 0, :], FLAG[:, 0:1], None, ALU.mult, None, ['X0', 'FLAG'], ['X0'])
        P.op('dve', lambda e: e.memset(HALO[:], 0.0), writes=['HALO'])
        AH = HALO[:, :, 0:15]; EH = HALO[:, :, 15:17]; CH = HALO[:, :, 17:47]

        def state_out(src_fm, skeys, ncols, dsts):
            b = nb()

            def f(e):
                ins = None
                for g in range(4):
                    ins = e.transpose(out=bank(b)[:ncols, g * 128:(g + 1) * 128], in_=src_fm[:, g, :], identity=IDF[:, :])
                return ins
            P.op('pe', f, reads=skeys + ['IDF'], writes=[bk(b)])
            copy(evac_eng(), STG[:ncols, :], bank(b)[:ncols, :], [bk(b)], ['STG'])
            for (r0, nr, dst) in dsts:
                dma('sp', dst, STG[r0:r0 + nr, :], ['STG'], [], 'STGo')

        for p in range(3):
            cur_pass[0] = p
            tiles, C, Cp = pass_cols(p)
            has_s = PASSES[p][2]
            for (t, np_, c0) in tiles:
                norm_tile(X[:np_, t, :], [f'X{t}'], np_, HN1, T2K, hnT, 'hnT', c0)
            if has_s:
                def load_state(src, R, EXS, exkey):
                    rows = 16 * R
                    flat = src.rearrange("n r c -> (n r) c")
                    r0 = 0
                    while r0 < rows:
                        nseq = min(128, rows - r0) // R
                        nr = nseq * R
                        dma('sp', STG[:nr, :], flat[r0:r0 + nr, :], [], ['STG'], 'STG')
                        b = nb()

                        def f(e, nr=nr, b=b):
                            ins = None
                            for g in range(4):
                                ins = e.transpose(out=bank(b)[:, g * 128:g * 128 + nr], in_=STG[:nr, g * 128:(g + 1) * 128], identity=IDF[:nr, :nr])
                            return ins
                        P.op('pe', f, reads=['STG', 'IDF'], writes=[bk(b)])
                        n0 = r0 // R
                        src_v = bank(b).rearrange("p (g c) -> p g c", g=4)[:, :, :nr].rearrange("p g (n r) -> p g n r", r=R)
                        copy(evac_eng(), EXS[:, :, n0:n0 + nseq, 0:R], src_v, [bk(b)], [exkey])
                        r0 += nr
                load_state(spool[l], 15, AEXS, 'MERGED')
                load_state(ssc[l], 2, EXS_E, 'EXS_E')
                load_state(scc[l], 30, CEXS, 'MERGED')
                dma('sp', pool_s_o[l][:, 0:11, :], spool[l][:, 4:15, :], [], [], 'dd0')
                dma('sp', cm_s_o[l][:, 0:26, :], scc[l][:, 4:30, :], [], [], 'dd1')

            def snew_view(g):
                return SNEW[:, g, :, :].rearrange("p t n -> p n t")

            SNEWf = SNEW.rearrange("p g t n -> p g (t n)")
            steps = []
            base = w_in[l]

            def stepA(s, half):
                AEX = T1
                if half == 0:
                    copy('dve', AEX[:, :, 0:15], AH, ['HALO'], ['T1h'])
                for cg in range(2):
                    g = half * 2 + cg
                    b = nb()
                    mm_fm(b, s, cg, hnT, C, ['hnT'])
                    copy('act', AEX[:, g, 15:15 + Cp], bank(b)[:, :Cp], [bk(b)], [f'T1m{g}'])
                    if has_s:
                        copy('act', AEXS[:, g, :, 15:19], sview(bank(b)[:, Cp:Cp + 64]), [bk(b)], ['MERGED'])
                        copy('dve', snew_view(g), sview(bank(b)[:, Cp:Cp + 64]), [bk(b)], ['SNEW'])
                if half == 0:
                    return
                if has_s:
                    state_out(SNEWf, ['SNEW'], 64, [(tq * 16, 16, pool_s_o[l][:, 11 + tq, :]) for tq in range(4)])
                Sb = T2
                for g, w in enumerate(POOL_W):
                    rk = ['T1h', f'T1m{g}']
                    tt(Sb[:, g, :Cp], AEX[:, g, 15:15 + Cp], AEX[:, g, 14:14 + Cp], ALU.add, rk, [f'T2{g}'])
                    for j in range(2, w):
                        tt(Sb[:, g, :Cp], Sb[:, g, :Cp], AEX[:, g, 15 - j:15 - j + Cp], ALU.add, rk + [f'T2{g}'], [f'T2{g}'])
                    stt(PP[:, g, :Cp], Sb[:, g, :Cp], 1.0 / w, AEX[:, g, 15:15 + Cp], ALU.mult, ALU.subtract, rk + [f'T2{g}'], T3K)
                    if p == 0:
                        tt(Sb[:, g, 128:143], Sb[:, g, 128:143], CNT[:, g, :], ALU.mult, [f'T2{g}', 'CNT'] + T3K, [f'T2{g}'])
                        tt(PP[:, g, 128:143], Sb[:, g, 128:143], AEX[:, g, 143:158], ALU.subtract, rk + [f'T2{g}'], T3K)
                    if has_s:
                        Ss = sview(Sb[:, g, Cp:Cp + 64])
                        tt(Ss, AEXS[:, g, :, 15:19], AEXS[:, g, :, 14:18], ALU.add, ['MERGED'], [f'T2s{g}'])
                        for j in range(2, w):
                            tt(Ss, Ss, AEXS[:, g, :, 15 - j:19 - j], ALU.add, ['MERGED', f'T2s{g}'], [f'T2s{g}'])
                        stt(sview(PP[:, g, Cp:Cp + 64]), Ss, 1.0 / w, AEXS[:, g, :, 15:19], ALU.mult, ALU.subtract, ['MERGED', f'T2s{g}'], T3K)
                for g in range(4):
                    b = nb()
                    P.op('pe', lambda e, g=g, b=b: e.matmul(bank(b)[:, :C], lhsT=POOLW[:, g, :], rhs=PP[:, g, :C], start=True, stop=True),
                         reads=['POOLW'] + T3K, writes=[bk(b)])
                    P.op('act', lambda e, g=g, b=b: e.mul(out=PBR[0][:, g, :C], in_=bank(b)[:, :C], mul=PSCALE[:, g:g + 1]),
                         reads=[bk(b), 'PSCALE'], writes=[f'PA{g}'])
                copy('dve', AH, AEX[:, :, Cp:Cp + 15], T1K, ['HALO'])
                if p == 2:
                    state_out(AEX[:, :, Cp:Cp + 15], T1K, 15, [(0, 15, pool_p_o[l])])
            steps += halves(base[:, 0:512], stepA)

            def stepB_cg(s, half):
                for cg in range(2):
                    g = half * 2 + cg
                    b = nb()
                    mm_fm(b, s, cg, hnT, C, ['hnT'])
                    copy('act', T3[:, g, :C], bank(b)[:, :C], [bk(b)], [f'T3{g}'])
            steps += halves(base[:, 1024:1536], stepB_cg)

            def stepB_hb(s, half):
                EEX = T1
                if half == 0:
                    copy('dve', EEX[:, :, 0:2], EH, ['HALO'], ['T1h'])
                for cg in range(2):
                    g = half * 2 + cg
                    b = nb()
                    mm_fm(b, s, cg, hnT, C, ['hnT'])
                    tt(EEX[:, g, 2:2 + Cp], bank(b)[:, :Cp], T3[:, g, :Cp], ALU.mult, [bk(b), f'T3{g}'], [f'T1m{g}'])
                    if has_s:
                        tt(EXS_E[:, g, :, 2:6], sview(bank(b)[:, Cp:Cp + 64]), sview(T3[:, g, Cp:Cp + 64]), ALU.mult, [bk(b), f'T3{g}'], ['EXS_E'])
                        copy('dve', snew_view(g), EXS_E[:, g, :, 2:6], ['EXS_E'], ['SNEW'])
                if half == 0:
                    return
                if has_s:
                    state_out(SNEWf, ['SNEW'], 64, [(tq * 16, 16, sc_s_o[l][:, tq - 2, :]) for tq in (2, 3)])
                ACC = T2
                for g in range(4):
                    rk = ['T1h', f'T1m{g}', 'SCW']
                    tsc(ACC[:, g, :Cp], EEX[:, g, 0:Cp], SCW[:, g, 0:1], None, ALU.mult, None, rk, [f'T2{g}'])
                    for k in (1, 2):
                        stt(ACC[:, g, :Cp], EEX[:, g, k:k + Cp], SCW[:, g, k:k + 1], ACC[:, g, :Cp], ALU.mult, ALU.add, rk + [f'T2{g}'], [f'T2{g}'])
                    if has_s:
                        As = sview(ACC[:, g, Cp:Cp + 64])
                        tsc(As, EXS_E[:, g, :, 0:4], SCW[:, g, 0:1], None, ALU.mult, None, ['EXS_E', 'SCW'], [f'T2s{g}'])
                        for k in (1, 2):
                            stt(As, EXS_E[:, g, :, k:k + 4], SCW[:, g, k:k + 1], As, ALU.mult, ALU.add, ['EXS_E', 'SCW', f'T2s{g}'], [f'T2s{g}'])
                copy('dve', EH, EEX[:, :, Cp:Cp + 2], T1K, ['HALO'])
                if p == 2:
                    state_out(EEX[:, :, Cp:Cp + 2], T1K, 2, [(0, 2, sc_p_o[l])])
            steps += halves(base[:, 1536:2048], stepB_hb)

            def stepB_bg(s, half):
                for cg in range(2):
                    g = half * 2 + cg
                    b = nb()
                    mm_fm(b, s, cg, hnT, C, ['hnT'])
                    tt(PBR[1][:, g, :C], bank(b)[:, :C], T2[:, g, :C], ALU.mult, [bk(b), f'T2{g}', f'T2s{g}'], [f'PB{g}'])
            steps += halves(base[:, 512:1024], stepB_bg)

            def stepC_g2(s, half):
                for cg in range(2):
                    g = half * 2 + cg
                    b = nb()
                    mm_fm(b, s, cg, hnT, C, ['hnT'])
                    P.op('act', lambda e, g=g, b=b: e.activation(out=T3[:, g, :C], in_=bank(b)[:, :C], func=AF.Sigmoid),
                         reads=[bk(b)], writes=[f'T3{g}'])
            steps += halves(base[:, 2560:3072], stepC_g2)

            def stepC_g1(s, half):
                CEX = T1
                if half == 0:
                    copy('dve', CEX[:, :, 0:30], CH, ['HALO'], ['T1h'])
                for cg in range(2):
                    g = half * 2 + cg
                    b = nb()
                    mm_fm(b, s, cg, hnT, C, ['hnT'])
                    tt(CEX[:, g, 30:30 + Cp], bank(b)[:, :Cp], T3[:, g, :Cp], ALU.mult, [bk(b), f'T3{g}'], [f'T1m{g}'])
                    if has_s:
                        tt(CEXS[:, g, :, 30:34], sview(bank(b)[:, Cp:Cp + 64]), sview(T3[:, g, Cp:Cp + 64]), ALU.mult, [bk(b), f'T3{g}'], ['MERGED'])
                        copy('dve', snew_view(g), CEXS[:, g, :, 30:34], ['MERGED'], ['SNEW'])
                if half == 0:
                    return
                if has_s:
                    state_out(SNEWf, ['SNEW'], 64, [(tq * 16, 16, cm_s_o[l][:, 26 + tq, :]) for tq in range(4)])
                ACC = T2
                for g in range(4):
                    rk = ['T1h', f'T1m{g}', 'CMW', 'CMB']
                    tsc(ACC[:, g, :Cp], CEX[:, g, 0:Cp], CMW[:, g, 0:1], CMB[:, g:g + 1], ALU.mult, ALU.add, rk, [f'T2{g}'])
                    for k in range(1, 31):
                        stt(ACC[:, g, :Cp], CEX[:, g, k:k + Cp], CMW[:, g, k:k + 1], ACC[:, g, :Cp], ALU.mult, ALU.add, rk + [f'T2{g}'], [f'T2{g}'])
                    if has_s:
                        As = sview(ACC[:, g, Cp:Cp + 64])
                        tsc(As, CEXS[:, g, :, 0:4], CMW[:, g, 0:1], CMB[:, g:g + 1], ALU.mult, ALU.add, ['MERGED', 'CMW', 'CMB'], [f'T2s{g}'])
                        for k in range(1, 31):
                            stt(As, CEXS[:, g, :, k:k + 4], CMW[:, g, k:k + 1], As, ALU.mult, ALU.add, ['MERGED', 'CMW', f'T2s{g}'], [f'T2s{g}'])
                copy('dve', CH, CEX[:, :, Cp:Cp + 30], T1K, ['HALO'])
                if p == 2:
                    state_out(CEX[:, :, Cp:Cp + 30], T1K, 30, [(0, 30, cm_p_o[l])])
                SQ = T3
                for g in range(4):
                    P.op('act', lambda e, g=g: e.activation(out=SQ[:, g, :C], in_=ACC[:, g, :C], func=AF.Square),
                         reads=[f'T2{g}', f'T2s{g}'], writes=[f'T3{g}'])
                bm = nb(); bq = nb()

                def fm(e):
                    ins = None
                    for g in range(4):
                        ins = e.matmul(bank(bm)[:, :C], lhsT=ONESF[:], rhs=ACC[:, g, :C], start=(g == 0), stop=(g == 3))
                    return ins
                P.op('pe', fm, reads=['ONESF'] + T2K, writes=[bk(bm)])

                def fq(e):
                    ins = None
                    for g in range(4):
                        ins = e.matmul(bank(bq)[:, :C], lhsT=ONESF[:], rhs=SQ[:, g, :C], start=(g == 0), stop=(g == 3))
                    return ins
                P.op('pe', fq, reads=['ONESF'] + T3K, writes=[bk(bq)])
                copy('act', MEAN[:, :C], bank(bm)[:, :C], [bk(bm)], ['MEAN'])
                tt(RSTD[:, :C], MEAN[:, :C], MEAN[:, :C], ALU.mult, ['MEAN'], ['RSTD'])
                tt(RSTD[:, :C], bank(bq)[:, :C], RSTD[:, :C], ALU.subtract, [bk(bq), 'RSTD'], ['RSTD'])
                rsqrt(RSTD[:, :C], RSTD[:, :C], 1.0, EPS, ['RSTD'], ['RSTD'])
                for g in range(4):
                    tt(ACC[:, g, :C], ACC[:, g, :C], MEAN[:, :C], ALU.subtract, [f'T2{g}', f'T2s{g}', 'MEAN'], [f'T2{g}', f'T2s{g}'])
                    tt(ACC[:, g, :C], ACC[:, g, :C], RSTD[:, :C], ALU.mult, [f'T2{g}', f'T2s{g}', 'RSTD'], [f'T2{g}', f'T2s{g}'])
                    P.op('act', lambda e, g=g: e.activation(out=PBR[2][:, g, :C], in_=ACC[:, g, :C], func=AF.Silu,
                                                            bias=CMBB[:, g:g + 1], scale=CMG[:, g:g + 1]),
                         reads=[f'T2{g}', f'T2s{g}', 'CMG', 'CMBB'], writes=[f'PC{g}'])
            steps += halves(base[:, 2048:2560], stepC_g1)

            def stepD_u(s, half):
                for cg in range(2):
                    g = half * 2 + cg
                    b = nb()
                    mm_fm(b, s, cg, hnT, C, ['hnT'])
                    copy('act', T3[:, g, :C], bank(b)[:, :C], [bk(b)], [f'T3{g}'])
            steps += halves(base[:, 3072:3584], stepD_u)

            def stepD_v(s, half):
                if half == 0:
                    banks_avail[0] = [0, 1, 2, 3]
                for ti, (t, np_, c0) in enumerate(tiles):
                    b = 4 + ti

                    def f(e, b=b, np_=np_, c0=c0):
                        ins = None
                        for k in range(16):
                            ins = e.matmul(bank(b)[:np_, half * 256:(half + 1) * 256], lhsT=hnT[:, k, c0:c0 + np_], rhs=WS[:, s, k, :], start=(k == 0), stop=(k == 15))
                        return ins
                    P.op('pe', f, reads=wq(s) + ['hnT'], writes=[bk(b)])
                if half == 0:
                    return
                if os.environ.get('DBG_CUT') == '1':
                    raise _Stop()
                for ti, (t, np_, c0) in enumerate(tiles):
                    b = 4 + ti
                    P.op('dve', lambda e, b=b, np_=np_: e.tensor_reduce(out=SS[:np_, 4:5], in_=bank(b)[:np_, :], axis=AX.X, op=ALU.add), reads=[bk(b)], writes=['SS4'])
                    P.op('act', lambda e, b=b, np_=np_: e.activation(out=VN[:np_, :], in_=bank(b)[:np_, :], func=AF.Square, accum_out=SS[:np_, 5:6]),
                         reads=[bk(b)], writes=T1K + ['SS5'])
                    if os.environ.get('DBG_CUT') == '2':
                        raise _Stop()
                    tsc(SS[:np_, 4:5], SS[:np_, 4:5], 1.0 / 512.0, None, ALU.mult, None, ['SS4'], ['SS4'])
                    tt(SS[:np_, 7:8], SS[:np_, 4:5], SS[:np_, 4:5], ALU.mult, ['SS4'], ['SS7'])
                    stt(SS[:np_, 5:6], SS[:np_, 5:6], 1.0 / 512.0, SS[:np_, 7:8], ALU.mult, ALU.subtract, ['SS5', 'SS7'], ['SS5'])
                    rsqrt(SS[:np_, 6:7], SS[:np_, 5:6], 1.0, EPS, ['SS5'], ['SS6'])
                    if os.environ.get('DBG_CUT') == '3':
                        raise _Stop()
                    tsc(VN[:np_, :], bank(b)[:np_, :], SS[:np_, 4:5], SS[:np_, 6:7], ALU.subtract, ALU.mult, [bk(b), 'SS4', 'SS6'], T1K)
                    if os.environ.get('DBG_CUT') == '4':
                        raise _Stop()
                    tt(VN[:np_, :], VN[:np_, :], SGG[:np_, :], ALU.mult, T1K + ['SGG'], T1K)
                    tt(VN[:np_, :], VN[:np_, :], SGB[:np_, :], ALU.add, T1K + ['SGB'], T1K)
                    copy('act', VNB[:np_, ti, :], VN[:np_, :], T1K, T1K)
                    if np_ == 64:
                        dma('sp', cv_s_o[l].rearrange("n t c -> (n t) c"), VN[:64, :], T1K, [], 'VNo')
                    if os.environ.get('DBG_CUT') == '5':
                        raise _Stop()
                    for g in range(4):
                        b2 = nb()
                        if np_ == 128:
                            def f2(e, b2=b2, g=g, ti=ti):
                                e.matmul(bank(b2)[:, :128], lhsT=VNB[:, ti, g * 128:(g + 1) * 128], rhs=SGWT[:, g, :], start=True, stop=False)
                                return e.matmul(bank(b2)[:, :128], lhsT=ONESB[:, :], rhs=SGBR[:, g, :], start=False, stop=True)
                            P.op('pe', f2, reads=T1K + ['SGWT', 'SGBR', 'ONESB'], writes=[bk(b2)])
                        else:
                            def f2(e, b2=b2, g=g, ti=ti):
                                e.matmul(bank(b2)[:, :64], lhsT=VNB[:64, ti, g * 128:(g + 1) * 128], rhs=SGWS[:, g, :], start=True, stop=False)
                                return e.matmul(bank(b2)[:, :64], lhsT=ONESB[:, :], rhs=SGBRS[:, g, :], start=False, stop=True)
                            P.op('pe', f2, reads=T1K + ['SGWS', 'SGBRS', 'ONESB'], writes=[bk(b2)])
                        if os.environ.get('DBG_CUT') == '6':
                            raise _Stop()
                        tt(PBR[3][:, g, c0:c0 + np_], bank(b2)[:, :np_], T3[:, g, c0:c0 + np_], ALU.mult, [bk(b2), f'T3{g}'], [f'PD{g}'])
                banks_avail[0] = list(range(8))
            steps += halves(base[:, 3584:4096], stepD_v)
            n_branch_steps = len(steps)

            wouts = [w_pool_out, w_sc_out, w_cm_out, w_sg_out]
            pkeys = [[f'PA{g}' for g in range(4)], [f'PB{g}' for g in range(4)], [f'PC{g}' for g in range(4)], [f'PD{g}' for g in range(4)]]
            MACC = [T1[:, dc, :448] for dc in range(2)]
            SIG = [T2[:, j, :448] for j in range(2)]
            TMP = [T2[:, 2 + j, :448] for j in range(2)]
            MK = ['MACC0', 'MACC1', 'SIG0', 'SIG1', 'TMP0', 'TMP1', 'WO0', 'WO1']
            won = [0]
            for dg in range(8):
                for i in range(4):
                    def stepM(s, dg=dg, i=i):
                        wo = won[0] % 2
                        won[0] += 1
                        dma('pool', WO[wo][:], wouts[i][l][:, dg * 256:(dg + 1) * 256].rearrange("(k p) c -> p k c", p=128), [], [f'WO{wo}'], f'WO{wo}')
                        for dc in range(2):
                            d = dg * 2 + dc
                            bg_ = nb()
                            mm_fm(bg_, s, dc, hnT, C, ['hnT'])
                            by = nb()

                            def fy(e, by=by, wo=wo, dc=dc, i=i):
                                ins = None
                                for kc in range(4):
                                    ins = e.matmul(bank(by)[:, :C], lhsT=WO[wo][:, kc, dc * 128:(dc + 1) * 128], rhs=PBR[i][:, kc, :C],
                                                   start=(kc == 0), stop=(kc == 3))
                                return ins
                            P.op('pe', fy, reads=[f'WO{wo}'] + pkeys[i], writes=[bk(by)])
                            sg = SIG[dc]
                            sk = f'SIG{dc}'
                            P.op('act', lambda e, bg_=bg_, sg=sg, i=i, d=d: e.activation(out=sg[:, :C], in_=bank(bg_)[:, :C], func=AF.Sigmoid,
                                                                                      bias=BGATE[:, i, d:d + 1]),
                                 reads=[bk(bg_), 'BGATE'], writes=[sk])
                            if i == 0:
                                tt(MACC[dc][:, :C], sg[:, :C], bank(by)[:, :C], ALU.mult, [sk, bk(by)], [f'MACC{dc}'])
                            else:
                                tm = TMP[dc]
                                tk = f'TMP{dc}'
                                tt(tm[:, :C], sg[:, :C], bank(by)[:, :C], ALU.mult, [sk, bk(by)], [tk])
                                if i < 3:
                                    tt(MACC[dc][:, :C], MACC[dc][:, :C], tm[:, :C], ALU.add, [f'MACC{dc}', tk], [f'MACC{dc}'])
                                else:
                                    tt(MERGED[:, d, :C], MACC[dc][:, :C], tm[:, :C], ALU.add, [f'MACC{dc}', tk], ['MERGED'])
                    c0_ = 4096 + i * 2048 + dg * 256
                    steps.append((full(base[:, c0_:c0_ + 256]), stepM))

            for dg in range(8):
                def stepO(s, dg=dg):
                    for (t, np_, c0) in tiles:
                        b = nb()

                        def f(e, b=b, np_=np_, c0=c0):
                            ins = None
                            for k in range(16):
                                ins = e.matmul(bank(b)[:np_, 0:256], lhsT=MERGED[:, k, c0:c0 + np_], rhs=WS[:, s, k, :], start=(k == 0), stop=(k == 15))
                            return ins
                        P.op('pe', f, reads=wq(s) + ['MERGED'], writes=[bk(b)])
                        tt(X[:np_, t, dg * 256:(dg + 1) * 256], X[:np_, t, dg * 256:(dg + 1) * 256], bank(b)[:np_, 0:256], ALU.add,
                           [f'X{t}', bk(b)], [f'X{t}'])
                steps.append((full(w_o[l][:, dg * 256:(dg + 1) * 256]), stepO))

            def hook(i):
                if i == n_branch_steps:
                    guard(T1K + T2K + T3K + ['MERGED', 'EXS_E'], MK + ['MERGED'])
            run_steps(steps, hook)
            if os.environ.get('DBG_END') == str(p):
                raise _Stop()
            guard(MK, T1K + T2K + T3K)
            if os.environ.get('DBG_END2') == str(p):
                raise _Stop()

    def attn_phase(l):
        cv_ = Carver()
        hnT = cv_.bf16([128, 16, 448])
        HN1 = cv_.bf16([128, D])
        QTa = cv_.bf16([128, 4, 448])
        OT = cv_.bf16([128, 4, 448])
        PR = cv_.f32([128, 4, 256])
        PN = cv_.bf16([128, 4, 256])
        PT = cv_.bf16([128, 8, 128])
        PTn = [cv_.bf16([128, 8, 64]) for _ in range(2)]
        QBLK = cv_.bf16([128, 4, 16, 64])
        KS = [cv_.f32([128, 2, 512]) for _ in range(2)]
        KTS = [cv_.bf16([128, 4, 256]) for _ in range(2)]
        VS = [cv_.bf16([128, 2, 512]) for _ in range(2)]
        STG = cv_.f32([128, 256])
        KT = cv_.bf16([128, 4, 256]); VB = cv_.bf16([128, 2, 512])
        MEMX = cv_.f32([128, 2, D])
        load_gb(g_mem[l:l + 1, :])
        for mt in range(2):
            dma('sp', MEMX[:, mt, :], mem[mt * 128:(mt + 1) * 128, :], [], [f'MEMX{mt}'], f'MEMX{mt}')
            norm_tile(MEMX[:, mt, :], [f'MEMX{mt}'], 128, HN1, ['HN1'], hnT, 'hnT', mt * 128)

        def kv_k(s, half):
            for mt in range(2):
                b = nb()

                def f(e, b=b, mt=mt):
                    ins = None
                    for k in range(16):
                        ins = e.matmul(bank(b)[:, 0:256], lhsT=hnT[:, k, mt * 128:(mt + 1) * 128], rhs=WS[:, s, k, :], start=(k == 0), stop=(k == 15))
                    return ins
                P.op('pe', f, reads=wq(s) + ['hnT'], writes=[bk(b)])
                copy('act', STG[:, :], bank(b)[:, 0:256], [bk(b)], ['STG'])
                dma('sp', mk_o[l][mt * 128:(mt + 1) * 128, half * 256:(half + 1) * 256], STG[:, :], ['STG'], [], 'STGo')
            for cg in range(2):
                h = half * 2 + cg
                b = nb()
                mm_fm(b, s, cg, hnT, 256, ['hnT'])
                copy(evac_eng(), KT[:, h, :], bank(b)[:, :256], [bk(b)], ['KT'])

        def kv_v(s, half):
            for mt in range(2):
                b = nb()

                def f(e, b=b, mt=mt):
                    ins = None
                    for k in range(16):
                        ins = e.matmul(bank(b)[:, 0:256], lhsT=hnT[:, k, mt * 128:(mt + 1) * 128], rhs=WS[:, s, k, :], start=(k == 0), stop=(k == 15))
                    return ins
                P.op('pe', f, reads=wq(s) + ['hnT'], writes=[bk(b)])
                copy('act', STG[:, :], bank(b)[:, 0:256], [bk(b)], ['STG'])
                copy('dve', VB[:, mt, half * 256:(half + 1) * 256], bank(b)[:, 0:256], [bk(b)], ['VB'])
                dma('sp', mv_o[l][mt * 128:(mt + 1) * 128, half * 256:(half + 1) * 256], STG[:, :], ['STG'], [], 'STGo')
        run_steps(halves(w_xk[l], kv_k) + halves(w_xv[l], kv_v))

        load_gb(g_x[l:l + 1, :])
        tcount = [0]

        def softmax_and_pt(np_, bs, S4=None, skeys=None, tbank=None):
            if S4 is None:
                S4 = PSB[bs][:np_, 0:1024].rearrange("p (h m) -> p h m", h=4)
                skeys = [bk(bs * 4), bk(bs * 4 + 1)]
                tbank = bs * 4 + 2
            P.op('dve', lambda e: e.tensor_reduce(out=SS[:np_, 0:4], in_=S4, axis=AX.X, op=ALU.max), reads=skeys, writes=['SS0'])
            tsc(SS[:np_, 0:4], SS[:np_, 0:4], -SCL, None, ALU.mult, None, ['SS0'], ['SS0'])
            for h in range(4):
                P.op('act', lambda e, h=h: e.activation(out=PR[:np_, h, :], in_=S4[:, h, :], func=AF.Exp, bias=SS[:np_, h:h + 1], scale=SCL,
                                                        accum_out=SS[:np_, 4 + h:5 + h]),
                     reads=skeys + ['SS0'], writes=[f'PR{h}', f'SSa{h}'])
            sak = [f'SSa{h}' for h in range(4)]
            P.op('dve', lambda e: e.reciprocal(out=SS[:np_, 4:8], in_=SS[:np_, 4:8]), reads=sak, writes=sak)
            tt(PN[:np_, :, :], PR[:np_, :, :], SS[:np_, 4:8].unsqueeze(2).to_broadcast([np_, 4, 256]), ALU.mult,
               [f'PR{h}' for h in range(4)] + sak, ['PN'])
            b = tbank
            pb = bank(b).bitcast(BF16)

            def f(e):
                ins = None
                for h in range(4):
                    for mc in range(2):
                        j = h * 2 + mc
                        ins = e.transpose(out=pb[:, j * 128:j * 128 + np_], in_=PN[:np_, h, mc * 128:(mc + 1) * 128], identity=IDB[:np_, :np_])
                return ins
            P.op('pe', f, reads=['PN', 'IDB'], writes=[bk(b)])
            copy(evac_eng(), PT[:, :, :np_], pb.rearrange("p (j c) -> p j c", c=128)[:, :, :np_], [bk(b)], ['PT'])

        for p in range(3):
            tiles, C, Cp = pass_cols(p)
            for (t, np_, c0) in tiles:
                norm_tile(X[:np_, t, :], [f'X{t}'], np_, HN1, ['HN1'], hnT, 'hnT', c0)

            def step_q(s, half):
                for cg in range(2):
                    h = half * 2 + cg
                    b = nb()
                    mm_fm(b, s, cg, hnT, C, ['hnT'])
                    copy(evac_eng(), QTa[:, h, :C], bank(b)[:, :C], [bk(b)], ['QTa'])
                if half == 0:
                    return
                for (t, np_, c0) in tiles:
                    bs = tcount[0] % 2
                    tcount[0] += 1
                    if np_ == 128:
                        def f(e, bs=bs, c0=c0):
                            ins = None
                            for h in range(4):
                                ins = e.matmul(PSB[bs][:, h * 256:(h + 1) * 256], lhsT=QTa[:, h, c0:c0 + 128], rhs=KT[:, h, :], start=True, stop=True)
                            return ins
                        P.op('pe', f, reads=['QTa', 'KT'], writes=[bk(bs * 4), bk(bs * 4 + 1)])
                        softmax_and_pt(128, bs)
                        b = bs * 4 + 3

                        def f2(e, b=b):
                            ins = None
                            for h in range(4):
                                for mc in range(2):
                                    ins = e.matmul(bank(b)[:, h * 128:(h + 1) * 128], lhsT=VB[:, mc, h * 128:(h + 1) * 128], rhs=PT[:, h * 2 + mc, :],
                                                   start=(mc == 0), stop=(mc == 1))
                            return ins
                        P.op('pe', f2, reads=['VB', 'PT'], writes=[bk(b)])
                        copy(evac_eng(), OT[:, :, c0:c0 + 128], bank(b).rearrange("p (h c) -> p h c", h=4), [bk(b)], ['OT'])
                    else:
                        for h in range(4):
                            tt(QBLK[:, h, :, :], QTa[:, h, c0:c0 + 64].unsqueeze(1).to_broadcast([128, 16, 64]), AMASK[:, :, :], ALU.mult,
                               ['QTa', 'AMASK'], [f'QBLK{h}'])
                        obs = 1 - bs
                        for n in range(16):
                            kb = n % 2
                            dma('sp', KS[kb][:], ck[l, n].rearrange("(mc p) c -> p mc c", p=128), [], [f'KS{kb}'], f'KS{kb}')
                            for hf_ in range(2):
                                tb = obs * 4 + hf_

                                def ft(e, kb=kb, hf_=hf_, tb=tb):
                                    ins = None
                                    for hh in range(2):
                                        h = hf_ * 2 + hh
                                        for mc in range(2):
                                            ins = e.transpose(out=bank(tb)[:, (hh * 2 + mc) * 128:(hh * 2 + mc + 1) * 128],
                                                              in_=KS[kb][:, mc, h * 128:(h + 1) * 128], identity=IDF[:, :])
                                    return ins
                                P.op('pe', ft, reads=[f'KS{kb}', 'IDF'], writes=[bk(tb)])
                                copy(evac_eng(), KTS[kb][:, hf_ * 2:hf_ * 2 + 2, :], bank(tb).rearrange("p (h m) -> p h m", h=2), [bk(tb)], [f'KTS{kb}_{hf_}'])

                            def fs(e, n=n, kb=kb, bs=bs):
                                ins = None
                                for h in range(4):
                                    ins = e.matmul(PSB[bs][:64, h * 512:h * 512 + 256], lhsT=QBLK[:, h, n, :], rhs=KTS[kb][:, h, :],
                                                   start=(n == 0), stop=(n == 15))
                                return ins
                            P.op('pe', fs, reads=[f'QBLK{h}' for h in range(4)] + [f'KTS{kb}_0', f'KTS{kb}_1'], writes=[bk(bs * 4 + h) for h in range(4)])
                        softmax_and_pt(64, bs, S4=PSB[bs][:64, :].rearrange("p (h x) -> p h x", h=4)[:, :, 0:256],
                                       skeys=[bk(bs * 4 + h) for h in range(4)], tbank=obs * 4 + 3)
                        ob = obs * 4 + 2
                        for n in range(16):
                            vb_ = n % 2
                            dma('pool', VS[vb_][:], cvv[l, n].rearrange("(mc p) c -> p mc c", p=128), [], [f'VS{vb_}'], f'VS{vb_}')
                            tt(PTn[vb_][:, :, :], PT[:, :, :64], AMASK[:, n:n + 1, :].to_broadcast([128, 8, 64]), ALU.mult, ['PT', 'AMASK'], [f'PTn{vb_}'])

                            def fo(e, n=n, vb_=vb_, ob=ob):
                                ins = None
                                for h in range(4):
                                    for mc in range(2):
                                        ins = e.matmul(bank(obs * 4 + h)[:, 0:64], lhsT=VS[vb_][:, mc, h * 128:(h + 1) * 128], rhs=PTn[vb_][:, h * 2 + mc, :],
                                                       start=(n == 0 and mc == 0), stop=(n == 15 and mc == 1))
                                return ins
                            P.op('pe', fo, reads=[f'VS{vb_}', f'PTn{vb_}'], writes=[bk(obs * 4 + h) for h in range(4)])
                        copy(evac_eng(), OT[:, :, c0:c0 + 64], PSB[obs][:, :].rearrange("p (h x) -> p h x", h=4)[:, :, 0:64], [bk(obs * 4 + h) for h in range(4)], ['OT'])

            def step_o(s, half):
                for (t, np_, c0) in tiles:
                    for dq in range(4):
                        dg = half * 4 + dq
                        b = nb()

                        def f(e, b=b, np_=np_, c0=c0, dq=dq):
                            ins = None
                            for h in range(4):
                                ins = e.matmul(bank(b)[:np_, 0:256], lhsT=OT[:, h, c0:c0 + np_], rhs=WS[:, s, dq * 4 + h, :], start=(h == 0), stop=(h == 3))
                            return ins
                        P.op('pe', f, reads=wq(s) + ['OT'], writes=[bk(b)])
                        tt(X[:np_, t, dg * 256:(dg + 1) * 256], X[:np_, t, dg * 256:(dg + 1) * 256], bank(b)[:np_, 0:256], ALU.add,
                           [f'X{t}', bk(b)], [f'X{t}'])
            xo_steps = []
            for half in range(2):
                parts = [(dq * 4, 4, w_xo[l][:, (half * 4 + dq) * 256:(half * 4 + dq + 1) * 256]) for dq in range(4)]
                xo_steps.append((parts, (lambda s, half=half: step_o(s, half))))
            run_steps(halves(w_xq[l], step_q) + xo_steps)

    def peer_phase(l):
        cv_ = Carver()
        hnT = cv_.bf16([128, 16, 448])
        HN1 = cv_.bf16([128, D])
        QT = cv_.bf16([128, 16, 448])
        CAND = cv_.f32([128, 8, 256])
        S2 = cv_.f32([128, 256])
        SV = cv_.f32([128, 16, 16]); SIi = cv_.i32([128, 16, 16]); SIF = cv_.f32([128, 16, 16])
        CVv = cv_.f32([128, 8, 16]); CIi = cv_.i32([128, 8, 16])
        AI = cv_.i32([128, 8, 16]); BI = cv_.i32([128, 8, 16])
        AFl = cv_.f32([128, 8, 16]); BFl = cv_.f32([128, 8, 16])
        I1 = cv_.f32([128, 8, 16]); I2 = cv_.f32([128, 8, 16]); EIDF = cv_.f32([128, 128])
        GW = cv_.f32([128, 8, 16])
        EIDT = cv_.i32([128, 128]); GWT = cv_.f32([128, 128])
        ACTV = cv_.f32([128, 4]); CMAT = cv_.f32([128, 128])
        KEYST = cv_.bf16([128, 2048])
        UG = [cv_.f32([128, D]) for _ in range(2)]
        VG = [cv_.f32([128, D]) for _ in range(2)]
        S = SCR[:, 0:2048]
        SI = SIi.bitcast(U32); CI = CIi.bitcast(U32)
        MASK = CAND.rearrange("p h (a b) -> p h a b", a=16)

        dma('pool', KEYST[:], keysT[l], [], ['KEYST'], 'KEYST')
        load_gb(g_peer[l:l + 1, :])
        for p in range(3):
            tiles, C, Cp = pass_cols(p)
            for ti, (t, np_, c0) in enumerate(tiles):
                norm_tile(X[:np_, t, :], [f'X{t}'], np_, HN1, ['HN1'], hnT, 'hnT', c0)
            steps = []
            for sg_ in range(4):
                def step_pq(s, half, sg_=sg_):
                    for cg in range(2):
                        j = sg_ * 4 + half * 2 + cg
                        b = nb()
                        mm_fm(b, s, cg, hnT, C, ['hnT'])
                        copy(evac_eng(), QT[:, j, :C], bank(b)[:, :C], [bk(b)], ['QT'])
                steps += halves(w_pq[l][:, sg_ * 512:(sg_ + 1) * 512], step_pq)
            run_steps(steps)
            def peer_tile(ti, t, np_, c0):
                if t == 0 and l == 1:
                    return
                tk0 = 64 if (t == 0 and l == 0) else 0
                norm_tile(X[:np_, t, :], [f'X{t}'], np_, HN1, ['HN1'], None, None, 0)

                def fsc(e, np_=np_, c0=c0):
                    ins = None
                    for j in range(16):
                        ins = e.matmul(PSB[0][:np_, j * 128:(j + 1) * 128], lhsT=QT[:, j, c0:c0 + np_], rhs=KEYST[:, j * 128:(j + 1) * 128], start=True, stop=True)
                    return ins
                P.op('pe', fsc, reads=['QT', 'KEYST'], writes=[bk(0), bk(1), bk(2), bk(3)])
                for q_ in range(4):
                    copy(evac_eng(), S[:np_, q_ * 512:(q_ + 1) * 512], bank(q_)[:np_, :], [bk(q_)], ['hnT'])
                for j in range(16):
                    Sj = S[:np_, j * 128:(j + 1) * 128]
                    P.op('dve', lambda e, j=j, Sj=Sj: e.max(out=SV[:np_, j, 0:8], in_=Sj), reads=['hnT'], writes=['SV'])
                    P.op('dve', lambda e, j=j, Sj=Sj: e.max_index(out=SI[:np_, j, 0:8], in_max=SV[:np_, j, 0:8], in_values=Sj), reads=['hnT', 'SV'], writes=['SI'])
                    P.op('dve', lambda e, j=j, Sj=Sj: e.match_replace(out=S2[:np_, 0:128], in_to_replace=SV[:np_, j, 0:8], in_values=Sj, imm_value=-1e30),
                         reads=['hnT', 'SV'], writes=['S2'])
                    P.op('dve', lambda e, j=j: e.max(out=SV[:np_, j, 8:16], in_=S2[:np_, 0:128]), reads=['S2'], writes=['SV'])
                    P.op('dve', lambda e, j=j: e.max_index(out=SI[:np_, j, 8:16], in_max=SV[:np_, j, 8:16], in_values=S2[:np_, 0:128]), reads=['S2', 'SV'], writes=['SI'])
                SV4 = SV.rearrange("p (h q) k -> p h q k", q=2)
                SIF4 = SIF.rearrange("p (h q) k -> p h q k", q=2)
                copy('dve', SIF[:np_], SIi[:np_], ['SI'], ['SIF'])
                C4 = CAND.rearrange("p h (a b) -> p h a b", a=16)
                tt(C4[:np_], SV4[:np_, :, 0, :].unsqueeze(3).to_broadcast([np_, 8, 16, 16]), SV4[:np_, :, 1, :].unsqueeze(2).to_broadcast([np_, 8, 16, 16]),
                   ALU.add, ['SV'], ['CAND'])
                for h in range(8):
                    Ch = CAND[:np_, h, :]
                    P.op('dve', lambda e, h=h, Ch=Ch: e.max(out=CVv[:np_, h, 0:8], in_=Ch), reads=['CAND'], writes=['CV'])
                    P.op('dve', lambda e, h=h, Ch=Ch: e.max_index(out=CI[:np_, h, 0:8], in_max=CVv[:np_, h, 0:8], in_values=Ch), reads=['CAND', 'CV'], writes=['CI'])
                    P.op('dve', lambda e, h=h, Ch=Ch: e.match_replace(out=S2[:np_, :], in_to_replace=CVv[:np_, h, 0:8], in_values=Ch, imm_value=-1e30),
                         reads=['CAND', 'CV'], writes=['S2'])
                    P.op('dve', lambda e, h=h: e.max(out=CVv[:np_, h, 8:16], in_=S2[:np_, :]), reads=['S2'], writes=['CV'])
                    P.op('dve', lambda e, h=h: e.max_index(out=CI[:np_, h, 8:16], in_max=CVv[:np_, h, 8:16], in_values=S2[:np_, :]), reads=['S2', 'CV'], writes=['CI'])
                P.op('dve', lambda e: e.tensor_single_scalar(out=AI[:np_], in_=CIi[:np_], scalar=4, op=ALU.arith_shift_right), reads=['CI'], writes=['AI'])
                P.op('dve', lambda e: e.tensor_single_scalar(out=BI[:np_], in_=CIi[:np_], scalar=15, op=ALU.bitwise_and), reads=['CI'], writes=['BI'])
                copy('dve', AFl[:np_], AI[:np_], ['AI'], ['AFl'])
                copy('dve', BFl[:np_], BI[:np_], ['BI'], ['BFl'])
                iota4 = IOTA[:np_, :].unsqueeze(1).unsqueeze(1).to_broadcast([np_, 8, 16, 16])
                for (Fl, fk, q, Io, ik) in ((AFl, 'AFl', 0, I1, 'I1'), (BFl, 'BFl', 1, I2, 'I2')):
                    tt(MASK[:np_], Fl[:np_].unsqueeze(3).to_broadcast([np_, 8, 16, 16]), iota4, ALU.is_equal, [fk, 'IOTA', 'CV', 'CI'], ['CAND'])
                    tt(MASK[:np_], MASK[:np_], SIF4[:np_, :, q, :].unsqueeze(2).to_broadcast([np_, 8, 16, 16]), ALU.mult, ['CAND', 'SIF'], ['CAND'])
                    P.op('dve', lambda e, Io=Io: e.tensor_reduce(out=Io[:np_], in_=MASK[:np_], axis=AX.X, op=ALU.add), reads=['CAND'], writes=[ik])
                stt(EIDF[:np_, :], I1[:np_].rearrange("p h k -> p (h k)"), 128.0, I2[:np_].rearrange("p h k -> p (h k)"), ALU.mult, ALU.add,
                    ['I1', 'I2'], ['EIDF'])
                tt(GW[:np_], CVv[:np_], CVv[:np_, :, 0:1].to_broadcast([np_, 8, 16]), ALU.subtract, ['CV'], ['GW'])
                P.op('act', lambda e: e.activation(out=GW[:np_], in_=GW[:np_], func=AF.Exp), reads=['GW'], writes=['GW'])
                P.op('dve', lambda e: e.tensor_reduce(out=SS[:np_, 0:8], in_=GW[:np_], axis=AX.X, op=ALU.add), reads=['GW'], writes=['SS0'])
                P.op('dve', lambda e: e.reciprocal(out=SS[:np_, 0:8], in_=SS[:np_, 0:8]), reads=['SS0'], writes=['SS0'])
                tt(GW[:np_], GW[:np_], SS[:np_, 0:8].unsqueeze(2).to_broadcast([np_, 8, 16]), ALU.mult, ['GW', 'SS0'], ['GW'])
                P.op('pe', lambda e: e.transpose(out=bank(0)[:, :np_], in_=EIDF[:np_, :], identity=IDF[:np_, :np_]), reads=['EIDF', 'IDF'], writes=[bk(0)])
                copy('dve', EIDT[:, :np_], bank(0)[:, :np_], [bk(0)], ['EIDT'])
                P.op('pe', lambda e: e.transpose(out=bank(1)[:, :np_], in_=GW[:np_].rearrange("p h k -> p (h k)"), identity=IDF[:np_, :np_]), reads=['GW', 'IDF'], writes=[bk(1)])
                copy('act', GWT[:, :np_], bank(1)[:, :np_], [bk(1)], ['GWT'])
                okeys = [bk(4), bk(5), bk(6), bk(7)]
                akeys = [bk(0), bk(1), bk(2), bk(3)]
                def fb_op(tk):
                    def fb(e, tk=tk):
                        ins = None
                        for c in range(4):
                            ins = e.matmul(bank(c)[:, :], lhsT=IDB[:np_, tk:tk + 1].to_broadcast([np_, 128]), rhs=HN1[:np_, c * 512:(c + 1) * 512], start=True, stop=True)
                        return ins
                    P.op('pe', fb, reads=['IDB', 'HN1'], writes=akeys)

                def gather(tk, gb_):
                    P.op('pool', lambda e, tk=tk, gb_=gb_: e.indirect_dma_start(out=UG[gb_], out_offset=None, in_=peer_u.rearrange("l r d -> (l r) d"), element_offset=l * peer_rows * D,
                                                                               in_offset=bass.IndirectOffsetOnAxis(ap=EIDT[:, tk:tk + 1], axis=0)),
                         reads=['EIDT'], writes=[f'UG{gb_}'], dma_sem=f'UG{gb_}')
                    P.op('pool', lambda e, tk=tk, gb_=gb_: e.indirect_dma_start(out=VG[gb_], out_offset=None, in_=peer_v.rearrange("l r d -> (l r) d"), element_offset=l * peer_rows * D,
                                                                               in_offset=bass.IndirectOffsetOnAxis(ap=EIDT[:, tk:tk + 1], axis=0)),
                         reads=['EIDT'], writes=[f'VG{gb_}'], dma_sem=f'VG{gb_}')
                gather(tk0, 0)
                fb_op(tk0)
                for tk in range(tk0, np_):
                    gb_ = (tk - tk0) % 2
                    if tk + 1 < np_:
                        gather(tk + 1, 1 - gb_)
                    P.op('dve', lambda e: e.memset(ACTV[:, 0:1], 0.0), writes=['ACTV'])
                    stt(UG[gb_], UG[gb_], 1.0, PSB[0][:, :], ALU.mult, ALU.mult, [f'UG{gb_}', 'ACTV'] + akeys, [f'UG{gb_}', 'ACTV'], accum=ACTV[:, 0:1])
                    if tk + 1 < np_:
                        fb_op(tk + 1)
                    P.op('act', lambda e: e.activation(out=ACTV[:, 1:2], in_=ACTV[:, 0:1], func=AF.Gelu), reads=['ACTV'], writes=['GEL'])
                    tsc(CMAT[:, :], WWIN[:, 127 - tk:255 - tk], ACTV[:, 1:2], GWT[:, tk:tk + 1], ALU.mult, ALU.mult, ['WWIN', 'GEL', 'GWT'], ['CMAT'])

                    def fv(e, tk=tk, gb_=gb_):
                        ins = None
                        for c in range(4):
                            ins = e.matmul(bank(4 + c)[:np_, :], lhsT=CMAT[:, :np_], rhs=VG[gb_][:, c * 512:(c + 1) * 512], start=(tk == tk0), stop=(tk == np_ - 1))
                        return ins
                    P.op('pe', fv, reads=['CMAT', f'VG{gb_}'], writes=okeys)
                tt(X[:np_, t, :], X[:np_, t, :], PSB[1][:np_, :], ALU.add, [f'X{t}'] + okeys, [f'X{t}'])
            for ti, (t, np_, c0) in enumerate(tiles):
                peer_tile(ti, t, np_, c0)

    def final_phase():
        load_gb(g_final[0:1, :])
        YS = [SCR[:, i * D:(i + 1) * D] for i in range(2)]
        for t in range(NT):
            np_ = 128 if t < 9 else 64
            yb = t % 2
            P.op('act', lambda e, t=t, np_=np_, yb=yb: e.activation(out=YS[yb][:np_, :], in_=X[:np_, t, :], func=AF.Square, accum_out=SS[:np_, 0:1]),
                 reads=[f'X{t}'], writes=[f'YS{yb}', 'SS0'])
            rsqrt(SS[:np_, 2:3], SS[:np_, 0:1], 1.0 / D, EPS, ['SS0'], ['SS2'])
            stt(YS[yb][:np_, :], X[:np_, t, :], SS[:np_, 2:3], GB[:np_, :], ALU.mult, ALU.mult, [f'X{t}', 'SS2', 'GB'], [f'YS{yb}'])
            dma('sp', y_o[t * 128:t * 128 + np_, :], YS[yb][:np_, :], [f'YS{yb}'], [], f'YSo{yb}')

    def dump_x():
        for t in range(NT):
            np_ = 128 if t < 9 else 64
            dma('sp', dbg_o[t * 128:t * 128 + np_, :], X[:np_, t, :], [f'X{t}'], [], 'dbg')

    done = False
    for l in range(2):
        if stop_after == ('load', 0):
            done = True
            break
        try:
            mixer_phase(l)
        except _Stop:
            done = True
            break
        barrier()
        if stop_after == ('mixer', l):
            done = True
            break
        attn_phase(l)
        barrier()
        if stop_after == ('attn', l):
            done = True
            break
        peer_phase(l)
        barrier()
        if stop_after == ('peer', l):
            done = True
            break
    if dbg:
        dump_x()
    if not done:
        final_phase()
    P.emit(nc)
    return nc, st


_CACHE = {}


def _consts():
    identf = np.eye(128, dtype=np.float32)
    s = np.arange(128)
    tri = (s[:, None] <= s[None, :]).astype(np.float32)
    q = np.arange(64)
    blk = ((q[:, None] // 4 == q[None, :] // 4) & (q[:, None] % 4 <= q[None, :] % 4)).astype(np.float32)
    wwin = np.zeros((128, 255), np.float32)
    wwin[:, 127] = 1.0
    iota = np.tile(np.arange(16, dtype=np.float32)[None, :], (128, 1))
    am = (np.arange(64)[None, :] // 4 == np.arange(16)[:, None]).astype(np.float32)
    amask = np.tile(am[None], (128, 1, 1)).astype(np.float32)
    return dict(c_identf=identf, c_tri=tri, c_blk=blk, c_wwin=wwin, c_iota=iota, c_amask=amask)


def make_in_maps(inp, peer_rows=16384):
    f = lambda a: np.ascontiguousarray(a, dtype=np.float32)
    shared = {}
    for k in ['g_mix', 'w_in', 'b_gate', 'pool_w', 'pool_scale', 'w_pool_out', 'sc_w', 'w_sc_out', 'cm_w', 'cm_b', 'cm_ln_g', 'cm_ln_b',
              'w_cm_out', 'sg_ln_g', 'sg_ln_b', 'sg_b', 'w_sg_out', 'w_o', 'g_x', 'g_mem', 'w_xq', 'w_xk', 'w_xv', 'w_xo', 'g_peer', 'w_pq',
              'peer_u', 'peer_v']:
        shared[k] = f(inp[k]) if not k.startswith('peer_') else f(inp[k][:, :peer_rows])
    shared['g_final'] = f(inp['g_final']).reshape(1, D)
    sgw = f(inp['sg_w'])
    shared['sg_wT'] = np.ascontiguousarray(sgw.transpose(0, 3, 1, 2))
    small = sgw[:, :, 0:4, 0:4].transpose(0, 3, 1, 2)
    shared['sgws'] = np.ascontiguousarray(np.tile(small, (1, 16, 1, 16)))
    pk = f(inp['peer_keys'])
    shared['keysT'] = np.ascontiguousarray(pk.transpose(0, 4, 1, 2, 3).reshape(2, 128, 2048))
    shared.update(_consts())
    xp = f(inp['x_prompt']); xs = f(inp['x_sample'])
    maps = []
    for c in range(8):
        b, hf = c // 2, c % 2
        m = dict(shared)
        xin = np.zeros((TOK, D), np.float32)
        if hf == 1:
            xin[0:1152] = xp[b, 896:2048]
        else:
            xin[128:1152] = xp[b, 0:1024]
        xin[1152:1216] = xs[16 * c:16 * c + 16].reshape(64, D)
        m['xin'] = xin
        m['flag'] = np.full((128, 1), float(hf), np.float32)
        cnt = np.zeros((128, 4, 15), np.float32)
        for g, w in enumerate(POOL_W):
            for pos in range(15):
                cnt[:, g, pos] = (1.0 / min(w, pos + 1)) if hf == 0 else (1.0 / w)
        m['cnt'] = cnt
        m['mem'] = f(inp['mem_prompt'][b])
        m['ck'] = f(inp['cache_mem_k'][:, 16 * c:16 * c + 16]).reshape(2, 16, 256, 512)
        m['cvv'] = f(inp['cache_mem_v'][:, 16 * c:16 * c + 16]).reshape(2, 16, 256, 512)
        m['spool'] = f(inp['state_pool'][:, 16 * c:16 * c + 16])
        m['ssc'] = f(inp['state_sconv'][:, 16 * c:16 * c + 16])
        m['scc'] = f(inp['state_cconv'][:, 16 * c:16 * c + 16])
        maps.append(m)
    return maps


def assemble(res):
    y_p = np.zeros((4, 2048, D), np.float32); y_s = np.zeros((128, 4, D), np.float32)
    mk = np.zeros((2, 4, 256, 4, 128), np.float32); mv = np.zeros_like(mk)
    pool_p = np.zeros((2, 4, 15, 512), np.float32); sc_p = np.zeros((2, 4, 2, 512), np.float32); cm_p = np.zeros((2, 4, 30, 512), np.float32)
    pool_s = np.zeros((2, 128, 15, 512), np.float32); sc_s = np.zeros((2, 128, 2, 512), np.float32); cm_s = np.zeros((2, 128, 30, 512), np.float32)
    cv_s = np.zeros((2, 128, 4, 512), np.float32)
    for c in range(8):
        r = res[c]
        b, hf = c // 2, c % 2
        y = np.asarray(r['y'])
        y_p[b, hf * 1024:(hf + 1) * 1024] = y[128:1152]
        y_s[16 * c:16 * c + 16] = y[1152:1216].reshape(16, 4, D)
        if hf == 1:
            mk[:, b] = np.asarray(r['mk']).reshape(2, 256, 4, 128)
            mv[:, b] = np.asarray(r['mv']).reshape(2, 256, 4, 128)
            pool_p[:, b] = np.asarray(r['pool_p']); sc_p[:, b] = np.asarray(r['sc_p']); cm_p[:, b] = np.asarray(r['cm_p'])
        pool_s[:, 16 * c:16 * c + 16] = np.asarray(r['pool_s']); sc_s[:, 16 * c:16 * c + 16] = np.asarray(r['sc_s'])
        cm_s[:, 16 * c:16 * c + 16] = np.asarray(r['cm_s']); cv_s[:, 16 * c:16 * c + 16] = np.asarray(r['cv_s'])
    return (y_p, y_s, mk, mv, pool_p, sc_p, cm_p, pool_s, sc_s, cm_s, cv_s)


def kernel(**inputs):
    if 'nc' not in _CACHE:
        _CACHE['nc'] = build_program()
    nc, _st = _CACHE['nc']
    maps = make_in_maps(inputs)
    res = run_bass_kernel_spmd(nc, maps, core_ids=list(range(8)))
    return assemble(res.results)
```

```python
import os
import numpy as np
import concourse.bass as bass
import concourse.mybir as mybir
from concourse.bass_utils import run_bass_kernel_spmd
from contextlib import ExitStack

F32 = mybir.dt.float32
BF16 = mybir.dt.bfloat16
I32 = mybir.dt.int32
U32 = mybir.dt.uint32
ALU = mybir.AluOpType
AF = mybir.ActivationFunctionType
AX = mybir.AxisListType

ENG = ['pe', 'act', 'dve', 'pool', 'sp']
BNAME = {'pe': 'tensor', 'act': 'scalar', 'dve': 'vector', 'pool': 'gpsimd', 'sp': 'sync'}

D = 2048
NIN = 12288
EPS = 1e-6
NT = 10
TOK = 1216
PASSES = [(0, 3, False), (3, 3, False), (6, 3, True)]
POOL_W = (2, 4, 8, 16)
SCL = 128.0 ** -0.5


class Prog:
    def __init__(self):
        self.q = {e: [] for e in ENG}
        self.nev = {e: 0 for e in ENG}
        self.dmacnt = {}
        self.buf = {}
        self.seen = {e: {} for e in ENG}

    def op(self, eng, fn, reads=(), writes=(), dma_sem=None):
        writes = list(writes) + [b for b in reads if b.startswith('ps') and b not in writes]
        need = {}
        for b in reads:
            st = self.buf.setdefault(b, [[], []])
            for (s, v) in st[0]:
                need[s] = max(need.get(s, 0), v)
        for b in writes:
            st = self.buf.setdefault(b, [[], []])
            for (s, v) in st[0] + st[1]:
                need[s] = max(need.get(s, 0), v)
        seen = self.seen[eng]
        waits = []
        for s, v in need.items():
            if seen.get(s, 0) < v:
                seen[s] = v
                waits.append((s, v))
        if dma_sem is None:
            self.nev[eng] += 1
            ev = ('e_' + eng, self.nev[eng])
            inc = ('e_' + eng, 1)
        else:
            self.dmacnt[dma_sem] = self.dmacnt.get(dma_sem, 0) + 16
            ev = ('d_' + dma_sem, self.dmacnt[dma_sem])
            inc = ('d_' + dma_sem, 16)
        for b in reads:
            self.buf[b][1].append(ev)
        for b in writes:
            self.buf[b][0] = [ev]
            self.buf[b][1] = []
        self.q[eng].append((waits, fn, inc))
        return ev

    def emit(self, nc):
        names = ['e_' + e for e in ENG] + ['d_' + k for k in self.dmacnt]
        with ExitStack() as st:
            sems = {n: st.enter_context(nc.semaphore(n)) for n in names}
            with nc.Block() as block:
                for e in ENG:
                    def body(engine, e=e):
                        for (waits, fn, inc) in self.q[e]:
                            for (s, v) in waits:
                                engine.wait_ge(sems[s], v)
                            ins = fn(engine)
                            ins.then_inc(sems[inc[0]], inc[1])
                        if e == 'sp':
                            for k, v in self.dmacnt.items():
                                engine.wait_ge(sems['d_' + k], v)
                            for e2 in ENG:
                                if e2 != 'sp' and self.nev[e2] > 0:
                                    engine.wait_ge(sems['e_' + e2], self.nev[e2])
                    getattr(block, BNAME[e])(body)


class _Stop(Exception):
    pass


def build_program(stop_after=None, dbg=False, peer_rows=16384):
    nc = bass.Bass("TRN2", target_bir_lowering=False)
    P = Prog()
    st = ExitStack()

    def din(name, shape, dt=F32):
        return nc.dram_tensor(name, list(shape), dt, kind="ExternalInput").ap()

    def dout(name, shape, dt=F32):
        return nc.dram_tensor(name, list(shape), dt, kind="ExternalOutput").ap()

    xin = din("xin", [TOK, D])
    flag = din("flag", [128, 1])
    cnt = din("cnt", [128, 4, 15])
    mem = din("mem", [256, D])
    ck = din("ck", [2, 16, 256, 512])
    cvv = din("cvv", [2, 16, 256, 512])
    spool = din("spool", [2, 16, 15, 512])
    ssc = din("ssc", [2, 16, 2, 512])
    scc = din("scc", [2, 16, 30, 512])
    g_mix = din("g_mix", [2, D]); w_in = din("w_in", [2, D, NIN]); b_gate = din("b_gate", [2, 4, D])
    pool_w = din("pool_w", [2, 4, 128, 128]); pool_scale = din("pool_scale", [2, 512])
    w_pool_out = din("w_pool_out", [2, 512, D]); sc_w = din("sc_w", [2, 3, 512]); w_sc_out = din("w_sc_out", [2, 512, D])
    cm_w = din("cm_w", [2, 31, 512]); cm_b = din("cm_b", [2, 512]); cm_ln_g = din("cm_ln_g", [2, 512]); cm_ln_b = din("cm_ln_b", [2, 512])
    w_cm_out = din("w_cm_out", [2, 512, D]); sg_ln_g = din("sg_ln_g", [2, 512]); sg_ln_b = din("sg_ln_b", [2, 512])
    sg_wT = din("sg_wT", [2, 128, 4, 128]); sgws = din("sgws", [2, 64, 4, 64]); sg_b = din("sg_b", [2, 4, 128])
    w_sg_out = din("w_sg_out", [2, 512, D]); w_o = din("w_o", [2, D, D]); g_x = din("g_x", [2, D]); g_mem = din("g_mem", [2, D])
    w_xq = din("w_xq", [2, D, 512]); w_xk = din("w_xk", [2, D, 512]); w_xv = din("w_xv", [2, D, 512]); w_xo = din("w_xo", [2, 512, D])
    g_peer = din("g_peer", [2, D]); w_pq = din("w_pq", [2, D, D]); keysT = din("keysT", [2, 128, 2048])
    peer_u = din("peer_u", [2, peer_rows, D]); peer_v = din("peer_v", [2, peer_rows, D]); g_final = din("g_final", [1, D])
    c_identf = din("c_identf", [128, 128]); c_tri = din("c_tri", [128, 128]); c_blk = din("c_blk", [64, 64])
    c_wwin = din("c_wwin", [128, 255]); c_iota = din("c_iota", [128, 16]); c_amask = din("c_amask", [128, 16, 64])

    y_o = dout("y", [TOK, D])
    mk_o = dout("mk", [2, 256, 512]); mv_o = dout("mv", [2, 256, 512])
    pool_p_o = dout("pool_p", [2, 15, 512]); sc_p_o = dout("sc_p", [2, 2, 512]); cm_p_o = dout("cm_p", [2, 30, 512])
    pool_s_o = dout("pool_s", [2, 16, 15, 512]); sc_s_o = dout("sc_s", [2, 16, 2, 512]); cm_s_o = dout("cm_s", [2, 16, 30, 512])
    cv_s_o = dout("cv_s", [2, 16, 4, 512])
    dbg_o = dout("dbg", [TOK, D]) if dbg else None

    def sb(name, shape, dt):
        return st.enter_context(nc.sbuf_tensor(name, list(shape), dt))

    X = sb("X", [128, NT, D], F32)
    WS = sb("WS", [128, 2, 16, 256], BF16)
    GB = sb("GB", [128, D], F32)
    IDF = sb("IDF", [128, 128], F32); IDB = sb("IDB", [128, 128], BF16)
    ONESF = sb("ONESF", [128, 128], F32); ONESB = sb("ONESB", [128, 128], BF16)
    WWIN = sb("WWIN", [128, 255], F32); IOTA = sb("IOTA", [128, 16], F32)
    TRI = sb("TRI", [128, 128], F32); BLK = sb("BLK", [64, 64], F32)
    AMASK = sb("AMASK", [128, 16, 64], BF16)
    FLAG = sb("FLAG", [128, 1], F32); CNT = sb("CNT", [128, 4, 15], F32)
    SS = sb("SS", [128, 8], F32)
    SCRN = 24576
    SCR = sb("SCR", [128, SCRN], F32)
    PSB = [st.enter_context(nc.psum_tensor(f"PSB{i}", [128, 2048], F32)) for i in range(2)]

    def bank(i):
        return PSB[i // 4][:, (i % 4) * 512:(i % 4 + 1) * 512]

    def _pat(shape):
        names = "abcdef"[:len(shape) - 1]
        return "p (" + " ".join(names) + ") -> p " + " ".join(names)

    def _kw(shape):
        names = "abcdef"[:len(shape) - 1]
        return {n: s_ for n, s_ in zip(names[1:], shape[2:])}

    class Carver:
        def __init__(self):
            self.off = 0

        def _shape(self, ap, shape):
            ap = ap[:shape[0]] if shape[0] < 128 else ap
            return ap.rearrange(_pat(shape), **_kw(shape)) if len(shape) > 2 else ap

        def f32(self, shape):
            n = int(np.prod(shape[1:]))
            ap = SCR[:, self.off:self.off + n]
            self.off += n
            assert self.off <= SCRN, self.off
            return self._shape(ap, shape)

        def bf16(self, shape):
            n = int(np.prod(shape[1:]))
            nf = (n + 1) // 2
            ap = SCR[:, self.off:self.off + nf].bitcast(BF16)[:, 0:n]
            self.off += nf
            assert self.off <= SCRN, self.off
            return self._shape(ap, shape)

        def i32(self, shape):
            n = int(np.prod(shape[1:]))
            ap = SCR[:, self.off:self.off + n].bitcast(I32)
            self.off += n
            assert self.off <= SCRN, self.off
            return self._shape(ap, shape)

    cur_pass = [0]
    banks_avail = [list(range(8))]
    _bk = [0]

    def nb():
        lst = banks_avail[0]
        _bk[0] = (_bk[0] + 1) % len(lst)
        return lst[_bk[0]]

    def bk(i):
        return f'ps{i}'

    slow_mode = [False]

    def dma(eng, out, in_, reads, writes, sem):
        if slow_mode[0]:
            P.op(eng, lambda e: e.dma_start(out=out, in_=in_, allow_slow_non_contiguous=True), reads=reads, writes=writes, dma_sem=sem)
        else:
            P.op(eng, lambda e: e.dma_start(out=out, in_=in_), reads=reads, writes=writes, dma_sem=sem)

    class slow_dma:
        def __enter__(self):
            slow_mode[0] = True
        def __exit__(self, *a):
            slow_mode[0] = False

    def wq(s):
        return [f'w{s}q{j}' for j in range(4)]

    wn = [0]

    def wload(parts):
        s = wn[0] % 2
        wn[0] += 1
        for (k0, nk, src) in parts:
            keys = [f'w{s}q{j}' for j in range(k0 // 4, (k0 + nk + 3) // 4)]
            sem = f'w{s}' if nk == 16 else f'w{s}q{k0 // 4}'
            dma('pool', WS[:, s, k0:k0 + nk, :], src.rearrange("(k p) c -> p k c", p=128), [], keys, sem)
        return s

    def run_steps(steps, hook=None):
        if not steps:
            return
        slots = {0: wload(steps[0][0])}
        for i, (parts, fn) in enumerate(steps):
            if i + 1 < len(steps):
                slots[i + 1] = wload(steps[i + 1][0])
            if hook is not None:
                hook(i)
            if stop_after is not None and stop_after[0] == 'step' and stop_after[1] == i and cur_pass[0] == int(os.environ.get('DBG_PASS', '0')):
                raise _Stop()
            fn(slots[i])

    def full(src):
        return [(0, 16, src)]

    def halves(src512, fn):
        return [(full(src512[:, h * 256:(h + 1) * 256]), (lambda s, h=h: fn(s, h))) for h in range(2)]

    def mm_fm(b, s, cg, rhs_ap, C, rkeys, nk=16, k0=0):
        def f(e):
            ins = None
            for k in range(nk):
                ins = e.matmul(bank(b)[:, :C], lhsT=WS[:, s, k0 + k, cg * 128:(cg + 1) * 128], rhs=rhs_ap[:, k, :C],
                               start=(k == 0), stop=(k == nk - 1))
            return ins
        keys = [f'w{s}q{j}' for j in range(k0 // 4, (k0 + nk + 3) // 4)]
        P.op('pe', f, reads=keys + rkeys, writes=[bk(b)])

    _alt = [0]

    def evac_eng():
        _alt[0] ^= 1
        return 'act' if _alt[0] else 'dve'

    def copy(eng, out, in_, reads, writes):
        if eng == 'act':
            P.op('act', lambda e: e.copy(out=out, in_=in_), reads=reads, writes=writes)
        else:
            P.op('dve', lambda e: e.tensor_copy(out=out, in_=in_), reads=reads, writes=writes)

    def tt(out, in0, in1, op, reads, writes, eng='dve'):
        P.op(eng, lambda e: e.tensor_tensor(out=out, in0=in0, in1=in1, op=op), reads=reads, writes=writes)

    def tsc(out, in0, s1, s2, op0, op1, reads, writes, eng='dve'):
        if s2 is None:
            P.op(eng, lambda e: e.tensor_scalar(out=out, in0=in0, scalar1=s1, scalar2=None, op0=op0), reads=reads, writes=writes)
        else:
            P.op(eng, lambda e: e.tensor_scalar(out=out, in0=in0, scalar1=s1, scalar2=s2, op0=op0, op1=op1), reads=reads, writes=writes)

    def stt(out, in0, scalar, in1, op0, op1, reads, writes, eng='dve', accum=None):
        if accum is None:
            P.op(eng, lambda e: e.scalar_tensor_tensor(out=out, in0=in0, scalar=scalar, in1=in1, op0=op0, op1=op1), reads=reads, writes=writes)
        else:
            P.op(eng, lambda e: e.scalar_tensor_tensor(out=out, in0=in0, scalar=scalar, in1=in1, op0=op0, op1=op1, accum_out=accum), reads=reads, writes=writes)

    def rsqrt(out, in_, scale, bias, reads, writes):
        P.op('act', lambda e: e.activation(out=out, in_=in_, func=AF.Sqrt, bias=bias, scale=scale), reads=reads, writes=writes)
        P.op('dve', lambda e: e.reciprocal(out=out, in_=out), reads=writes, writes=writes)

    def guard(rkeys, wkeys):
        mx = {}
        for k in list(rkeys) + list(wkeys):
            st_ = P.buf.setdefault(k, [[], []])
            for (sname, v) in st_[0] + st_[1]:
                mx[sname] = max(mx.get(sname, 0), v)
        evs = list(mx.items())
        for k in wkeys:
            st_ = P.buf.setdefault(k, [[], []])
            st_[0] = list(evs)
            st_[1] = []

    def barrier():
        allk = list(P.buf.keys())
        guard(allk, allk)

    dma('sp', IDF[:], c_identf, [], ['IDF'], 'c0')
    dma('sp', WWIN[:], c_wwin, [], ['WWIN'], 'c1')
    dma('sp', IOTA[:], c_iota, [], ['IOTA'], 'c2')
    dma('sp', TRI[:], c_tri, [], ['TRI'], 'c3')
    dma('sp', BLK[:], c_blk, [], ['BLK'], 'c4')
    dma('pool', AMASK[:], c_amask, [], ['AMASK'], 'c5')
    dma('sp', FLAG[:], flag, [], ['FLAG'], 'c6')
    dma('sp', CNT[:], cnt, [], ['CNT'], 'c7')
    copy('dve', IDB[:], IDF[:], ['IDF'], ['IDB'])
    P.op('dve', lambda e: e.memset(ONESF[:], 1.0 / 512.0), writes=['ONESF'])
    P.op('dve', lambda e: e.memset(ONESB[:], 1.0), writes=['ONESB'])
    for t in range(NT):
        np_ = 128 if t < 9 else 64
        dma('sp', X[:np_, t, :], xin[t * 128:t * 128 + np_, :], [], [f'X{t}'], f'X{t}')

    def load_gb(src_row):
        dma('sp', GB[:], src_row.to_broadcast([128, D]), [], ['GB'], 'GB')

    def norm_tile(xt_ap, xkeys, np_, hn_ap, hnkeys, hnT, hnTkey, col0):
        P.op('act', lambda e: e.activation(out=hn_ap[:np_, :], in_=xt_ap, func=AF.Square, accum_out=SS[:np_, 0:1]),
             reads=xkeys, writes=hnkeys + ['SS0'])
        rsqrt(SS[:np_, 2:3], SS[:np_, 0:1], 1.0 / D, EPS, ['SS0'], ['SS2'])
        stt(hn_ap[:np_, :], xt_ap, SS[:np_, 2:3], GB[:np_, :], ALU.mult, ALU.mult, xkeys + ['SS2', 'GB'], hnkeys)
        if hnT is None:
            return
        for half in range(2):
            b = nb()
            pb = bank(b).bitcast(BF16)

            def f(e, half=half, pb=pb):
                ins = None
                for kk in range(8):
                    k = half * 8 + kk
                    ins = e.transpose(out=pb[:, kk * 128:kk * 128 + np_], in_=hn_ap[:np_, k * 128:(k + 1) * 128], identity=IDB[:np_, :np_])
                return ins
            P.op('pe', f, reads=hnkeys + ['IDB'], writes=[bk(b)])
            src = pb.rearrange("p (k c) -> p k c", c=128)[:, :, :np_]
            copy(evac_eng(), hnT[:, half * 8:(half + 1) * 8, col0:col0 + np_], src, [bk(b)], [hnTkey])

    def pass_cols(p):
        t0, npt, has_s = PASSES[p]
        tiles = [(t0 + i, 128, i * 128) for i in range(npt)]
        if has_s:
            tiles.append((9, 64, npt * 128))
        C = npt * 128 + (64 if has_s else 0)
        return tiles, C, npt * 128

    def sview(ap2d):
        return ap2d.rearrange("p (n t) -> p n t", t=4)

    def mixer_phase(l):
        cv_ = Carver()
        hnT = cv_.bf16([128, 16, 448])
        PBR = [cv_.bf16([128, 4, 448]) for _ in range(4)]
        m0 = cv_.off
        MERGED = cv_.bf16([128, 16, 448])
        T1 = cv_.f32([128, 4, 480])
        T2o = cv_.off
        T2 = cv_.f32([128, 4, 480])
        T3o = cv_.off
        T3 = cv_.f32([128, 4, 512])
        EXS_E = cv_.f32([128, 4, 16, 6])
        HALO = cv_.f32([128, 4, 48])
        SNEW = cv_.f32([128, 4, 4, 16])
        STG = cv_.f32([128, 512])
        MEAN = cv_.f32([128, 448]); RSTD = cv_.f32([128, 448])
        BST = cv_.f32([128, 8])
        POOLW = cv_.bf16([128, 4, 128]); PSCALE = cv_.f32([128, 4])
        SCW = cv_.f32([128, 4, 3]); CMW = cv_.f32([128, 4, 31]); CMB = cv_.f32([128, 4])
        CMG = cv_.f32([128, 4]); CMBB = cv_.f32([128, 4])
        SGG = cv_.f32([128, 512]); SGB = cv_.f32([128, 512])
        SGWT = cv_.bf16([128, 4, 128]); SGWS = cv_.bf16([64, 4, 64])
        SGTMP = cv_.f32([128, 4, 128])
        SGBR = cv_.bf16([128, 4, 128]); SGBRS = cv_.bf16([128, 4, 64])
        BGATE = cv_.f32([128, 4, 16])
        AEXS = SCR[:, m0:m0 + 1216].rearrange("p (g n r) -> p g n r", g=4, n=16)
        CEXS = SCR[:, m0 + 1216:m0 + 1216 + 2176].rearrange("p (g n r) -> p g n r", g=4, n=16)
        HN1 = SCR[:, T2o:T2o + 1024].bitcast(BF16)
        PP = SCR[:, T3o:T3o + 896].bitcast(BF16).rearrange("p (g c) -> p g c", g=4)
        WO = [SCR[:, T3o + 1024 + i * 512:T3o + 1024 + (i + 1) * 512].bitcast(BF16).rearrange("p (k c) -> p k c", k=4) for i in range(2)]
        T1o = T2o - 1920
        VN = SCR[:, T1o:T1o + 512]
        VNB = SCR[:, T1o + 512:T1o + 512 + 1024].bitcast(BF16).rearrange("p (t c) -> p t c", t=4)
        T1K = ['T1h'] + [f'T1m{g}' for g in range(4)]
        T2K = [f'T2{g}' for g in range(4)] + [f'T2s{g}' for g in range(4)]
        T3K = [f'T3{g}' for g in range(4)]

        dma('pool', POOLW[:], pool_w[l].rearrange("g c d -> c g d"), [], ['POOLW'], 'lp0')
        with slow_dma():
            dma('sp', PSCALE[:], pool_scale[l].rearrange("(g c) -> c g", c=128), [], ['PSCALE'], 'lp1')
            for g in range(4):
                dma('sp', SCW[:, g, :], sc_w[l][:, g * 128:(g + 1) * 128].rearrange("k c -> c k"), [], ['SCW'], 'lp2')
                dma('sp', CMW[:, g, :], cm_w[l][:, g * 128:(g + 1) * 128].rearrange("k c -> c k"), [], ['CMW'], 'lp3')
            dma('sp', CMB[:], cm_b[l].rearrange("(g c) -> c g", c=128), [], ['CMB'], 'lp4')
            dma('sp', CMG[:], cm_ln_g[l].rearrange("(g c) -> c g", c=128), [], ['CMG'], 'lp5')
            dma('sp', CMBB[:], cm_ln_b[l].rearrange("(g c) -> c g", c=128), [], ['CMBB'], 'lp6')
            for i in range(4):
                dma('sp', BGATE[:, i, :], b_gate[l][i].rearrange("(d c) -> c d", c=128), [], ['BGATE'], 'lp7')
        dma('sp', SGG[:], sg_ln_g[l:l + 1, :].to_broadcast([128, 512]), [], ['SGG'], 'lp8')
        dma('sp', SGB[:], sg_ln_b[l:l + 1, :].to_broadcast([128, 512]), [], ['SGB'], 'lp9')
        dma('sp', SGTMP[:], sg_wT[l], [], ['SGTMP'], 'lp10')
        tt(SGWT[:], SGTMP[:], TRI[:].unsqueeze(1).to_broadcast([128, 4, 128]), ALU.mult, ['SGTMP', 'TRI'], ['SGWT'])
        dma('sp', SGTMP[:64, :, 0:64], sgws[l], ['SGWT'], ['SGTMP'], 'lp10')
        tt(SGWS[:], SGTMP[:64, :, 0:64], BLK[:].unsqueeze(1).to_broadcast([64, 4, 64]), ALU.mult, ['SGTMP', 'BLK'], ['SGWS'])
        dma('sp', SGTMP[:], sg_b[l:l + 1, :, :].to_broadcast([128, 4, 128]), ['SGWS'], ['SGTMP'], 'lp10')
        tsc(SGBR[:], SGTMP[:], 1.0 / 128.0, None, ALU.mult, None, ['SGTMP'], ['SGBR'])
        copy('dve', SGBRS[:].rearrange("p g (n t) -> p g n t", t=4), SGBR[:, :, 0:4].unsqueeze(2).to_broadcast([128, 4, 16, 4]), ['SGBR'], ['SGBRS'])
        load_gb(g_mix[l:l + 1, :])
        tsc(X[:, 0, :], X[:, 0, :], FLAG[:, 0:1], None, ALU.mult, None, ['X0', 'FLAG'], ['X0'])
        P.op('dve', lambda e: e.memset(HALO[:], 0.0), writes=['HALO'])
        AH = HALO[:, :, 0:15]; EH = HALO[:, :, 15:17]; CH = HALO[:, :, 17:47]

        def state_out(src_fm, skeys, ncols, dsts):
            b = nb()

            def f(e):
                ins = None
                for g in range(4):
                    ins = e.transpose(out=bank(b)[:ncols, g * 128:(g + 1) * 128], in_=src_fm[:, g, :], identity=IDF[:, :])
                return ins
            P.op('pe', f, reads=skeys + ['IDF'], writes=[bk(b)])
            copy(evac_eng(), STG[:ncols, :], bank(b)[:ncols, :], [bk(b)], ['STG'])
            for (r0, nr, dst) in dsts:
                dma('sp', dst, STG[r0:r0 + nr, :], ['STG'], [], 'STGo')

        for p in range(3):
            cur_pass[0] = p
            tiles, C, Cp = pass_cols(p)
            has_s = PASSES[p][2]
            for (t, np_, c0) in tiles:
                norm_tile(X[:np_, t, :], [f'X{t}'], np_, HN1, T2K, hnT, 'hnT', c0)
            if has_s:
                def load_state(src, R, EXS, exkey):
                    rows = 16 * R
                    flat = src.rearrange("n r c -> (n r) c")
                    r0 = 0
                    while r0 < rows:
                        nseq = min(128, rows - r0) // R
                        nr = nseq * R
                        dma('sp', STG[:nr, :], flat[r0:r0 + nr, :], [], ['STG'], 'STG')
                        b = nb()

                        def f(e, nr=nr, b=b):
                            ins = None
                            for g in range(4):
                                ins = e.transpose(out=bank(b)[:, g * 128:g * 128 + nr], in_=STG[:nr, g * 128:(g + 1) * 128], identity=IDF[:nr, :nr])
                            return ins
                        P.op('pe', f, reads=['STG', 'IDF'], writes=[bk(b)])
                        n0 = r0 // R
                        src_v = bank(b).rearrange("p (g c) -> p g c", g=4)[:, :, :nr].rearrange("p g (n r) -> p g n r", r=R)
                        copy(evac_eng(), EXS[:, :, n0:n0 + nseq, 0:R], src_v, [bk(b)], [exkey])
                        r0 += nr
                load_state(spool[l], 15, AEXS, 'MERGED')
                load_state(ssc[l], 2, EXS_E, 'EXS_E')
                load_state(scc[l], 30, CEXS, 'MERGED')
                dma('sp', pool_s_o[l][:, 0:11, :], spool[l][:, 4:15, :], [], [], 'dd0')
                dma('sp', cm_s_o[l][:, 0:26, :], scc[l][:, 4:30, :], [], [], 'dd1')

            def snew_view(g):
                return SNEW[:, g, :, :].rearrange("p t n -> p n t")

            SNEWf = SNEW.rearrange("p g t n -> p g (t n)")
            steps = []
            base = w_in[l]

            def stepA(s, half):
                AEX = T1
                if half == 0:
                    copy('dve', AEX[:, :, 0:15], AH, ['HALO'], ['T1h'])
                for cg in range(2):
                    g = half * 2 + cg
                    b = nb()
                    mm_fm(b, s, cg, hnT, C, ['hnT'])
                    copy('act', AEX[:, g, 15:15 + Cp], bank(b)[:, :Cp], [bk(b)], [f'T1m{g}'])
                    if has_s:
                        copy('act', AEXS[:, g, :, 15:19], sview(bank(b)[:, Cp:Cp + 64]), [bk(b)], ['MERGED'])
                        copy('dve', snew_view(g), sview(bank(b)[:, Cp:Cp + 64]), [bk(b)], ['SNEW'])
                if half == 0:
                    return
                if has_s:
                    state_out(SNEWf, ['SNEW'], 64, [(tq * 16, 16, pool_s_o[l][:, 11 + tq, :]) for tq in range(4)])
                Sb = T2
                for g, w in enumerate(POOL_W):
                    rk = ['T1h', f'T1m{g}']
                    tt(Sb[:, g, :Cp], AEX[:, g, 15:15 + Cp], AEX[:, g, 14:14 + Cp], ALU.add, rk, [f'T2{g}'])
                    for j in range(2, w):
                        tt(Sb[:, g, :Cp], Sb[:, g, :Cp], AEX[:, g, 15 - j:15 - j + Cp], ALU.add, rk + [f'T2{g}'], [f'T2{g}'])
                    stt(PP[:, g, :Cp], Sb[:, g, :Cp], 1.0 / w, AEX[:, g, 15:15 + Cp], ALU.mult, ALU.subtract, rk + [f'T2{g}'], T3K)
                    if p == 0:
                        tt(Sb[:, g, 128:143], Sb[:, g, 128:143], CNT[:, g, :], ALU.mult, [f'T2{g}', 'CNT'] + T3K, [f'T2{g}'])
                        tt(PP[:, g, 128:143], Sb[:, g, 128:143], AEX[:, g, 143:158], ALU.subtract, rk + [f'T2{g}'], T3K)
                    if has_s:
                        Ss = sview(Sb[:, g, Cp:Cp + 64])
                        tt(Ss, AEXS[:, g, :, 15:19], AEXS[:, g, :, 14:18], ALU.add, ['MERGED'], [f'T2s{g}'])
                        for j in range(2, w):
                            tt(Ss, Ss, AEXS[:, g, :, 15 - j:19 - j], ALU.add, ['MERGED', f'T2s{g}'], [f'T2s{g}'])
                        stt(sview(PP[:, g, Cp:Cp + 64]), Ss, 1.0 / w, AEXS[:, g, :, 15:19], ALU.mult, ALU.subtract, ['MERGED', f'T2s{g}'], T3K)
                for g in range(4):
                    b = nb()
                    P.op('pe', lambda e, g=g, b=b: e.matmul(bank(b)[:, :C], lhsT=POOLW[:, g, :], rhs=PP[:, g, :C], start=True, stop=True),
                         reads=['POOLW'] + T3K, writes=[bk(b)])
                    P.op('act', lambda e, g=g, b=b: e.mul(out=PBR[0][:, g, :C], in_=bank(b)[:, :C], mul=PSCALE[:, g:g + 1]),
                         reads=[bk(b), 'PSCALE'], writes=[f'PA{g}'])
                copy('dve', AH, AEX[:, :, Cp:Cp + 15], T1K, ['HALO'])
                if p == 2:
                    state_out(AEX[:, :, Cp:Cp + 15], T1K, 15, [(0, 15, pool_p_o[l])])
            steps += halves(base[:, 0:512], stepA)

            def stepB_cg(s, half):
                for cg in range(2):
                    g = half * 2 + cg
                    b = nb()
                    mm_fm(b, s, cg, hnT, C, ['hnT'])
                    copy('act', T3[:, g, :C], bank(b)[:, :C], [bk(b)], [f'T3{g}'])
            steps += halves(base[:, 1024:1536], stepB_cg)

            def stepB_hb(s, half):
                EEX = T1
                if half == 0:
                    copy('dve', EEX[:, :, 0:2], EH, ['HALO'], ['T1h'])
                for cg in range(2):
                    g = half * 2 + cg
                    b = nb()
                    mm_fm(b, s, cg, hnT, C, ['hnT'])
                    tt(EEX[:, g, 2:2 + Cp], bank(b)[:, :Cp], T3[:, g, :Cp], ALU.mult, [bk(b), f'T3{g}'], [f'T1m{g}'])
                    if has_s:
                        tt(EXS_E[:, g, :, 2:6], sview(bank(b)[:, Cp:Cp + 64]), sview(T3[:, g, Cp:Cp + 64]), ALU.mult, [bk(b), f'T3{g}'], ['EXS_E'])
                        copy('dve', snew_view(g), EXS_E[:, g, :, 2:6], ['EXS_E'], ['SNEW'])
                if half == 0:
                    return
                if has_s:
                    state_out(SNEWf, ['SNEW'], 64, [(tq * 16, 16, sc_s_o[l][:, tq - 2, :]) for tq in (2, 3)])
                ACC = T2
                for g in range(4):
                    rk = ['T1h', f'T1m{g}', 'SCW']
                    tsc(ACC[:, g, :Cp], EEX[:, g, 0:Cp], SCW[:, g, 0:1], None, ALU.mult, None, rk, [f'T2{g}'])
                    for k in (1, 2):
                        stt(ACC[:, g, :Cp], EEX[:, g, k:k + Cp], SCW[:, g, k:k + 1], ACC[:, g, :Cp], ALU.mult, ALU.add, rk + [f'T2{g}'], [f'T2{g}'])
                    if has_s:
                        As = sview(ACC[:, g, Cp:Cp + 64])
                        tsc(As, EXS_E[:, g, :, 0:4], SCW[:, g, 0:1], None, ALU.mult, None, ['EXS_E', 'SCW'], [f'T2s{g}'])
                        for k in (1, 2):
                            stt(As, EXS_E[:, g, :, k:k + 4], SCW[:, g, k:k + 1], As, ALU.mult, ALU.add, ['EXS_E', 'SCW', f'T2s{g}'], [f'T2s{g}'])
                copy('dve', EH, EEX[:, :, Cp:Cp + 2], T1K, ['HALO'])
                if p == 2:
                    state_out(EEX[:, :, Cp:Cp + 2], T1K, 2, [(0, 2, sc_p_o[l])])
            steps += halves(base[:, 1536:2048], stepB_hb)

            def stepB_bg(s, half):
                for cg in range(2):
                    g = half * 2 + cg
                    b = nb()
                    mm_fm(b, s, cg, hnT, C, ['hnT'])
                    tt(PBR[1][:, g, :C], bank(b)[:, :C], T2[:, g, :C], ALU.mult, [bk(b), f'T2{g}', f'T2s{g}'], [f'PB{g}'])
            steps += halves(base[:, 512:1024], stepB_bg)

            def stepC_g2(s, half):
                for cg in range(2):
                    g = half * 2 + cg
                    b = nb()
                    mm_fm(b, s, cg, hnT, C, ['hnT'])
                    P.op('act', lambda e, g=g, b=b: e.activation(out=T3[:, g, :C], in_=bank(b)[:, :C], func=AF.Sigmoid),
                         reads=[bk(b)], writes=[f'T3{g}'])
            steps += halves(base[:, 2560:3072], stepC_g2)

            def stepC_g1(s, half):
                CEX = T1
                if half == 0:
                    copy('dve', CEX[:, :, 0:30], CH, ['HALO'], ['T1h'])
                for cg in range(2):
                    g = half * 2 + cg
                    b = nb()
                    mm_fm(b, s, cg, hnT, C, ['hnT'])
                    tt(CEX[:, g, 30:30 + Cp], bank(b)[:, :Cp], T3[:, g, :Cp], ALU.mult, [bk(b), f'T3{g}'], [f'T1m{g}'])
                    if has_s:
                        tt(CEXS[:, g, :, 30:34], sview(bank(b)[:, Cp:Cp + 64]), sview(T3[:, g, Cp:Cp + 64]), ALU.mult, [bk(b), f'T3{g}'], ['MERGED'])
                        copy('dve', snew_view(g), CEXS[:, g, :, 30:34], ['MERGED'], ['SNEW'])
                if half == 0:
                    return
                if has_s:
                    state_out(SNEWf, ['SNEW'], 64, [(tq * 16, 16, cm_s_o[l][:, 26 + tq, :]) for tq in range(4)])
                ACC = T2
                for g in range(4):
                    rk = ['T1h', f'T1m{g}', 'CMW', 'CMB']
                    tsc(ACC[:, g, :Cp], CEX[:, g, 0:Cp], CMW[:, g, 0:1], CMB[:, g:g + 1], ALU.mult, ALU.add, rk, [f'T2{g}'])
                    for k in range(1, 31):
                        stt(ACC[:, g, :Cp], CEX[:, g, k:k + Cp], CMW[:, g, k:k + 1], ACC[:, g, :Cp], ALU.mult, ALU.add, rk + [f'T2{g}'], [f'T2{g}'])
                    if has_s:
                        As = sview(ACC[:, g, Cp:Cp + 64])
                        tsc(As, CEXS[:, g, :, 0:4], CMW[:, g, 0:1], CMB[:, g:g + 1], ALU.mult, ALU.add, ['MERGED', 'CMW', 'CMB'], [f'T2s{g}'])
                        for k in range(1, 31):
                            stt(As, CEXS[:, g, :, k:k + 4], CMW[:, g, k:k + 1], As, ALU.mult, ALU.add, ['MERGED', 'CMW', f'T2s{g}'], [f'T2s{g}'])
                copy('dve', CH, CEX[:, :, Cp:Cp + 30], T1K, ['HALO'])
                if p == 2:
                    state_out(CEX[:, :, Cp:Cp + 30], T1K, 30, [(0, 30, cm_p_o[l])])
                SQ = T3
                for g in range(4):
                    P.op('act', lambda e, g=g: e.activation(out=SQ[:, g, :C], in_=ACC[:, g, :C], func=AF.Square),
                         reads=[f'T2{g}', f'T2s{g}'], writes=[f'T3{g}'])
                bm = nb(); bq = nb()

                def fm(e):
                    ins = None
                    for g in range(4):
                        ins = e.matmul(bank(bm)[:, :C], lhsT=ONESF[:], rhs=ACC[:, g, :C], start=(g == 0), stop=(g == 3))
                    return ins
                P.op('pe', fm, reads=['ONESF'] + T2K, writes=[bk(bm)])

                def fq(e):
                    ins = None
                    for g in range(4):
                        ins = e.matmul(bank(bq)[:, :C], lhsT=ONESF[:], rhs=SQ[:, g, :C], start=(g == 0), stop=(g == 3))
                    return ins
                P.op('pe', fq, reads=['ONESF'] + T3K, writes=[bk(bq)])
                copy('act', MEAN[:, :C], bank(bm)[:, :C], [bk(bm)], ['MEAN'])
                tt(RSTD[:, :C], MEAN[:, :C], MEAN[:, :C], ALU.mult, ['MEAN'], ['RSTD'])
                tt(RSTD[:, :C], bank(bq)[:, :C], RSTD[:, :C], ALU.subtract, [bk(bq), 'RSTD'], ['RSTD'])
                rsqrt(RSTD[:, :C], RSTD[:, :C], 1.0, EPS, ['RSTD'], ['RSTD'])
                for g in range(4):
                    tt(ACC[:, g, :C], ACC[:, g, :C], MEAN[:, :C], ALU.subtract, [f'T2{g}', f'T2s{g}', 'MEAN'], [f'T2{g}', f'T2s{g}'])
                    tt(ACC[:, g, :C], ACC[:, g, :C], RSTD[:, :C], ALU.mult, [f'T2{g}', f'T2s{g}', 'RSTD'], [f'T2{g}', f'T2s{g}'])
                    P.op('act', lambda e, g=g: e.activation(out=PBR[2][:, g, :C], in_=ACC[:, g, :C], func=AF.Silu,
                                                            bias=CMBB[:, g:g + 1], scale=CMG[:, g:g + 1]),
                         reads=[f'T2{g}', f'T2s{g}', 'CMG', 'CMBB'], writes=[f'PC{g}'])
            steps += halves(base[:, 2048:2560], stepC_g1)

            def stepD_u(s, half):
                for cg in range(2):
                    g = half * 2 + cg
                    b = nb()
                    mm_fm(b, s, cg, hnT, C, ['hnT'])
                    copy('act', T3[:, g, :C], bank(b)[:, :C], [bk(b)], [f'T3{g}'])
            steps += halves(base[:, 3072:3584], stepD_u)

            def stepD_v(s, half):
                if half == 0:
                    banks_avail[0] = [0, 1, 2, 3]
                for ti, (t, np_, c0) in enumerate(tiles):
                    b = 4 + ti

                    def f(e, b=b, np_=np_, c0=c0):
                        ins = None
                        for k in range(16):
                            ins = e.matmul(bank(b)[:np_, half * 256:(half + 1) * 256], lhsT=hnT[:, k, c0:c0 + np_], rhs=WS[:, s, k, :], start=(k == 0), stop=(k == 15))
                        return ins
                    P.op('pe', f, reads=wq(s) + ['hnT'], writes=[bk(b)])
                if half == 0:
                    return
                if os.environ.get('DBG_CUT') == '1':
                    raise _Stop()
                for ti, (t, np_, c0) in enumerate(tiles):
                    b = 4 + ti
                    P.op('dve', lambda e, b=b, np_=np_: e.tensor_reduce(out=SS[:np_, 4:5], in_=bank(b)[:np_, :], axis=AX.X, op=ALU.add), reads=[bk(b)], writes=['SS4'])
                    P.op('act', lambda e, b=b, np_=np_: e.activation(out=VN[:np_, :], in_=bank(b)[:np_, :], func=AF.Square, accum_out=SS[:np_, 5:6]),
                         reads=[bk(b)], writes=T1K + ['SS5'])
                    if os.environ.get('DBG_CUT') == '2':
                        raise _Stop()
                    tsc(SS[:np_, 4:5], SS[:np_, 4:5], 1.0 / 512.0, None, ALU.mult, None, ['SS4'], ['SS4'])
                    tt(SS[:np_, 7:8], SS[:np_, 4:5], SS[:np_, 4:5], ALU.mult, ['SS4'], ['SS7'])
                    stt(SS[:np_, 5:6], SS[:np_, 5:6], 1.0 / 512.0, SS[:np_, 7:8], ALU.mult, ALU.subtract, ['SS5', 'SS7'], ['SS5'])
                    rsqrt(SS[:np_, 6:7], SS[:np_, 5:6], 1.0, EPS, ['SS5'], ['SS6'])
                    if os.environ.get('DBG_CUT') == '3':
                        raise _Stop()
                    tsc(VN[:np_, :], bank(b)[:np_, :], SS[:np_, 4:5], SS[:np_, 6:7], ALU.subtract, ALU.mult, [bk(b), 'SS4', 'SS6'], T1K)
                    if os.environ.get('DBG_CUT') == '4':
                        raise _Stop()
                    tt(VN[:np_, :], VN[:np_, :], SGG[:np_, :], ALU.mult, T1K + ['SGG'], T1K)
                    tt(VN[:np_, :], VN[:np_, :], SGB[:np_, :], ALU.add, T1K + ['SGB'], T1K)
                    copy('act', VNB[:np_, ti, :], VN[:np_, :], T1K, T1K)
                    if np_ == 64:
                        dma('sp', cv_s_o[l].rearrange("n t c -> (n t) c"), VN[:64, :], T1K, [], 'VNo')
                    if os.environ.get('DBG_CUT') == '5':
                        raise _Stop()
                    for g in range(4):
                        b2 = nb()
                        if np_ == 128:
                            def f2(e, b2=b2, g=g, ti=ti):
                                e.matmul(bank(b2)[:, :128], lhsT=VNB[:, ti, g * 128:(g + 1) * 128], rhs=SGWT[:, g, :], start=True, stop=False)
                                return e.matmul(bank(b2)[:, :128], lhsT=ONESB[:, :], rhs=SGBR[:, g, :], start=False, stop=True)
                            P.op('pe', f2, reads=T1K + ['SGWT', 'SGBR', 'ONESB'], writes=[bk(b2)])
                        else:
                            def f2(e, b2=b2, g=g, ti=ti):
                                e.matmul(bank(b2)[:, :64], lhsT=VNB[:64, ti, g * 128:(g + 1) * 128], rhs=SGWS[:, g, :], start=True, stop=False)
                                return e.matmul(bank(b2)[:, :64], lhsT=ONESB[:, :], rhs=SGBRS[:, g, :], start=False, stop=True)
                            P.op('pe', f2, reads=T1K + ['SGWS', 'SGBRS', 'ONESB'], writes=[bk(b2)])
                        if os.environ.get('DBG_CUT') == '6':
                            raise _Stop()
                        tt(PBR[3][:, g, c0:c0 + np_], bank(b2)[:, :np_], T3[:, g, c0:c0 + np_], ALU.mult, [bk(b2), f'T3{g}'], [f'PD{g}'])
                banks_avail[0] = list(range(8))
            steps += halves(base[:, 3584:4096], stepD_v)
            n_branch_steps = len(steps)

            wouts = [w_pool_out, w_sc_out, w_cm_out, w_sg_out]
            pkeys = [[f'PA{g}' for g in range(4)], [f'PB{g}' for g in range(4)], [f'PC{g}' for g in range(4)], [f'PD{g}' for g in range(4)]]
            MACC = [T1[:, dc, :448] for dc in range(2)]
            SIG = [T2[:, j, :448] for j in range(2)]
            TMP = [T2[:, 2 + j, :448] for j in range(2)]
            MK = ['MACC0', 'MACC1', 'SIG0', 'SIG1', 'TMP0', 'TMP1', 'WO0', 'WO1']
            won = [0]
            for dg in range(8):
                for i in range(4):
                    def stepM(s, dg=dg, i=i):
                        wo = won[0] % 2
                        won[0] += 1
                        dma('pool', WO[wo][:], wouts[i][l][:, dg * 256:(dg + 1) * 256].rearrange("(k p) c -> p k c", p=128), [], [f'WO{wo}'], f'WO{wo}')
                        for dc in range(2):
                            d = dg * 2 + dc
                            bg_ = nb()
                            mm_fm(bg_, s, dc, hnT, C, ['hnT'])
                            by = nb()

                            def fy(e, by=by, wo=wo, dc=dc, i=i):
                                ins = None
                                for kc in range(4):
                                    ins = e.matmul(bank(by)[:, :C], lhsT=WO[wo][:, kc, dc * 128:(dc + 1) * 128], rhs=PBR[i][:, kc, :C],
                                                   start=(kc == 0), stop=(kc == 3))
                                return ins
                            P.op('pe', fy, reads=[f'WO{wo}'] + pkeys[i], writes=[bk(by)])
                            sg = SIG[dc]
                            sk = f'SIG{dc}'
                            P.op('act', lambda e, bg_=bg_, sg=sg, i=i, d=d: e.activation(out=sg[:, :C], in_=bank(bg_)[:, :C], func=AF.Sigmoid,
                                                                                      bias=BGATE[:, i, d:d + 1]),
                                 reads=[bk(bg_), 'BGATE'], writes=[sk])
                            if i == 0:
                                tt(MACC[dc][:, :C], sg[:, :C], bank(by)[:, :C], ALU.mult, [sk, bk(by)], [f'MACC{dc}'])
                            else:
                                tm = TMP[dc]
                                tk = f'TMP{dc}'
                                tt(tm[:, :C], sg[:, :C], bank(by)[:, :C], ALU.mult, [sk, bk(by)], [tk])
                                if i < 3:
                                    tt(MACC[dc][:, :C], MACC[dc][:, :C], tm[:, :C], ALU.add, [f'MACC{dc}', tk], [f'MACC{dc}'])
                                else:
                                    tt(MERGED[:, d, :C], MACC[dc][:, :C], tm[:, :C], ALU.add, [f'MACC{dc}', tk], ['MERGED'])
                    c0_ = 4096 + i * 2048 + dg * 256
                    steps.append((full(base[:, c0_:c0_ + 256]), stepM))

            for dg in range(8):
                def stepO(s, dg=dg):
                    for (t, np_, c0) in tiles:
                        b = nb()

                        def f(e, b=b, np_=np_, c0=c0):
                            ins = None
                            for k in range(16):
                                ins = e.matmul(bank(b)[:np_, 0:256], lhsT=MERGED[:, k, c0:c0 + np_], rhs=WS[:, s, k, :], start=(k == 0), stop=(k == 15))
                            return ins
                        P.op('pe', f, reads=wq(s) + ['MERGED'], writes=[bk(b)])
                        tt(X[:np_, t, dg * 256:(dg + 1) * 256], X[:np_, t, dg * 256:(dg + 1) * 256], bank(b)[:np_, 0:256], ALU.add,
                           [f'X{t}', bk(b)], [f'X{t}'])
                steps.append((full(w_o[l][:, dg * 256:(dg + 1) * 256]), stepO))

            def hook(i):
                if i == n_branch_steps:
                    guard(T1K + T2K + T3K + ['MERGED', 'EXS_E'], MK + ['MERGED'])
            run_steps(steps, hook)
            if os.environ.get('DBG_END') == str(p):
                raise _Stop()
            guard(MK, T1K + T2K + T3K)
            if os.environ.get('DBG_END2') == str(p):
                raise _Stop()

    def attn_phase(l):
        cv_ = Carver()
        hnT = cv_.bf16([128, 16, 448])
        HN1 = cv_.bf16([128, D])
        QTa = cv_.bf16([128, 4, 448])
        OT = cv_.bf16([128, 4, 448])
        PR = cv_.f32([128, 4, 256])
        PN = cv_.bf16([128, 4, 256])
        PT = cv_.bf16([128, 8, 128])
        PTn = [cv_.bf16([128, 8, 64]) for _ in range(2)]
        QBLK = cv_.bf16([128, 4, 16, 64])
        KS = [cv_.f32([128, 2, 512]) for _ in range(2)]
        KTS = [cv_.bf16([128, 4, 256]) for _ in range(2)]
        VS = [cv_.bf16([128, 2, 512]) for _ in range(2)]
        STG = cv_.f32([128, 256])
        KT = cv_.bf16([128, 4, 256]); VB = cv_.bf16([128, 2, 512])
        MEMX = cv_.f32([128, 2, D])
        load_gb(g_mem[l:l + 1, :])
        for mt in range(2):
            dma('sp', MEMX[:, mt, :], mem[mt * 128:(mt + 1) * 128, :], [], [f'MEMX{mt}'], f'MEMX{mt}')
            norm_tile(MEMX[:, mt, :], [f'MEMX{mt}'], 128, HN1, ['HN1'], hnT, 'hnT', mt * 128)

        def kv_k(s, half):
            for mt in range(2):
                b = nb()

                def f(e, b=b, mt=mt):
                    ins = None
                    for k in range(16):
                        ins = e.matmul(bank(b)[:, 0:256], lhsT=hnT[:, k, mt * 128:(mt + 1) * 128], rhs=WS[:, s, k, :], start=(k == 0), stop=(k == 15))
                    return ins
                P.op('pe', f, reads=wq(s) + ['hnT'], writes=[bk(b)])
                copy('act', STG[:, :], bank(b)[:, 0:256], [bk(b)], ['STG'])
                dma('sp', mk_o[l][mt * 128:(mt + 1) * 128, half * 256:(half + 1) * 256], STG[:, :], ['STG'], [], 'STGo')
            for cg in range(2):
                h = half * 2 + cg
                b = nb()
                mm_fm(b, s, cg, hnT, 256, ['hnT'])
                copy(evac_eng(), KT[:, h, :], bank(b)[:, :256], [bk(b)], ['KT'])

        def kv_v(s, half):
            for mt in range(2):
                b = nb()

                def f(e, b=b, mt=mt):
                    ins = None
                    for k in range(16):
                        ins = e.matmul(bank(b)[:, 0:256], lhsT=hnT[:, k, mt * 128:(mt + 1) * 128], rhs=WS[:, s, k, :], start=(k == 0), stop=(k == 15))
                    return ins
                P.op('pe', f, reads=wq(s) + ['hnT'], writes=[bk(b)])
                copy('act', STG[:, :], bank(b)[:, 0:256], [bk(b)], ['STG'])
                copy('dve', VB[:, mt, half * 256:(half + 1) * 256], bank(b)[:, 0:256], [bk(b)], ['VB'])
                dma('sp', mv_o[l][mt * 128:(mt + 1) * 128, half * 256:(half + 1) * 256], STG[:, :], ['STG'], [], 'STGo')
        run_steps(halves(w_xk[l], kv_k) + halves(w_xv[l], kv_v))

        load_gb(g_x[l:l + 1, :])
        tcount = [0]

        def softmax_and_pt(np_, bs, S4=None, skeys=None, tbank=None):
            if S4 is None:
                S4 = PSB[bs][:np_, 0:1024].rearrange("p (h m) -> p h m", h=4)
                skeys = [bk(bs * 4), bk(bs * 4 + 1)]
                tbank = bs * 4 + 2
            P.op('dve', lambda e: e.tensor_reduce(out=SS[:np_, 0:4], in_=S4, axis=AX.X, op=ALU.max), reads=skeys, writes=['SS0'])
            tsc(SS[:np_, 0:4], SS[:np_, 0:4], -SCL, None, ALU.mult, None, ['SS0'], ['SS0'])
            for h in range(4):
                P.op('act', lambda e, h=h: e.activation(out=PR[:np_, h, :], in_=S4[:, h, :], func=AF.Exp, bias=SS[:np_, h:h + 1], scale=SCL,
                                                        accum_out=SS[:np_, 4 + h:5 + h]),
                     reads=skeys + ['SS0'], writes=[f'PR{h}', f'SSa{h}'])
            sak = [f'SSa{h}' for h in range(4)]
            P.op('dve', lambda e: e.reciprocal(out=SS[:np_, 4:8], in_=SS[:np_, 4:8]), reads=sak, writes=sak)
            tt(PN[:np_, :, :], PR[:np_, :, :], SS[:np_, 4:8].unsqueeze(2).to_broadcast([np_, 4, 256]), ALU.mult,
               [f'PR{h}' for h in range(4)] + sak, ['PN'])
            b = tbank
            pb = bank(b).bitcast(BF16)

            def f(e):
                ins = None
                for h in range(4):
                    for mc in range(2):
                        j = h * 2 + mc
                        ins = e.transpose(out=pb[:, j * 128:j * 128 + np_], in_=PN[:np_, h, mc * 128:(mc + 1) * 128], identity=IDB[:np_, :np_])
                return ins
            P.op('pe', f, reads=['PN', 'IDB'], writes=[bk(b)])
            copy(evac_eng(), PT[:, :, :np_], pb.rearrange("p (j c) -> p j c", c=128)[:, :, :np_], [bk(b)], ['PT'])

        for p in range(3):
            tiles, C, Cp = pass_cols(p)
            for (t, np_, c0) in tiles:
                norm_tile(X[:np_, t, :], [f'X{t}'], np_, HN1, ['HN1'], hnT, 'hnT', c0)

            def step_q(s, half):
                for cg in range(2):
                    h = half * 2 + cg
                    b = nb()
                    mm_fm(b, s, cg, hnT, C, ['hnT'])
                    copy(evac_eng(), QTa[:, h, :C], bank(b)[:, :C], [bk(b)], ['QTa'])
                if half == 0:
                    return
                for (t, np_, c0) in tiles:
                    bs = tcount[0] % 2
                    tcount[0] += 1
                    if np_ == 128:
                        def f(e, bs=bs, c0=c0):
                            ins = None
                            for h in range(4):
                                ins = e.matmul(PSB[bs][:, h * 256:(h + 1) * 256], lhsT=QTa[:, h, c0:c0 + 128], rhs=KT[:, h, :], start=True, stop=True)
                            return ins
                        P.op('pe', f, reads=['QTa', 'KT'], writes=[bk(bs * 4), bk(bs * 4 + 1)])
                        softmax_and_pt(128, bs)
                        b = bs * 4 + 3

                        def f2(e, b=b):
                            ins = None
                            for h in range(4):
                                for mc in range(2):
                                    ins = e.matmul(bank(b)[:, h * 128:(h + 1) * 128], lhsT=VB[:, mc, h * 128:(h + 1) * 128], rhs=PT[:, h * 2 + mc, :],
                                                   start=(mc == 0), stop=(mc == 1))
                            return ins
                        P.op('pe', f2, reads=['VB', 'PT'], writes=[bk(b)])
                        copy(evac_eng(), OT[:, :, c0:c0 + 128], bank(b).rearrange("p (h c) -> p h c", h=4), [bk(b)], ['OT'])
                    else:
                        for h in range(4):
                            tt(QBLK[:, h, :, :], QTa[:, h, c0:c0 + 64].unsqueeze(1).to_broadcast([128, 16, 64]), AMASK[:, :, :], ALU.mult,
                               ['QTa', 'AMASK'], [f'QBLK{h}'])
                        obs = 1 - bs
                        for n in range(16):
                            kb = n % 2
                            dma('sp', KS[kb][:], ck[l, n].rearrange("(mc p) c -> p mc c", p=128), [], [f'KS{kb}'], f'KS{kb}')
                            for hf_ in range(2):
                                tb = obs * 4 + hf_

                                def ft(e, kb=kb, hf_=hf_, tb=tb):
                                    ins = None
                                    for hh in range(2):
                                        h = hf_ * 2 + hh
                                        for mc in range(2):
                                            ins = e.transpose(out=bank(tb)[:, (hh * 2 + mc) * 128:(hh * 2 + mc + 1) * 128],
                                                              in_=KS[kb][:, mc, h * 128:(h + 1) * 128], identity=IDF[:, :])
                                    return ins
                                P.op('pe', ft, reads=[f'KS{kb}', 'IDF'], writes=[bk(tb)])
                                copy(evac_eng(), KTS[kb][:, hf_ * 2:hf_ * 2 + 2, :], bank(tb).rearrange("p (h m) -> p h m", h=2), [bk(tb)], [f'KTS{kb}_{hf_}'])

                            def fs(e, n=n, kb=kb, bs=bs):
                                ins = None
                                for h in range(4):
                                    ins = e.matmul(PSB[bs][:64, h * 512:h * 512 + 256], lhsT=QBLK[:, h, n, :], rhs=KTS[kb][:, h, :],
                                                   start=(n == 0), stop=(n == 15))
                                return ins
                            P.op('pe', fs, reads=[f'QBLK{h}' for h in range(4)] + [f'KTS{kb}_0', f'KTS{kb}_1'], writes=[bk(bs * 4 + h) for h in range(4)])
                        softmax_and_pt(64, bs, S4=PSB[bs][:64, :].rearrange("p (h x) -> p h x", h=4)[:, :, 0:256],
                                       skeys=[bk(bs * 4 + h) for h in range(4)], tbank=obs * 4 + 3)
                        ob = obs * 4 + 2
                        for n in range(16):
                            vb_ = n % 2
                            dma('pool', VS[vb_][:], cvv[l, n].rearrange("(mc p) c -> p mc c", p=128), [], [f'VS{vb_}'], f'VS{vb_}')
                            tt(PTn[vb_][:, :, :], PT[:, :, :64], AMASK[:, n:n + 1, :].to_broadcast([128, 8, 64]), ALU.mult, ['PT', 'AMASK'], [f'PTn{vb_}'])

                            def fo(e, n=n, vb_=vb_, ob=ob):
                                ins = None
                                for h in range(4):
                                    for mc in range(2):
                                        ins = e.matmul(bank(obs * 4 + h)[:, 0:64], lhsT=VS[vb_][:, mc, h * 128:(h + 1) * 128], rhs=PTn[vb_][:, h * 2 + mc, :],
                                                       start=(n == 0 and mc == 0), stop=(n == 15 and mc == 1))
                                return ins
                            P.op('pe', fo, reads=[f'VS{vb_}', f'PTn{vb_}'], writes=[bk(obs * 4 + h) for h in range(4)])
                        copy(evac_eng(), OT[:, :, c0:c0 + 64], PSB[obs][:, :].rearrange("p (h x) -> p h x", h=4)[:, :, 0:64], [bk(obs * 4 + h) for h in range(4)], ['OT'])

            def step_o(s, half):
                for (t, np_, c0) in tiles:
                    for dq in range(4):
                        dg = half * 4 + dq
                        b = nb()

                        def f(e, b=b, np_=np_, c0=c0, dq=dq):
                            ins = None
                            for h in range(4):
                                ins = e.matmul(bank(b)[:np_, 0:256], lhsT=OT[:, h, c0:c0 + np_], rhs=WS[:, s, dq * 4 + h, :], start=(h == 0), stop=(h == 3))
                            return ins
                        P.op('pe', f, reads=wq(s) + ['OT'], writes=[bk(b)])
                        tt(X[:np_, t, dg * 256:(dg + 1) * 256], X[:np_, t, dg * 256:(dg + 1) * 256], bank(b)[:np_, 0:256], ALU.add,
                           [f'X{t}', bk(b)], [f'X{t}'])
            xo_steps = []
            for half in range(2):
                parts = [(dq * 4, 4, w_xo[l][:, (half * 4 + dq) * 256:(half * 4 + dq + 1) * 256]) for dq in range(4)]
                xo_steps.append((parts, (lambda s, half=half: step_o(s, half))))
            run_steps(halves(w_xq[l], step_q) + xo_steps)

    def peer_phase(l):
        cv_ = Carver()
        hnT = cv_.bf16([128, 16, 448])
        HN1 = cv_.bf16([128, D])
        QT = cv_.bf16([128, 16, 448])
        CAND = cv_.f32([128, 8, 256])
        S2 = cv_.f32([128, 256])
        SV = cv_.f32([128, 16, 16]); SIi = cv_.i32([128, 16, 16]); SIF = cv_.f32([128, 16, 16])
        CVv = cv_.f32([128, 8, 16]); CIi = cv_.i32([128, 8, 16])
        AI = cv_.i32([128, 8, 16]); BI = cv_.i32([128, 8, 16])
        AFl = cv_.f32([128, 8, 16]); BFl = cv_.f32([128, 8, 16])
        I1 = cv_.f32([128, 8, 16]); I2 = cv_.f32([128, 8, 16]); EIDF = cv_.f32([128, 128])
        GW = cv_.f32([128, 8, 16])
        EIDT = cv_.i32([128, 128]); GWT = cv_.f32([128, 128])
        ACTV = cv_.f32([128, 4]); CMAT = cv_.f32([128, 128])
        KEYST = cv_.bf16([128, 2048])
        UG = [cv_.f32([128, D]) for _ in range(2)]
        VG = [cv_.f32([128, D]) for _ in range(2)]
        S = SCR[:, 0:2048]
        SI = SIi.bitcast(U32); CI = CIi.bitcast(U32)
        UG = UG + [SCR[:, 0:2048]]
        VG = VG + [CAND.rearrange("p h c -> p (h c)")]
        UGK = ['UG0', 'UG1', 'hnT']
        VGK = ['VG0', 'VG1', 'CAND']
        MASK = CAND.rearrange("p h (a b) -> p h a b", a=16)

        dma('pool', KEYST[:], keysT[l], [], ['KEYST'], 'KEYST')
        load_gb(g_peer[l:l + 1, :])
        for p in range(3):
            tiles, C, Cp = pass_cols(p)
            for ti, (t, np_, c0) in enumerate(tiles):
                norm_tile(X[:np_, t, :], [f'X{t}'], np_, HN1, ['HN1'], hnT, 'hnT', c0)
            steps = []
            for sg_ in range(4):
                def step_pq(s, half, sg_=sg_):
                    for cg in range(2):
                        j = sg_ * 4 + half * 2 + cg
                        b = nb()
                        mm_fm(b, s, cg, hnT, C, ['hnT'])
                        copy(evac_eng(), QT[:, j, :C], bank(b)[:, :C], [bk(b)], ['QT'])
                steps += halves(w_pq[l][:, sg_ * 512:(sg_ + 1) * 512], step_pq)
            run_steps(steps)
            def peer_tile(ti, t, np_, c0):
                if t == 0 and l == 1:
                    return
                tk0 = 64 if (t == 0 and l == 0) else 0
                norm_tile(X[:np_, t, :], [f'X{t}'], np_, HN1, ['HN1'], None, None, 0)

                def fsc(e, np_=np_, c0=c0):
                    ins = None
                    for j in range(16):
                        ins = e.matmul(PSB[0][:np_, j * 128:(j + 1) * 128], lhsT=QT[:, j, c0:c0 + np_], rhs=KEYST[:, j * 128:(j + 1) * 128], start=True, stop=True)
                    return ins
                P.op('pe', fsc, reads=['QT', 'KEYST'], writes=[bk(0), bk(1), bk(2), bk(3)])
                for q_ in range(4):
                    copy(evac_eng(), S[:np_, q_ * 512:(q_ + 1) * 512], bank(q_)[:np_, :], [bk(q_)], ['hnT'])
                for j in range(16):
                    Sj = S[:np_, j * 128:(j + 1) * 128]
                    P.op('dve', lambda e, j=j, Sj=Sj: e.max(out=SV[:np_, j, 0:8], in_=Sj), reads=['hnT'], writes=['SV'])
                    P.op('dve', lambda e, j=j, Sj=Sj: e.max_index(out=SI[:np_, j, 0:8], in_max=SV[:np_, j, 0:8], in_values=Sj), reads=['hnT', 'SV'], writes=['SI'])
                    P.op('dve', lambda e, j=j, Sj=Sj: e.match_replace(out=S2[:np_, 0:128], in_to_replace=SV[:np_, j, 0:8], in_values=Sj, imm_value=-1e30),
                         reads=['hnT', 'SV'], writes=['S2'])
                    P.op('dve', lambda e, j=j: e.max(out=SV[:np_, j, 8:16], in_=S2[:np_, 0:128]), reads=['S2'], writes=['SV'])
                    P.op('dve', lambda e, j=j: e.max_index(out=SI[:np_, j, 8:16], in_max=SV[:np_, j, 8:16], in_values=S2[:np_, 0:128]), reads=['S2', 'SV'], writes=['SI'])
                SV4 = SV.rearrange("p (h q) k -> p h q k", q=2)
                SIF4 = SIF.rearrange("p (h q) k -> p h q k", q=2)
                copy('dve', SIF[:np_], SIi[:np_], ['SI'], ['SIF'])
                C4 = CAND.rearrange("p h (a b) -> p h a b", a=16)
                tt(C4[:np_], SV4[:np_, :, 0, :].unsqueeze(3).to_broadcast([np_, 8, 16, 16]), SV4[:np_, :, 1, :].unsqueeze(2).to_broadcast([np_, 8, 16, 16]),
                   ALU.add, ['SV'], ['CAND'])
                for h in range(8):
                    Ch = CAND[:np_, h, :]
                    P.op('dve', lambda e, h=h, Ch=Ch: e.max(out=CVv[:np_, h, 0:8], in_=Ch), reads=['CAND'], writes=['CV'])
                    P.op('dve', lambda e, h=h, Ch=Ch: e.max_index(out=CI[:np_, h, 0:8], in_max=CVv[:np_, h, 0:8], in_values=Ch), reads=['CAND', 'CV'], writes=['CI'])
                    P.op('dve', lambda e, h=h, Ch=Ch: e.match_replace(out=S2[:np_, :], in_to_replace=CVv[:np_, h, 0:8], in_values=Ch, imm_value=-1e30),
                         reads=['CAND', 'CV'], writes=['S2'])
                    P.op('dve', lambda e, h=h: e.max(out=CVv[:np_, h, 8:16], in_=S2[:np_, :]), reads=['S2'], writes=['CV'])
                    P.op('dve', lambda e, h=h: e.max_index(out=CI[:np_, h, 8:16], in_max=CVv[:np_, h, 8:16], in_values=S2[:np_, :]), reads=['S2', 'CV'], writes=['CI'])
                P.op('dve', lambda e: e.tensor_single_scalar(out=AI[:np_], in_=CIi[:np_], scalar=4, op=ALU.arith_shift_right), reads=['CI'], writes=['AI'])
                P.op('dve', lambda e: e.tensor_single_scalar(out=BI[:np_], in_=CIi[:np_], scalar=15, op=ALU.bitwise_and), reads=['CI'], writes=['BI'])
                copy('dve', AFl[:np_], AI[:np_], ['AI'], ['AFl'])
                copy('dve', BFl[:np_], BI[:np_], ['BI'], ['BFl'])
                iota4 = IOTA[:np_, :].unsqueeze(1).unsqueeze(1).to_broadcast([np_, 8, 16, 16])
                for (Fl, fk, q, Io, ik) in ((AFl, 'AFl', 0, I1, 'I1'), (BFl, 'BFl', 1, I2, 'I2')):
                    tt(MASK[:np_], Fl[:np_].unsqueeze(3).to_broadcast([np_, 8, 16, 16]), iota4, ALU.is_equal, [fk, 'IOTA', 'CV', 'CI'], ['CAND'])
                    tt(MASK[:np_], MASK[:np_], SIF4[:np_, :, q, :].unsqueeze(2).to_broadcast([np_, 8, 16, 16]), ALU.mult, ['CAND', 'SIF'], ['CAND'])
                    P.op('dve', lambda e, Io=Io: e.tensor_reduce(out=Io[:np_], in_=MASK[:np_], axis=AX.X, op=ALU.add), reads=['CAND'], writes=[ik])
                stt(EIDF[:np_, :], I1[:np_].rearrange("p h k -> p (h k)"), 128.0, I2[:np_].rearrange("p h k -> p (h k)"), ALU.mult, ALU.add,
                    ['I1', 'I2'], ['EIDF'])
                tt(GW[:np_], CVv[:np_], CVv[:np_, :, 0:1].to_broadcast([np_, 8, 16]), ALU.subtract, ['CV'], ['GW'])
                P.op('act', lambda e: e.activation(out=GW[:np_], in_=GW[:np_], func=AF.Exp), reads=['GW'], writes=['GW'])
                P.op('dve', lambda e: e.tensor_reduce(out=SS[:np_, 0:8], in_=GW[:np_], axis=AX.X, op=ALU.add), reads=['GW'], writes=['SS0'])
                P.op('dve', lambda e: e.reciprocal(out=SS[:np_, 0:8], in_=SS[:np_, 0:8]), reads=['SS0'], writes=['SS0'])
                tt(GW[:np_], GW[:np_], SS[:np_, 0:8].unsqueeze(2).to_broadcast([np_, 8, 16]), ALU.mult, ['GW', 'SS0'], ['GW'])
                P.op('pe', lambda e: e.transpose(out=bank(0)[:, :np_], in_=EIDF[:np_, :], identity=IDF[:np_, :np_]), reads=['EIDF', 'IDF'], writes=[bk(0)])
                copy('dve', EIDT[:, :np_], bank(0)[:, :np_], [bk(0)], ['EIDT'])
                P.op('pe', lambda e: e.transpose(out=bank(1)[:, :np_], in_=GW[:np_].rearrange("p h k -> p (h k)"), identity=IDF[:np_, :np_]), reads=['GW', 'IDF'], writes=[bk(1)])
                copy('act', GWT[:, :np_], bank(1)[:, :np_], [bk(1)], ['GWT'])
                okeys = [bk(4), bk(5), bk(6), bk(7)]
                akeys = [bk(0), bk(1), bk(2), bk(3)]
                def fb_op(tk):
                    def fb(e, tk=tk):
                        ins = None
                        for c in range(4):
                            ins = e.matmul(bank(c)[:, :], lhsT=IDB[:np_, tk:tk + 1].to_broadcast([np_, 128]), rhs=HN1[:np_, c * 512:(c + 1) * 512], start=True, stop=True)
                        return ins
                    P.op('pe', fb, reads=['IDB', 'HN1'], writes=akeys)

                def gather(tk, gb_):
                    P.op('pool', lambda e, tk=tk, gb_=gb_: e.indirect_dma_start(out=UG[gb_], out_offset=None, in_=peer_u.rearrange("l r d -> (l r) d"), element_offset=l * peer_rows * D,
                                                                               in_offset=bass.IndirectOffsetOnAxis(ap=EIDT[:, tk:tk + 1], axis=0)),
                         reads=['EIDT'], writes=[UGK[gb_]], dma_sem=f'UG{gb_}')
                    P.op('pool', lambda e, tk=tk, gb_=gb_: e.indirect_dma_start(out=VG[gb_], out_offset=None, in_=peer_v.rearrange("l r d -> (l r) d"), element_offset=l * peer_rows * D,
                                                                               in_offset=bass.IndirectOffsetOnAxis(ap=EIDT[:, tk:tk + 1], axis=0)),
                         reads=['EIDT'], writes=[VGK[gb_]], dma_sem=f'VG{gb_}')
                gather(tk0, 0)
                gather(tk0 + 1, 1)
                fb_op(tk0)
                for tk in range(tk0, np_):
                    gb_ = (tk - tk0) % 3
                    if tk + 2 < np_:
                        gather(tk + 2, (tk - tk0 + 2) % 3)
                    P.op('dve', lambda e: e.memset(ACTV[:, 0:1], 0.0), writes=['ACTV'])
                    stt(UG[gb_], UG[gb_], 1.0, PSB[0][:, :], ALU.mult, ALU.mult, [UGK[gb_], 'ACTV'] + akeys, [UGK[gb_], 'ACTV'], accum=ACTV[:, 0:1])
                    if tk + 1 < np_:
                        fb_op(tk + 1)
                    P.op('act', lambda e: e.activation(out=ACTV[:, 1:2], in_=ACTV[:, 0:1], func=AF.Gelu), reads=['ACTV'], writes=['GEL'])
                    tsc(CMAT[:, :], WWIN[:, 127 - tk:255 - tk], ACTV[:, 1:2], GWT[:, tk:tk + 1], ALU.mult, ALU.mult, ['WWIN', 'GEL', 'GWT'], ['CMAT'])

                    def fv(e, tk=tk, gb_=gb_):
                        ins = None
                        for c in range(4):
                            ins = e.matmul(bank(4 + c)[:np_, :], lhsT=CMAT[:, :np_], rhs=VG[gb_][:, c * 512:(c + 1) * 512], start=(tk == tk0), stop=(tk == np_ - 1))
                        return ins
                    P.op('pe', fv, reads=['CMAT', VGK[gb_]], writes=okeys)
                tt(X[:np_, t, :], X[:np_, t, :], PSB[1][:np_, :], ALU.add, [f'X{t}'] + okeys, [f'X{t}'])
            for ti, (t, np_, c0) in enumerate(tiles):
                peer_tile(ti, t, np_, c0)

    def final_phase():
        load_gb(g_final[0:1, :])
        YS = [SCR[:, i * D:(i + 1) * D] for i in range(2)]
        for t in range(NT):
            np_ = 128 if t < 9 else 64
            yb = t % 2
            P.op('act', lambda e, t=t, np_=np_, yb=yb: e.activation(out=YS[yb][:np_, :], in_=X[:np_, t, :], func=AF.Square, accum_out=SS[:np_, 0:1]),
                 reads=[f'X{t}'], writes=[f'YS{yb}', 'SS0'])
            rsqrt(SS[:np_, 2:3], SS[:np_, 0:1], 1.0 / D, EPS, ['SS0'], ['SS2'])
            stt(YS[yb][:np_, :], X[:np_, t, :], SS[:np_, 2:3], GB[:np_, :], ALU.mult, ALU.mult, [f'X{t}', 'SS2', 'GB'], [f'YS{yb}'])
            dma('sp', y_o[t * 128:t * 128 + np_, :], YS[yb][:np_, :], [f'YS{yb}'], [], f'YSo{yb}')

    def dump_x():
        for t in range(NT):
            np_ = 128 if t < 9 else 64
            dma('sp', dbg_o[t * 128:t * 128 + np_, :], X[:np_, t, :], [f'X{t}'], [], 'dbg')

    done = False
    for l in range(2):
        if stop_after == ('load', 0):
            done = True
            break
        try:
            mixer_phase(l)
        except _Stop:
            done = True
            break
        barrier()
        if stop_after == ('mixer', l):
            done = True
            break
        attn_phase(l)
        barrier()
        if stop_after == ('attn', l):
            done = True
            break
        peer_phase(l)
        barrier()
        if stop_after == ('peer', l):
            done = True
            break
    if dbg:
        dump_x()
    if not done:
        final_phase()
    P.emit(nc)
    return nc, st


_CACHE = {}


def _consts():
    identf = np.eye(128, dtype=np.float32)
    s = np.arange(128)
    tri = (s[:, None] <= s[None, :]).astype(np.float32)
    q = np.arange(64)
    blk = ((q[:, None] // 4 == q[None, :] // 4) & (q[:, None] % 4 <= q[None, :] % 4)).astype(np.float32)
    wwin = np.zeros((128, 255), np.float32)
    wwin[:, 127] = 1.0
    iota = np.tile(np.arange(16, dtype=np.float32)[None, :], (128, 1))
    am = (np.arange(64)[None, :] // 4 == np.arange(16)[:, None]).astype(np.float32)
    amask = np.tile(am[None], (128, 1, 1)).astype(np.float32)
    return dict(c_identf=identf, c_tri=tri, c_blk=blk, c_wwin=wwin, c_iota=iota, c_amask=amask)


def make_in_maps(inp, peer_rows=16384):
    f = lambda a: np.ascontiguousarray(a, dtype=np.float32)
    shared = {}
    for k in ['g_mix', 'w_in', 'b_gate', 'pool_w', 'pool_scale', 'w_pool_out', 'sc_w', 'w_sc_out', 'cm_w', 'cm_b', 'cm_ln_g', 'cm_ln_b',
              'w_cm_out', 'sg_ln_g', 'sg_ln_b', 'sg_b', 'w_sg_out', 'w_o', 'g_x', 'g_mem', 'w_xq', 'w_xk', 'w_xv', 'w_xo', 'g_peer', 'w_pq',
              'peer_u', 'peer_v']:
        shared[k] = f(inp[k]) if not k.startswith('peer_') else f(inp[k][:, :peer_rows])
    shared['g_final'] = f(inp['g_final']).reshape(1, D)
    sgw = f(inp['sg_w'])
    shared['sg_wT'] = np.ascontiguousarray(sgw.transpose(0, 3, 1, 2))
    small = sgw[:, :, 0:4, 0:4].transpose(0, 3, 1, 2)
    shared['sgws'] = np.ascontiguousarray(np.tile(small, (1, 16, 1, 16)))
    pk = f(inp['peer_keys'])
    shared['keysT'] = np.ascontiguousarray(pk.transpose(0, 4, 1, 2, 3).reshape(2, 128, 2048))
    shared.update(_consts())
    xp = f(inp['x_prompt']); xs = f(inp['x_sample'])
    maps = []
    for c in range(8):
        b, hf = c // 2, c % 2
        m = dict(shared)
        xin = np.zeros((TOK, D), np.float32)
        if hf == 1:
            xin[0:1152] = xp[b, 896:2048]
        else:
            xin[128:1152] = xp[b, 0:1024]
        xin[1152:1216] = xs[16 * c:16 * c + 16].reshape(64, D)
        m['xin'] = xin
        m['flag'] = np.full((128, 1), float(hf), np.float32)
        cnt = np.zeros((128, 4, 15), np.float32)
        for g, w in enumerate(POOL_W):
            for pos in range(15):
                cnt[:, g, pos] = (1.0 / min(w, pos + 1)) if hf == 0 else (1.0 / w)
        m['cnt'] = cnt
        m['mem'] = f(inp['mem_prompt'][b])
        m['ck'] = f(inp['cache_mem_k'][:, 16 * c:16 * c + 16]).reshape(2, 16, 256, 512)
        m['cvv'] = f(inp['cache_mem_v'][:, 16 * c:16 * c + 16]).reshape(2, 16, 256, 512)
        m['spool'] = f(inp['state_pool'][:, 16 * c:16 * c + 16])
        m['ssc'] = f(inp['state_sconv'][:, 16 * c:16 * c + 16])
        m['scc'] = f(inp['state_cconv'][:, 16 * c:16 * c + 16])
        maps.append(m)
    return maps


def assemble(res):
    y_p = np.zeros((4, 2048, D), np.float32); y_s = np.zeros((128, 4, D), np.float32)
    mk = np.zeros((2, 4, 256, 4, 128), np.float32); mv = np.zeros_like(mk)
    pool_p = np.zeros((2, 4, 15, 512), np.float32); sc_p = np.zeros((2, 4, 2, 512), np.float32); cm_p = np.zeros((2, 4, 30, 512), np.float32)
    pool_s = np.zeros((2, 128, 15, 512), np.float32); sc_s = np.zeros((2, 128, 2, 512), np.float32); cm_s = np.zeros((2, 128, 30, 512), np.float32)
    cv_s = np.zeros((2, 128, 4, 512), np.float32)
    for c in range(8):
        r = res[c]
        b, hf = c // 2, c % 2
        y = np.asarray(r['y'])
        y_p[b, hf * 1024:(hf + 1) * 1024] = y[128:1152]
        y_s[16 * c:16 * c + 16] = y[1152:1216].reshape(16, 4, D)
        if hf == 1:
            mk[:, b] = np.asarray(r['mk']).reshape(2, 256, 4, 128)
            mv[:, b] = np.asarray(r['mv']).reshape(2, 256, 4, 128)
            pool_p[:, b] = np.asarray(r['pool_p']); sc_p[:, b] = np.asarray(r['sc_p']); cm_p[:, b] = np.asarray(r['cm_p'])
        pool_s[:, 16 * c:16 * c + 16] = np.asarray(r['pool_s']); sc_s[:, 16 * c:16 * c + 16] = np.asarray(r['sc_s'])
        cm_s[:, 16 * c:16 * c + 16] = np.asarray(r['cm_s']); cv_s[:, 16 * c:16 * c + 16] = np.asarray(r['cv_s'])
    return (y_p, y_s, mk, mv, pool_p, sc_p, cm_p, pool_s, sc_s, cm_s, cv_s)


def kernel(**inputs):
    if 'nc' not in _CACHE:
        _CACHE['nc'] = build_program()
    nc, _st = _CACHE['nc']
    maps = make_in_maps(inputs)
    res = run_bass_kernel_spmd(nc, maps, core_ids=list(range(8)))
    return assemble(res.results)
```
